# Optimizing a Trainium2 kernel written in Bass

```python
import jax, jax.numpy as jnp
from jax import lax
import numpy as np

D_MODEL = 1024
BATCH = 16
SEQ = 2048
DEPTH = 1
DEC_BATCH = 128
DEC_SEQ = 4
PAST_LEN = 8192
PAGE_SIZE = 128

HEAD_DIM = 64
N_HEADS = 8
ATTN_WIDTH = N_HEADS * HEAD_DIM
POOL_WIDTH = D_MODEL // 2
POOL_WINDOWS = (2, 4, 8, 16)
N_POOL_GROUPS = len(POOL_WINDOWS)
POOL_GROUP = POOL_WIDTH // N_POOL_GROUPS
POOL_HIST = max(POOL_WINDOWS) - 1
D_MIX = ATTN_WIDTH + POOL_WIDTH
PROJ_WIDTH = 3 * ATTN_WIDTH + POOL_WIDTH
DIL_PATTERNS = ((128, 1), (512, 4), (2048, 16))
N_STRIDED = 128
MAX_WINDOW = 2048
Q_BLOCK = 128
D_FF = 2816
CONV_WIDTH = 3
EPS = 1e-6
NEG = -1e30

kernel_name = 'hybrid_dilated_attn_pool_convffn_step'


def _rmsnorm(x, g):
    x32 = x.astype(jnp.float32)
    y = x32 * lax.rsqrt(jnp.mean(x32 * x32, axis=-1, keepdims=True) + EPS)
    return (y * g.astype(jnp.float32)).astype(x.dtype)


def _alibi_slopes():
    return jnp.asarray(2.0 ** (-8.0 * np.arange(1, N_HEADS + 1) / N_HEADS), dtype=jnp.float32)


def _dilated_branch_prompt(q, k, v, dilation, slopes):
    b, s, h, dh = q.shape
    L = s // dilation
    nb = -(-L // Q_BLOCK)
    Lp = nb * Q_BLOCK
    qs = q.reshape(b, L, dilation, h, dh)
    ks = k.reshape(b, L, dilation, h, dh)
    vs = v.reshape(b, L, dilation, h, dh)
    qb = jnp.pad(qs, ((0, 0), (0, Lp - L), (0, 0), (0, 0), (0, 0))).reshape(b, nb, Q_BLOCK, dilation, h, dh)
    pad_kv = ((0, 0), (Q_BLOCK, Lp - L), (0, 0), (0, 0), (0, 0))
    kp = jnp.pad(ks, pad_kv).reshape(b, nb + 1, Q_BLOCK, dilation, h, dh)
    vp = jnp.pad(vs, pad_kv).reshape(b, nb + 1, Q_BLOCK, dilation, h, dh)
    kb = jnp.concatenate([kp[:, :-1], kp[:, 1:]], axis=2)
    vb = jnp.concatenate([vp[:, :-1], vp[:, 1:]], axis=2)
    scores = jnp.einsum('bnqchd,bnkchd->bnchqk', qb, kb).astype(jnp.float32) * (HEAD_DIM ** -0.5)
    qi = jnp.arange(Q_BLOCK)[:, None]
    ki = jnp.arange(2 * Q_BLOCK)[None, :]
    diff = qi + Q_BLOCK - ki
    key_j = jnp.arange(nb)[:, None, None] * Q_BLOCK + ki[None] - Q_BLOCK
    valid = ((diff >= 0) & (diff <= N_STRIDED))[None] & (key_j >= 0)
    bias = -slopes[:, None, None] * (dilation * diff).astype(jnp.float32)[None]
    scores = jnp.where(valid[None, :, None, None], scores + bias[None, None, None], NEG)
    m = jnp.max(scores, axis=-1, keepdims=True)
    p = jnp.exp(scores - m)
    l = jnp.sum(p, axis=-1, keepdims=True)
    out = jnp.einsum('bnchqk,bnkchd->bnqchd', (p / l).astype(v.dtype), vb)
    lse = (m + jnp.log(l))[..., 0]
    out = out.reshape(b, Lp, dilation, h, dh)[:, :L].reshape(b, s, h, dh)
    lse = jnp.transpose(lse, (0, 1, 4, 2, 3)).reshape(b, Lp, dilation, h)[:, :L].reshape(b, s, h)
    return out, lse


def _dilated_branch_sample(q, k_all, v_all, dilation, slopes, n_hist):
    t = q.shape[1]
    n = jnp.arange(t)[:, None]
    kk = jnp.arange(N_STRIDED + 1)[None, :]
    idx = n_hist + n - kk * dilation
    valid = idx >= 0
    idx_c = jnp.maximum(idx, 0)
    kg = k_all[:, idx_c]
    vg = v_all[:, idx_c]
    scores = jnp.einsum('bthd,btkhd->bhtk', q, kg).astype(jnp.float32) * (HEAD_DIM ** -0.5)
    bias = -slopes[:, None, None] * (dilation * kk).astype(jnp.float32)[None]
    scores = jnp.where(valid[None, None], scores + bias[None], NEG)
    m = jnp.max(scores, axis=-1, keepdims=True)
    p = jnp.exp(scores - m)
    l = jnp.sum(p, axis=-1, keepdims=True)
    out = jnp.einsum('bhtk,btkhd->bthd', (p / l).astype(v_all.dtype), vg)
    lse = jnp.transpose((m + jnp.log(l))[..., 0], (0, 2, 1))
    return out, lse


def _merge_branches(branches):
    outs = jnp.stack([o for o, _ in branches], 0).astype(jnp.float32)
    lses = jnp.stack([s for _, s in branches], 0)
    w = jax.nn.softmax(lses, axis=0)[..., None]
    return jnp.sum(w * outs, axis=0)


def _pool_mixer(u_ext, pos, w_pool, pool_scale):
    b = u_ext.shape[0]
    t = pos.shape[0]
    u32 = u_ext.astype(jnp.float32)
    c = jnp.pad(jnp.cumsum(u32, axis=1), ((0, 0), (1, 0), (0, 0)))
    end = POOL_HIST + 1 + jnp.arange(t)
    groups = []
    for g, w in enumerate(POOL_WINDOWS):
        sl = slice(g * POOL_GROUP, (g + 1) * POOL_GROUP)
        wsum = c[:, end, sl] - c[:, end - w, sl]
        cnt = jnp.minimum(w, pos + 1).astype(jnp.float32)[None, :, None]
        groups.append(wsum / cnt)
    pooled = jnp.concatenate(groups, axis=-1)
    d = (pooled - u32[:, POOL_HIST:]).astype(u_ext.dtype).reshape(b, t, N_POOL_GROUPS, POOL_GROUP)
    mixed = jnp.einsum('btgc,gcd->btgd', d, w_pool).reshape(b, t, POOL_WIDTH)
    return mixed * pool_scale


def _conv_ffn(xn, hist, w_up, conv_w, conv_b, w_down):
    up = xn @ w_up
    ext = jnp.concatenate([hist, up], axis=1)
    t = up.shape[1]
    conv = conv_b + sum(ext[:, j:j + t] * conv_w[j] for j in range(CONV_WIDTH))
    gate, val = jnp.split(conv, 2, axis=-1)
    out = (jax.nn.silu(gate) * val) @ w_down
    return out, ext[:, -(CONV_WIDTH - 1):]


def _layer(x, pos, kv_hist, pool_prefix, ffn_prefix, g_attn_norm, w_in, g_q, g_k, w_pool, pool_scale,
           w_out, g_ffn_norm, w_up, conv_w, conv_b, w_down):
    b, t, _ = x.shape
    xn = _rmsnorm(x, g_attn_norm)
    proj = xn @ w_in
    q, k, v, u = jnp.split(proj, [ATTN_WIDTH, 2 * ATTN_WIDTH, 3 * ATTN_WIDTH], axis=-1)
    q = _rmsnorm(q.reshape(b, t, N_HEADS, HEAD_DIM), g_q)
    k = _rmsnorm(k.reshape(b, t, N_HEADS, HEAD_DIM), g_k)
    v = v.reshape(b, t, N_HEADS, HEAD_DIM)
    slopes = _alibi_slopes()
    if kv_hist is None:
        branches = [_dilated_branch_prompt(q, k, v, r, slopes) for (_, r) in DIL_PATTERNS]
        n_keep = min(MAX_WINDOW, t)
        new_k, new_v = k[:, t - n_keep:], v[:, t - n_keep:]
    else:
        k_hist, v_hist = kv_hist
        n_hist = k_hist.shape[1]
        k_all = jnp.concatenate([k_hist, k], axis=1)
        v_all = jnp.concatenate([v_hist, v], axis=1)
        branches = [_dilated_branch_sample(q, k_all, v_all, r, slopes, n_hist) for (_, r) in DIL_PATTERNS]
        new_k, new_v = k, v
    attn = _merge_branches(branches).astype(x.dtype).reshape(b, t, ATTN_WIDTH)
    u_ext = jnp.concatenate([pool_prefix, u], axis=1)
    pool_out = _pool_mixer(u_ext, pos, w_pool, pool_scale)
    new_pool = u_ext[:, -POOL_HIST:]
    h = x + jnp.concatenate([attn, pool_out.astype(x.dtype)], axis=-1) @ w_out
    ffn_out, new_ffn = _conv_ffn(_rmsnorm(h, g_ffn_norm), ffn_prefix, w_up, conv_w, conv_b, w_down)
    return h + ffn_out, new_k, new_v, new_pool, new_ffn


def setup_inputs(seed: int = 0) -> dict:
    key = jax.random.key(seed)
    ks = jax.random.split(key, 20)
    f32 = jnp.float32
    wb = min(MAX_WINDOW, PAST_LEN)

    def nrm(k, shape, scale):
        return jax.random.normal(k, shape, f32) * scale

    return {
        'x_prompt': nrm(ks[0], (BATCH, SEQ, D_MODEL), 1.0),
        'x_sample': nrm(ks[1], (DEC_BATCH, DEC_SEQ, D_MODEL), 1.0),
        'cache_k': nrm(ks[2], (DEPTH, DEC_BATCH, wb, N_HEADS, HEAD_DIM), 1.0),
        'cache_v': nrm(ks[3], (DEPTH, DEC_BATCH, wb, N_HEADS, HEAD_DIM), 1.0),
        'state_pool': nrm(ks[4], (DEPTH, DEC_BATCH, POOL_HIST, POOL_WIDTH), 1.0),
        'state_ffn_conv': nrm(ks[5], (DEPTH, DEC_BATCH, CONV_WIDTH - 1, 2 * D_FF), 1.0),
        'g_attn_norm': 1.0 + nrm(ks[6], (DEPTH, D_MODEL), 0.02),
        'w_in': nrm(ks[7], (DEPTH, D_MODEL, PROJ_WIDTH), D_MODEL ** -0.5),
        'g_q': 1.0 + nrm(ks[8], (DEPTH, N_HEADS, HEAD_DIM), 0.02),
        'g_k': 1.0 + nrm(ks[9], (DEPTH, N_HEADS, HEAD_DIM), 0.02),
        'w_pool': nrm(ks[10], (DEPTH, N_POOL_GROUPS, POOL_GROUP, POOL_GROUP), POOL_GROUP ** -0.5),
        'pool_scale': 1.0 + nrm(ks[11], (DEPTH, POOL_WIDTH), 0.02),
        'w_out': nrm(ks[12], (DEPTH, D_MIX, D_MODEL), D_MIX ** -0.5),
        'g_ffn_norm': 1.0 + nrm(ks[13], (DEPTH, D_MODEL), 0.02),
        'w_up': nrm(ks[14], (DEPTH, D_MODEL, 2 * D_FF), D_MODEL ** -0.5),
        'conv_w': nrm(ks[15], (DEPTH, CONV_WIDTH, 2 * D_FF), CONV_WIDTH ** -0.5),
        'conv_b': nrm(ks[16], (DEPTH, 2 * D_FF), 0.02),
        'w_down': nrm(ks[17], (DEPTH, D_FF, D_MODEL), D_FF ** -0.5),
    }


def reference(x_prompt, x_sample, cache_k, cache_v, state_pool, state_ffn_conv, g_attn_norm, w_in, g_q, g_k,
              w_pool, pool_scale, w_out, g_ffn_norm, w_up, conv_w, conv_b, w_down):
    pos_p = jnp.arange(x_prompt.shape[1])
    pos_s = PAST_LEN + jnp.arange(x_sample.shape[1])
    yp, ys = x_prompt, x_sample
    kp_l, vp_l, pp_l, fp_l, ks_l, vs_l, ps_l, fs_l = [], [], [], [], [], [], [], []
    for l in range(DEPTH):
        wts = (g_attn_norm[l], w_in[l], g_q[l], g_k[l], w_pool[l], pool_scale[l], w_out[l], g_ffn_norm[l],
               w_up[l], conv_w[l], conv_b[l], w_down[l])
        pool0 = jnp.zeros((yp.shape[0], POOL_HIST, POOL_WIDTH), yp.dtype)
        ffn0 = jnp.zeros((yp.shape[0], CONV_WIDTH - 1, 2 * D_FF), yp.dtype)
        yp, kp, vp, pp, fp = _layer(yp, pos_p, None, pool0, ffn0, *wts)
        ys, kn, vn, pn, fn = _layer(ys, pos_s, (cache_k[l], cache_v[l]), state_pool[l], state_ffn_conv[l], *wts)
        kp_l.append(kp); vp_l.append(vp); pp_l.append(pp); fp_l.append(fp)
        ks_l.append(kn); vs_l.append(vn); ps_l.append(pn); fs_l.append(fn)
    new_k_prompt = jnp.stack(kp_l, 0)
    new_v_prompt = jnp.stack(vp_l, 0)
    new_pool_prompt = jnp.stack(pp_l, 0)
    new_ffn_prompt = jnp.stack(fp_l, 0)
    new_k_sample = jnp.stack(ks_l, 0)
    new_v_sample = jnp.stack(vs_l, 0)
    new_pool_sample = jnp.stack(ps_l, 0)
    new_ffn_sample = jnp.stack(fs_l, 0)
    return (yp, ys, new_k_prompt, new_v_prompt, new_pool_prompt, new_ffn_prompt, new_k_sample, new_v_sample, new_pool_sample, new_ffn_sample)
```

```python
import os
import numpy as np
import ml_dtypes
from contextlib import ExitStack
import concourse.bass as bass
import concourse.mybir as mybir
from concourse.bass_utils import run_bass_kernel_spmd

F32 = mybir.dt.float32
BF16 = mybir.dt.bfloat16
AF = mybir.ActivationFunctionType
ALU = mybir.AluOpType
AX = mybir.AxisListType
EPS = 1e-6
NSLOT = 3
BF = ml_dtypes.bfloat16


class Buf:
    __slots__ = ('w', 'r')

    def __init__(s):
        s.w = None
        s.r = {}


class Group:
    def __init__(s, chan):
        s.chan = chan
        s.total = None


class Chan:
    def __init__(s, sem):
        s.sem = sem
        s.count = 0
        s.last = None


class Prog:
    ENG = ['pe', 'act', 'dve', 'pool', 'sp']

    def __init__(s, nc, es):
        s.nc = nc
        s.q = {e: [] for e in s.ENG}
        s.sem = {e: es.enter_context(nc.semaphore('sem_' + e)) for e in s.ENG if e != 'sp'}
        s.es = es
        s.chans = []

    def chan(s):
        c = Chan(s.es.enter_context(s.nc.semaphore('ch%d' % len(s.chans))))
        s.chans.append(c)
        return c

    def op(s, eng, fn, reads=(), writes=(), chan=None, group=None):
        q = s.q[eng]
        idx = len(q)
        deps = set()
        if chan is not None:
            if group is None:
                group = Group(chan)
            if chan.last is not group:
                if chan.last is not None:
                    deps.add(('dma', chan.last))
                chan.last = group
            chan.count += 16
            group.total = chan.count
            me = ('dma', group)
            mekey = ('dma', id(chan))
        else:
            me = (eng, idx)
            mekey = eng
        isdma = chan is not None
        for b in reads:
            if b.w is not None and b.w != me:
                if not (b.w[0] == 'pe' and eng == 'pe' and not isdma):
                    deps.add(b.w)
        for b in writes:
            if b.w is not None and b.w != me:
                if b.w[0] == 'dma' or isdma or b.w[0] != eng:
                    deps.add(b.w)
            for k, v in b.r.items():
                if v == me:
                    continue
                if v[0] == 'dma' or isdma or v[0] != eng:
                    deps.add(v)
        for b in reads:
            b.r[mekey] = me
        for b in writes:
            b.w = me
            b.r = {}
        q.append((fn, deps, chan))
        return group

    def inherit(s, dsts, srcs):
        for d in dsts:
            for b in srcs:
                if b.w is not None:
                    d.r[('w', id(b))] = b.w
                for k, v in b.r.items():
                    d.r[(k, id(b))] = v

    def finalize(s):
        sig = {e: set() for e in s.ENG}
        for e in s.ENG:
            for (fn, deps, chan) in s.q[e]:
                for d in deps:
                    if d[0] != 'dma':
                        sig[d[0]].add(d[1])
        s.rank = {}
        for e in s.ENG:
            for i, idx in enumerate(sorted(sig[e])):
                s.rank[(e, idx)] = i + 1

    def run(s, e, engobj):
        waited = {}
        for idx, (fn, deps, chan) in enumerate(s.q[e]):
            need = {}
            for d in deps:
                if d[0] == 'dma':
                    sm, val = d[1].chan.sem, d[1].total
                else:
                    sm, val = s.sem[d[0]], s.rank[d]
                k = id(sm)
                if need.get(k, (None, 0))[1] < val:
                    need[k] = (sm, val)
            for k, (sm, val) in need.items():
                if waited.get(k, 0) < val:
                    engobj.wait_ge(sm, val)
                    waited[k] = val
            ins = fn(engobj)
            if (e, idx) in s.rank:
                ins.then_inc(s.sem[e], 1)
            if chan is not None:
                ins.then_inc(chan.sem, 16)


class StopBuild(Exception):
    pass


def build(stop=None):
    def stage(name):
        if stop == name:
            raise StopBuild()
    nc = bass.Bass("TRN2", target_bir_lowering=False)

    def din(name, shape, dtype=F32):
        return nc.dram_tensor(name, shape, dtype, kind="ExternalInput").ap()

    def dout(name, shape):
        return nc.dram_tensor(name, shape, F32, kind="ExternalOutput").ap()

    def dscr(name, shape):
        return nc.dram_tensor(name, shape, BF16, kind="Internal").ap()

    xp = din("xp", [2, 2048, 1024]); xs = din("xs", [64, 1024])
    ck = din("ck", [16, 2048, 512]); cv = din("cv", [16, 2048, 512])
    spool = din("spool", [16, 15, 512]); sffn = din("sffn", [16, 2, 5632])
    g_attn = din("g_attn", [1, 1024]); w_in = din("w_in", [1024, 2048])
    gq_in = din("gq", [128, 512]); gk_in = din("gk", [128, 512])
    w_pool = din("w_pool", [4, 128, 128]); pool_scale = din("pool_scale", [1, 512])
    w_out = din("w_out", [1024, 1024]); g_ffn = din("g_ffn", [1, 1024])
    w_up = din("w_up", [1024, 5632]); conv_w = din("conv_w", [3, 5632]); conv_b = din("conv_b", [1, 5632])
    w_down = din("w_down", [2816, 1024])
    c_id = din("c_id", [128, 128], BF16); c_mcur = din("c_mcur", [128, 512], BF16)
    c_mprev = din("c_mprev", [128, 512], BF16); c_m16 = din("c_m16", [128, 4, 512], BF16)
    c_kaug = din("c_kaug", [4, 8, 2048], BF16); c_qaug = din("c_qaug", [4, 8, 2048], BF16)
    c_invcnt = din("c_invcnt", [128, 4, 16]); c_wt = din("c_wt", [128, 224]); c_wtn = din("c_wtn", [64, 16, 32])

    yp = dout("yp", [2, 2048, 1024]); ys = dout("ys", [64, 1024])
    nkp = dout("nkp", [2, 2048, 512]); nvp = dout("nvp", [2, 2048, 512])
    npp = dout("npp", [2, 15, 512]); nfp = dout("nfp", [2, 2, 5632])
    nks = dout("nks", [64, 512]); nvs = dout("nvs", [64, 512])
    nps = dout("nps", [16, 15, 512]); nfs = dout("nfs", [16, 2, 5632])

    win_s = dscr("win_s", [4, 128, 8, 512]); woa_s = dscr("woa_s", [2, 64, 8, 512]); wop_s = dscr("wop_s", [2, 128, 4, 512])
    wup_s = dscr("wup_s", [11, 128, 8, 512]); wdn_s = dscr("wdn_s", [2, 3, 128, 8, 512])

    with ExitStack() as es:
        def sb(name, shape, dtype):
            return es.enter_context(nc.sbuf_tensor(name, shape, dtype))

        P = Prog(nc, es)
        xh = sb("xh", [128, 4, 1024], F32); b_xh = Buf()
        xn = sb("xn", [128, 4, 1024], BF16); b_xn = Buf()
        actT = sb("actT", [128, 8, 512], BF16); b_actT = Buf()
        ss = sb("ss", [128, 16], F32); b_ss = Buf()
        ssqs = [sb("ssq", [128, 32], F32), sb("ssqb", [128, 32], F32)]; b_ssqs = [Buf(), Buf()]; sqc = [0]
        kT = sb("kT", [128, 8, 2048], BF16); b_kT = Buf()
        Vn = sb("Vn", [128, 5, 512], BF16); b_Vn = [Buf() for _ in range(5)]
        V4 = sb("V4", [128, 8, 512], BF16); b_V4 = Buf()
        V16 = sb("V16", [128, 16, 512], BF16); b_V16 = Buf()
        uT = sb("uT", [128, 4, 528], F32); b_uT = [Buf() for _ in range(4)]
        attnT = sb("attnT", [128, 8, 512], BF16); b_attnT = [Buf() for _ in range(8)]
        poolT = sb("poolT", [128, 4, 512], BF16); b_poolT = Buf()
        ring = sb("ring", [128, NSLOT, 4096], BF16); b_ring = [Buf() for _ in range(NSLOT)]
        identb = sb("identb", [128, 128], BF16); mcur = sb("mcur", [128, 512], BF16); mprev = sb("mprev", [128, 512], BF16)
        m16 = sb("m16", [128, 4, 512], BF16); onesb = sb("onesb", [128, 64], BF16)
        g1 = sb("g1", [128, 8], F32); g2 = sb("g2", [128, 8], F32)
        gqb = sb("gqb", [128, 512], F32); gkb = sb("gkb", [128, 512], F32); negM = sb("negM", [128, 4], F32)
        cpg = sb("cpg", [128, 22, 4], F32); cpv = sb("cpv", [128, 22, 4], F32)
        hist_g = sb("hist_g", [128, 22, 32], F32); hist_v = sb("hist_v", [128, 22, 32], F32)
        b_hist = Buf()
        hist2_g = sb("hist2_g", [128, 22, 2], F32); hist2_v = sb("hist2_v", [128, 22, 2], F32)
        pscale = sb("pscale", [128, 4], F32); wpool = sb("wpool", [128, 4, 128], BF16); invcnt = sb("invcnt", [128, 4, 16], F32)
        b_const = Buf()
        arena = sb("arena", [128, 21696], BF16)
        junk = sb("junk", [128, 1024], BF16); b_junk = Buf()
        off = [0]

        def carve(nbf16, reset=False):
            if reset:
                off[0] = 0
            a = arena[:, off[0]:off[0] + nbf16]
            off[0] += nbf16
            assert off[0] <= 21696
            return a
        qT = carve(8 * 512, True).rearrange("p (h t) -> p h t", h=8); b_qT = Buf()
        tmpn = carve(1024).bitcast(F32); b_tmpn = Buf()
        ksts = [carve(1024).bitcast(F32), carve(1024).bitcast(F32)]; b_ksts = [Buf(), Buf()]
        vsts = [carve(1024).bitcast(F32), carve(1024).bitcast(F32)]; b_vsts = [Buf(), Buf()]
        kvc = [0]
        qn_bfs = [carve(512), carve(512)]; b_qns = [Buf(), Buf()]
        kn_bfs = [carve(512), carve(512)]; b_kns = [Buf(), Buf()]
        nbc = [0]
        PT = [carve(512) for _ in range(3)]; b_PT = [Buf() for _ in range(3)]
        dT = carve(4 * 512).rearrange("p (g t) -> p g t", g=4); b_dT = Buf()
        sa = carve(1056).bitcast(F32); sbb = carve(1056).bitcast(F32); b_s = Buf()
        rlb = carve(1024).bitcast(F32); b_rlb = Buf()
        rlb2 = carve(1024).bitcast(F32); b_rlb2 = Buf()
        rlbs = [rlb, rlb2]; b_rlbs = [b_rlb, b_rlb2]
        tmp16 = carve(32).bitcast(F32)
        A_bufs = [b_qT, b_tmpn, b_dT, b_s, b_rlb, b_rlb2] + b_PT + b_qns + b_kns + b_ksts + b_vsts
        gT = carve(22 * 512, True).rearrange("p (k t) -> p k t", k=22); b_gT = Buf()
        ext = [[carve(1056).bitcast(F32) for _ in range(2)] for _ in range(2)]; b_ext = [[Buf(), Buf()], [Buf(), Buf()]]
        acc = [[carve(1024).bitcast(F32)], [carve(1024).bitcast(F32) for _ in range(2)]]; b_acc = [[Buf()], [Buf(), Buf()]]
        sg = [carve(1024).bitcast(F32) for _ in range(2)]; b_sg = [Buf(), Buf()]
        b_histd = {(k_, a_, p_): Buf() for k_ in 'gv' for a_ in range(22) for p_ in range(2)}
        b_hist_all = list(b_histd.values())
        ccnt = {'g': 0, 'v': 0}
        pend_silu = [None]
        pend_evac = [None]
        pend_tr = [None]
        upst = carve(1024).bitcast(F32); b_upst = Buf()
        F_bufs = [b_gT, b_upst] + b_sg + b_ext[0] + b_ext[1] + b_acc[0] + b_acc[1]
        kTflat = kT[:].rearrange("p h t -> p (h t)")
        Kc = [kTflat[:, 0:3584].rearrange("p (a c) -> p a c", a=7)] * 2; b_Kc = [Buf()] * 2
        Vc = [kTflat[:, 3584:7168].rearrange("p (a c) -> p a c", a=7)] * 2; b_Vc = [Buf()] * 2
        pbank = [es.enter_context(nc.psum_tensor("pb%d" % i, [128, 512], F32)) for i in range(8)]
        pA = [pbank[0], pbank[1]]; b_pA = [Buf(), Buf()]
        pTb = [pbank[2][:].bitcast(BF16), pbank[7][:].bitcast(BF16)]; b_pT = [Buf(), Buf()]
        pS = [pbank[3], pbank[4]]; b_pS = [Buf(), Buf()]
        pO = pbank[5]; b_pO = Buf()
        pL = pbank[6]; b_pL = Buf()

        def Aop(fn, r, w): P.op('act', fn, r, w)
        def Vop(fn, r, w): P.op('dve', fn, r, w)
        def Gop(fn, r, w): P.op('pool', fn, r, w)
        def Top(fn, r, w): P.op('pe', fn, r, w)
        def Dop(fn, r, w, ch, group=None, q='sp'): return P.op(q, fn, r, w, chan=ch, group=group)

        ch_c = [P.chan() for _ in range(4)]
        ch_x = P.chan(); ch_y = P.chan(); ch_k = P.chan(); ch_vo = P.chan(); ch_v = P.chan(); ch_q = P.chan()
        ch_w = [P.chan() for _ in range(NSLOT)]
        ch_pro = [P.chan() for _ in range(14)]
        ch_ol = [P.chan() for _ in range(4)]; ch_oc = [0]; ch_s = [P.chan(), P.chan()]

        def cho():
            ch_oc[0] += 1
            return ch_ol[ch_oc[0] % 4]

        cc = [0]

        def cload(out, in_, slow=False, q='sp'):
            c = ch_c[cc[0] % 4]; cc[0] += 1
            if slow:
                Dop(lambda e: e.dma_start(out=out, in_=in_, allow_slow_non_contiguous=True), [], [b_const], c, q=q)
            else:
                Dop(lambda e: e.dma_start(out=out, in_=in_), [], [b_const], c, q=q)
        cload(identb[:], c_id); cload(mcur[:], c_mcur); cload(mprev[:], c_mprev); cload(m16[:], c_m16)
        cload(gqb[:], gq_in); cload(gkb[:], gk_in); cload(invcnt[:], c_invcnt)
        cload(g1[:], g_attn[0, :].rearrange("(c p) -> p c", p=128), slow=True)
        cload(g2[:], g_ffn[0, :].rearrange("(c p) -> p c", p=128), slow=True)
        cload(pscale[:], pool_scale[0, :].rearrange("(g p) -> p g", p=128), slow=True)
        for j in range(3):
            cload(cpg[:, :, j], conv_w[j, 0:2816].rearrange("(a p) -> p a", p=128), slow=True)
            cload(cpv[:, :, j], conv_w[j, 2816:5632].rearrange("(a p) -> p a", p=128), slow=True)
        cload(cpg[:, :, 3], conv_b[0, 0:2816].rearrange("(a p) -> p a", p=128), slow=True)
        cload(cpv[:, :, 3], conv_b[0, 2816:5632].rearrange("(a p) -> p a", p=128), slow=True)
        cload(wpool[:], w_pool.rearrange("g c d -> c g d"), q='pool')
        Dop(lambda e: e.dma_start(out=kT[64:68, :, :], in_=c_kaug), [], [b_kT], ch_c[0])
        Vop(lambda e: e.tensor_scalar(out=g1[:], in0=g1[:], scalar1=32.0, scalar2=None, op0=ALU.mult), [b_const], [b_const])
        Vop(lambda e: e.tensor_scalar(out=g2[:], in0=g2[:], scalar1=32.0, scalar2=None, op0=ALU.mult), [b_const], [b_const])
        Vop(lambda e: e.memset(onesb[:], 1.0), [], [b_const])
        Vop(lambda e: e.tensor_tensor(out=xh[:, 0, 0:512], in0=gqb[:], in1=gqb[:], op=ALU.mult), [b_const], [b_xh])
        Vop(lambda e: e.reduce_max(out=negM[:, 1:2], in_=xh[:, 0, 0:512], axis=AX.X), [b_xh], [b_const])
        Vop(lambda e: e.tensor_tensor(out=xh[:, 0, 0:512], in0=gkb[:], in1=gkb[:], op=ALU.mult), [b_const], [b_xh])
        Vop(lambda e: e.reduce_max(out=negM[:, 2:3], in_=xh[:, 0, 0:512], axis=AX.X), [b_xh], [b_const])
        Vop(lambda e: e.tensor_tensor(out=negM[:, 3:4], in0=negM[:, 1:2], in1=negM[:, 2:3], op=ALU.add), [b_const], [b_const])
        Vop(lambda e: e.tensor_scalar(out=negM[:, 0:1], in0=negM[:, 3:4], scalar1=-4.0, scalar2=None, op0=ALU.mult), [b_const], [b_const])
        Vop(lambda e: e.tensor_scalar(out=gkb[:], in0=gkb[:], scalar1=8.0, scalar2=None, op0=ALU.mult), [b_const], [b_const])

        b_scr = {}
        pc_ = [0]

        def pro(key, out, in_):
            c = ch_pro[pc_[0] % 14]; pc_[0] += 1
            b = b_scr.setdefault(key, Buf())
            Dop(lambda e: e.dma_start(out=out, in_=in_), [], [b], c, q='pool')
        for g in range(4):
            pro(('in', g), win_s[g], w_in[:, g * 512:(g + 1) * 512].rearrange("(c p) n -> p c n", p=128))
        for f in range(2):
            pro(('oa', f), woa_s[f], w_out[0:512, f * 512:(f + 1) * 512].rearrange("(h p) n -> p h n", p=64))
            pro(('op', f), wop_s[f], w_out[512:1024, f * 512:(f + 1) * 512].rearrange("(g p) n -> p g n", p=128))
        def late_pro():
            for i in range(11):
                pro(('up', i), wup_s[i, :, :, 0:256], w_up[:, 256 * i:256 * i + 256].rearrange("(c p) n -> p c n", p=128))
                pro(('up', i), wup_s[i, :, :, 256:512], w_up[:, 2816 + 256 * i:2816 + 256 * i + 256].rearrange("(c p) n -> p c n", p=128))
            for f in range(2):
                for pc in range(3):
                    nk = 8 if pc < 2 else 6
                    pro(('dn', f, pc), wdn_s[f, pc, :, 0:nk, :],
                        w_down[pc * 1024:pc * 1024 + nk * 128, f * 512:(f + 1) * 512].rearrange("(k p) n -> p k n", p=128))

        piece_list = []
        NT_TILES = 9
        for t in range(NT_TILES):
            for g in (2, 0, 1, 3):
                piece_list.append((('in', g), win_s[g], 128, 4096))
            for f in range(2):
                piece_list.append((('oa', f), woa_s[f], 64, 4096))
                piece_list.append((('op', f), wop_s[f], 128, 2048))
            for i in range(11):
                piece_list.append((('up', i), wup_s[i], 128, 4096))
            for f in range(2):
                for pc in range(3):
                    piece_list.append((('dn', f, pc), wdn_s[f, pc], 128, 4096))
        issued = [0]

        def wget(i, hold=0):
            while issued[0] < min(len(piece_list), i - hold + NSLOT):
                k = issued[0]; issued[0] += 1
                key, src, npart, nel = piece_list[k]
                sl = k % NSLOT
                o = ring[0:npart, sl, 0:nel]
                s2 = src.rearrange("p a n -> p (a n)")
                Dop(lambda e, o=o, s2=s2: e.dma_start(out=o, in_=s2), [b_scr[key]], [b_ring[sl]], ch_w[sl])
            return i % NSLOT, b_ring[i % NSLOT]

        def slotv(sl, a):
            return ring[:, sl, 0:a * 512].rearrange("p (a n) -> p a n", n=512)

        pac = [0]

        rot6 = [(pA[0], b_pA[0]), (pA[1], b_pA[1]), (pS[0], b_pS[0]), (pS[1], b_pS[1]), (pO, b_pO), (pL, b_pL)]

        def nextpa():
            i = pac[0] % 6; pac[0] += 1
            return rot6[i]
        ptc = [0]

        def nextpt():
            i = ptc[0] % 2; ptc[0] += 1
            return pTb[i], b_pT[i]

        def do_tile(tidx, smp, s, m):
            NT = 64 if smp else 512; nsub = 1 if smp else 4; PP = 64 if smp else 128
            nseg = 16 if smp else 1; L = 4 if smp else 512; E = 16 + L
            T0 = 0 if smp else 512 * m
            base = tidx * 25
            last = smp or m == 3
            P.inherit(A_bufs, F_bufs)
            vgrp[0] = None
            if smp:
                xsrc = xs.rearrange("(j p) d -> p j d", p=64)
            else:
                xsrc = xp[s, T0:T0 + 512, :].rearrange("(j p) d -> p j d", p=128)
            Dop(lambda e: e.dma_start(out=xh[0:PP, 0:nsub, :], in_=xsrc), [], [b_xh], ch_x)
            if not smp:
                Dop(lambda e: e.dma_start(out=qT[64:68, :, :], in_=c_qaug[:, :, T0:T0 + 512]), [], [b_qT], ch_q)

            def norm_T(g):
                Vop(lambda e: e.memset(ss[:, 0:4], 0.0), [], [b_ss])
                for j in range(nsub):
                    Aop(lambda e, j=j: e.activation(out=junk[0:PP, :], in_=xh[0:PP, j, :], func=AF.Square, accum_out=ss[0:PP, j:j + 1]),
                        [b_xh, b_ss], [b_junk, b_ss])
                Vop(lambda e: e.tensor_scalar(out=ss[0:PP, 4:8], in0=ss[0:PP, 0:4], scalar1=1024 * EPS, scalar2=None, op0=ALU.add), [b_ss], [b_ss])
                Aop(lambda e: e.activation(out=ss[0:PP, 8:12], in_=ss[0:PP, 4:8], func=AF.Ln), [b_ss], [b_ss])
                Aop(lambda e: e.activation(out=ss[0:PP, 12:16], in_=ss[0:PP, 8:12], func=AF.Exp, scale=-0.5), [b_ss], [b_ss])
                for j in range(nsub):
                    Aop(lambda e, j=j: e.activation(out=xn[0:PP, j, :], in_=xh[0:PP, j, :], func=AF.Copy, scale=ss[0:PP, 12 + j:13 + j]),
                        [b_xh, b_ss], [b_xn])
                for c in range(8):
                    pt, bpt = nextpt()
                    for j in range(nsub):
                        Top(lambda e, pt=pt, j=j, c=c: e.transpose(out=pt[:, j * PP:(j + 1) * PP], in_=xn[0:PP, j, c * 128:(c + 1) * 128],
                                                                  identity=identb[0:PP, 0:PP]), [b_xn, b_const], [bpt])
                    Vop(lambda e, pt=pt, c=c: e.tensor_scalar(out=actT[:, c, 0:NT], in0=pt[:, 0:NT], scalar1=g[:, c:c + 1], scalar2=None, op0=ALU.mult),
                        [bpt, b_const], [b_actT])
            norm_T(g1)
            stage('norm1')

            for pos_, gi in enumerate((2, 0, 1)):
                sl, bsl = wget(base + pos_)
                sv = slotv(sl, 8)
                for j in range(nsub):
                    pa, bpa = nextpa()
                    for c in range(8):
                        Top(lambda e, pa=pa, j=j, c=c, sv=sv: e.matmul(pa[0:PP, :], lhsT=actT[:, c, j * PP:(j + 1) * PP], rhs=sv[:, c, :],
                                                                      start=(c == 0), stop=(c == 7)), [b_actT, bsl], [bpa])
                    if pend_tr[0] is not None:
                        pend_tr[0](); pend_tr[0] = None
                    rows = slice(T0 + j * 128, T0 + j * 128 + 128)
                    if gi < 2:
                        ssq = ssqs[sqc[0] % 2]; b_ssq = b_ssqs[sqc[0] % 2]; sqc[0] += 1
                        Vop(lambda e, ssq=ssq: e.memset(ssq[:, 0:8], 0.0), [], [b_ssq])
                        for h in range(8):
                            Aop(lambda e, pa=pa, h=h, ssq=ssq: e.activation(out=junk[0:PP, 0:64], in_=pa[0:PP, h * 64:(h + 1) * 64], func=AF.Square,
                                                                   accum_out=ssq[0:PP, h:h + 1]), [bpa, b_ssq], [b_junk, b_ssq])
                        Vop(lambda e, ssq=ssq: e.tensor_scalar(out=ssq[0:PP, 8:16], in0=ssq[0:PP, 0:8], scalar1=64 * EPS, scalar2=None, op0=ALU.add), [b_ssq], [b_ssq])
                        Aop(lambda e, ssq=ssq: e.activation(out=ssq[0:PP, 16:24], in_=ssq[0:PP, 8:16], func=AF.Ln), [b_ssq], [b_ssq])
                        Aop(lambda e, ssq=ssq: e.activation(out=ssq[0:PP, 24:32], in_=ssq[0:PP, 16:24], func=AF.Exp, scale=-0.5), [b_ssq], [b_ssq])
                        if pend_evac[0] is not None:
                            pend_evac[0](); pend_evac[0] = None
                        for h in range(8):
                            Vop(lambda e, pa=pa, h=h, ssq=ssq: e.tensor_scalar(out=tmpn[0:PP, h * 64:(h + 1) * 64], in0=pa[0:PP, h * 64:(h + 1) * 64],
                                                                      scalar1=ssq[0:PP, 24 + h:25 + h], scalar2=None, op0=ALU.mult), [bpa, b_ssq], [b_tmpn])
                        nbi = nbc[0] % 2; nbc[0] += 1
                        if gi == 0:
                            nb, bnb = qn_bfs[nbi], b_qns[nbi]
                            Vop(lambda e, nb=nb: e.tensor_tensor(out=nb[0:PP, :], in0=tmpn[0:PP, :], in1=gqb[0:PP, :], op=ALU.mult), [b_tmpn, b_const], [bnb])
                        else:
                            nb, bnb = kn_bfs[nbi], b_kns[nbi]
                            kst = ksts[nbi]; b_kst = b_ksts[nbi]
                            Vop(lambda e, kst=kst: e.tensor_tensor(out=kst[0:PP, :], in0=tmpn[0:PP, :], in1=gkb[0:PP, :], op=ALU.mult), [b_tmpn, b_const], [b_kst])
                            Vop(lambda e, nb=nb, kst=kst: e.tensor_copy(out=nb[0:PP, :], in_=kst[0:PP, :]), [b_kst], [bnb])
                            dst = nks if smp else nkp[s, rows, :]
                            Dop(lambda e, dst=dst, kst=kst: e.dma_start(out=dst, in_=kst[0:PP, :]), [b_kst], [], ch_k)
                        if not smp:
                            def tr(nb=nb, bnb=bnb, gi=gi, j=j, rows=rows):
                                pt, bpt = nextpt()
                                for h in range(8):
                                    Top(lambda e, h=h: e.transpose(out=pt[0:64, h * 128:(h + 1) * 128], in_=nb[:, h * 64:(h + 1) * 64], identity=identb[:]),
                                        [bnb, b_const], [bpt])
                                src = pt[0:64, :].rearrange("p (h t) -> p h t", t=128)
                                if gi == 0:
                                    pend_evac[0] = (lambda: Aop(lambda e: e.activation(out=qT[0:64, :, j * 128:(j + 1) * 128], in_=src, func=AF.Copy), [bpt], [b_qT]))
                                else:
                                    pend_evac[0] = (lambda: Aop(lambda e: e.activation(out=kT[0:64, :, rows], in_=src, func=AF.Copy), [bpt], [b_kT]))
                            pend_tr[0] = tr
                        else:
                            pt, bpt = nextpt()
                            for hp in range(4):
                                Top(lambda e, pt=pt, hp=hp, nb=nb: e.transpose(out=pt[:, hp * 64:(hp + 1) * 64], in_=nb[0:64, hp * 128:(hp + 1) * 128],
                                                                              identity=identb[0:64, 0:64]), [bnb, b_const], [bpt])
                            if gi == 0:
                                for hh in range(2):
                                    Aop(lambda e, pt=pt, hh=hh: e.activation(
                                        out=qbd[hh * 64:(hh + 1) * 64, :, :, hh * 4:hh * 4 + 4],
                                        in_=pt[hh * 64:(hh + 1) * 64, 0:256].rearrange("p (a b t) -> p a b t", a=4, b=16), func=AF.Copy), [bpt], [b_qbd])
                            else:
                                Aop(lambda e, pt=pt: e.activation(out=kTp[:, :, :], in_=pt[:, 0:256].rearrange("p (a t) -> p a t", a=4), func=AF.Copy), [bpt], [b_kTp])
                    else:
                        B = (4 * m + j) if not smp else 0
                        vs = B % 5
                        vst = vsts[kvc[0] % 2]; b_vst = b_vsts[kvc[0] % 2]; kvc[0] += 1
                        Aop(lambda e, pa=pa, vst=vst: e.activation(out=vst[0:PP, :], in_=pa[0:PP, :], func=AF.Copy), [bpa], [b_vst])
                        dst = nvs if smp else nvp[s, rows, :]
                        Dop(lambda e, dst=dst, vst=vst: e.dma_start(out=dst, in_=vst[0:PP, :]), [b_vst], [], ch_vo)
                        Vop(lambda e, vs=vs, vst=vst: e.tensor_copy(out=Vn[0:PP, vs, :], in_=vst[0:PP, :]), [b_vst], [b_Vn[vs]])
                        if not smp and os.environ.get('NOVDMA') is None:
                            grp = vgrp[0]
                            for c4 in range(4):
                                grp = Dop(lambda e, c4=c4, vs=vs, j=j: e.dma_start(out=V4[32 * j:32 * j + 32, (m % 2) * 4 + c4, :], in_=Vn[c4:128:4, vs, :]),
                                          [b_Vn[vs]], [b_V4], ch_v, group=grp, q='pool')
                            for c16 in range(16):
                                grp = Dop(lambda e, c16=c16, vs=vs, j=j: e.dma_start(out=V16[32 * m + 8 * j:32 * m + 8 * j + 8, c16, :], in_=Vn[c16:128:16, vs, :]),
                                          [b_Vn[vs]], [b_V16], ch_v, group=grp, q='pool')
                            vgrp[0] = grp
            if pend_tr[0] is not None:
                pend_tr[0](); pend_tr[0] = None
            if pend_evac[0] is not None:
                pend_evac[0](); pend_evac[0] = None
            stage('qkv')
            sl, bsl = wget(base + 3)
            sv = slotv(sl, 8)
            for g in range(4):
                pa, bpa = nextpa()
                for c in range(8):
                    Top(lambda e, pa=pa, g=g, c=c, sv=sv: e.matmul(pa[:, 0:NT], lhsT=sv[:, c, g * 128:(g + 1) * 128], rhs=actT[:, c, 0:NT],
                                                                  start=(c == 0), stop=(c == 7)), [b_actT, bsl], [bpa])
                ue = uT[:, g, 0:nseg * E].rearrange("p (s e) -> p s e", e=E)
                Aop(lambda e, pa=pa, ue=ue: e.activation(out=ue[:, :, 16:E], in_=pa[:, 0:NT].rearrange("p (s l) -> p s l", l=L), func=AF.Copy), [bpa], [b_uT[g]])
            if last:
                pa, bpa = nextpa()
                for c in range(8):
                    Top(lambda e, pa=pa, c=c, sv=sv: e.matmul(pa[0:PP, :], lhsT=actT[:, c, NT - PP:NT], rhs=sv[:, c, :], start=(c == 0), stop=(c == 7)),
                        [b_actT, bsl], [bpa])
                Aop(lambda e, pa=pa: e.activation(out=tmpn[0:PP, :], in_=pa[0:PP, :], func=AF.Copy), [bpa], [b_tmpn])
                if smp:
                    for t4 in range(4):
                        Dop(lambda e, t4=t4: e.dma_start(out=nps[:, 11 + t4, :], in_=tmpn[t4:64:4, :]), [b_tmpn], [], cho())
                else:
                    Dop(lambda e: e.dma_start(out=npp[s, :, :], in_=tmpn[113:128, :]), [b_tmpn], [], cho())
            stage('u')
            for g in range(4):
                w = 2 << g
                ue = uT[:, g, 0:nseg * E].rearrange("p (s e) -> p s e", e=E)
                s1 = sa[:, 0:nseg * E].rearrange("p (s e) -> p s e", e=E)
                s2 = sbb[:, 0:nseg * E].rearrange("p (s e) -> p s e", e=E)
                Gop(lambda e, ue=ue, s1=s1: e.tensor_tensor(out=s1[:, :, 1:E], in0=ue[:, :, 1:E], in1=ue[:, :, 0:E - 1], op=ALU.add), [b_uT[g]], [b_s])
                fin = s1
                if w >= 4:
                    Gop(lambda e, s1=s1, s2=s2: e.tensor_tensor(out=s2[:, :, 3:E], in0=s1[:, :, 3:E], in1=s1[:, :, 1:E - 2], op=ALU.add), [b_s], [b_s]); fin = s2
                if w >= 8:
                    Gop(lambda e, s1=s1, s2=s2: e.tensor_tensor(out=s1[:, :, 7:E], in0=s2[:, :, 7:E], in1=s2[:, :, 3:E - 4], op=ALU.add), [b_s], [b_s]); fin = s1
                if w >= 16:
                    Gop(lambda e, s1=s1, s2=s2: e.tensor_tensor(out=s2[:, :, 15:E], in0=s1[:, :, 15:E], in1=s1[:, :, 7:E - 8], op=ALU.add), [b_s], [b_s]); fin = s2
                dv = dT[:, g, 0:NT].rearrange("p (s l) -> p s l", l=L)
                Vop(lambda e, fin=fin, ue=ue, dv=dv, w=w: e.scalar_tensor_tensor(out=dv, in0=fin[:, :, 16:E], scalar=1.0 / w, in1=ue[:, :, 16:E],
                                                                                 op0=ALU.mult, op1=ALU.subtract), [b_s, b_uT[g]], [b_dT])
                if (not smp) and m == 0:
                    Gop(lambda e, fin=fin, g=g: e.tensor_tensor(out=tmp16[:, 0:16], in0=fin[:, 0, 16:32], in1=invcnt[:, g, :], op=ALU.mult), [b_s, b_const], [b_s])
                    Gop(lambda e, ue=ue, g=g: e.tensor_tensor(out=dT[:, g, 0:16], in0=tmp16[:, 0:16], in1=ue[:, 0, 16:32], op=ALU.subtract), [b_s, b_uT[g]], [b_dT])
                pa, bpa = nextpa()
                Top(lambda e, pa=pa, g=g: e.matmul(pa[:, 0:NT], lhsT=wpool[:, g, :], rhs=dT[:, g, 0:NT], start=True, stop=True), [b_dT, b_const], [bpa])
                Vop(lambda e, pa=pa, g=g: e.tensor_scalar(out=poolT[:, g, 0:NT], in0=pa[:, 0:NT], scalar1=pscale[:, g:g + 1], scalar2=None, op0=ALU.mult),
                    [bpa, b_const], [b_poolT])
                if not smp:
                    Gop(lambda e, g=g: e.tensor_copy(out=uT[:, g, 0:16], in_=uT[:, g, 512:528]), [b_uT[g]], [b_uT[g]])

            if tidx == 0:
                late_pro()
            stage('pool')
            if not smp:
                glist = []
                for h in range(8):
                    kinds = [k_ for k_ in ('1c', '1p', '4c', '4p', '16') if not (k_ == '4p' and m == 0)]
                    for ki_, kind in enumerate(kinds):
                        glist.append((h, kind, ki_ == 0, ki_ == len(kinds) - 1))

                def g_params(gi_):
                    h, kind, first, lastk = glist[gi_]
                    ps, bps = pS[gi_ % 2], b_pS[gi_ % 2]
                    pt_, bpt_ = PT[gi_ % 3], b_PT[gi_ % 3]
                    R = 128; c0 = 0
                    if kind == '16':
                        R = 32 * (m + 1); mask = m16[:, m, :]
                    else:
                        mask = mcur if kind in ('1c', '4c') else mprev
                        if kind == '1p' and m == 0:
                            c0 = 128
                    return h, kind, first, lastk, ps, bps, pt_, bpt_, R, c0, mask

                def emit_scores(gi_):
                    h, kind, first, lastk, ps, bps, pt_, bpt_, R, c0, mask = g_params(gi_)
                    Top(lambda e: e.matmul(ps[0:R, c0:512], lhsT=identb[0:R, 0:R], rhs=mask[0:R, c0:512], start=True, stop=False), [b_const], [bps])
                    if kind in ('1c', '1p'):
                        for n in range(c0 // 128, 4):
                            kb = T0 + n * 128 - (128 if kind == '1p' else 0)
                            Top(lambda e, n=n, kb=kb: e.matmul(ps[:, n * 128:(n + 1) * 128], lhsT=kT[0:68, h, kb:kb + 128],
                                                              rhs=qT[0:68, h, n * 128:(n + 1) * 128], start=False, stop=True, skip_group_check=True), [b_kT, b_qT], [bps])
                    elif kind in ('4c', '4p'):
                        for c4 in range(4):
                            kb = T0 + c4 - (512 if kind == '4p' else 0)
                            Top(lambda e, c4=c4, kb=kb: e.matmul(ps[:, c4 * 128:(c4 + 1) * 128], lhsT=kT[0:68, h, kb:kb + 509:4],
                                                                rhs=qT[0:68, h, c4:512:4], start=False, stop=True, skip_group_check=True), [b_kT, b_qT], [bps])
                    else:
                        for c16 in range(16):
                            Top(lambda e, c16=c16: e.matmul(ps[0:R, c16 * 32:(c16 + 1) * 32], lhsT=kT[0:68, h, c16:T0 + 512:16],
                                                           rhs=qT[0:68, h, c16:512:16], start=False, stop=True, skip_group_check=True), [b_kT, b_qT], [bps])
                    Aop(lambda e: e.activation(out=pt_[0:R, c0:512], in_=ps[0:R, c0:512], func=AF.Exp, bias=negM[0:R, 0:1], scale=1.0), [bps, b_const], [bpt_])

                def emit_pv(gi_):
                    h, kind, first, lastk, ps, bps, pt_, bpt_, R, c0, mask = g_params(gi_)
                    (pO_, b_pO_), (pL_, b_pL_) = ((pO, b_pO), (pL, b_pL)) if h % 2 == 0 else ((pA[0], b_pA[0]), (pA[1], b_pA[1]))
                    rlb_, b_rlb_ = rlbs[h % 2], b_rlbs[h % 2]
                    if first:
                        Vop(lambda e: e.memset(pO_[0:64, :], 0.0), [], [b_pO_])
                        Vop(lambda e: e.memset(pL_[0:64, :], 0.0), [], [b_pL_])
                    hs = slice(h * 64, (h + 1) * 64)
                    kw = dict(start=False, stop=False, skip_group_check=True)
                    if kind in ('1c', '1p'):
                        for n in range(c0 // 128, 4):
                            vs = (4 * m + n - (1 if kind == '1p' else 0)) % 5
                            Top(lambda e, n=n, vs=vs: e.matmul(pO_[0:64, n * 128:(n + 1) * 128], lhsT=Vn[:, vs, hs], rhs=pt_[:, n * 128:(n + 1) * 128], **kw),
                                [b_Vn[vs], bpt_], [b_pO_])
                        Top(lambda e: e.matmul(pL_[0:64, c0:512], lhsT=onesb[:, 0:64], rhs=pt_[:, c0:512], **kw), [b_const, bpt_], [b_pL_])
                    elif kind in ('4c', '4p'):
                        for c4 in range(4):
                            vsl = ((m if kind == '4c' else m - 1) % 2) * 4 + c4
                            Top(lambda e, c4=c4, vsl=vsl: e.matmul(pO_[0:64, c4:512:4], lhsT=V4[:, vsl, hs], rhs=pt_[:, c4 * 128:(c4 + 1) * 128], **kw), [b_V4, bpt_], [b_pO_])
                        Top(lambda e: e.matmul(pL_[0:64, :].rearrange("p (i c) -> p c i", c=4), lhsT=onesb[:, 0:64], rhs=pt_[:, 0:512].rearrange("p (c i) -> p c i", c=4), **kw),
                            [b_const, bpt_], [b_pL_])
                    else:
                        for c16 in range(16):
                            Top(lambda e, c16=c16: e.matmul(pO_[0:64, c16:512:16], lhsT=V16[0:R, c16, hs], rhs=pt_[0:R, c16 * 32:(c16 + 1) * 32], **kw), [b_V16, bpt_], [b_pO_])
                        Top(lambda e: e.matmul(pL_[0:64, :].rearrange("p (i c) -> p c i", c=16), lhsT=onesb[0:R, 0:64], rhs=pt_[0:R, 0:512].rearrange("p (c i) -> p c i", c=16), **kw),
                            [b_const, bpt_], [b_pL_])
                    if lastk:
                        Aop(lambda e: e.activation(out=rlb_[0:64, :], in_=pL_[0:64, :], func=AF.Ln), [b_pL_], [b_rlb_])
                        Aop(lambda e: e.activation(out=rlb_[0:64, :], in_=rlb_[0:64, :], func=AF.Exp, scale=-1.0), [b_rlb_], [b_rlb_])
                        Vop(lambda e: e.tensor_tensor(out=attnT[0:64, h, :], in0=pO_[0:64, :], in1=rlb_[0:64, :], op=ALU.mult), [b_pO_, b_rlb_], [b_attnT[h]])

                emit_scores(0)
                for gi_ in range(len(glist)):
                    if gi_ + 1 < len(glist):
                        emit_scores(gi_ + 1)
                    emit_pv(gi_)
            else:
                sample_attention()

            stage('attn')
            for f in range(2):
                sla, bsla = wget(base + 4 + 2 * f)
                slp, bslp = wget(base + 5 + 2 * f, hold=1)
                sva = slotv(sla, 8); svp = slotv(slp, 4)
                for j in range(nsub):
                    pa, bpa = nextpa()
                    for hh in range(8):
                        Top(lambda e, pa=pa, hh=hh, j=j, sva=sva: e.matmul(pa[0:PP, :], lhsT=attnT[0:64, hh, j * PP:(j + 1) * PP], rhs=sva[0:64, hh, :],
                                                                          start=(hh == 0), stop=False), [b_attnT[hh], bsla], [bpa])
                    for g in range(4):
                        Top(lambda e, pa=pa, g=g, j=j, svp=svp: e.matmul(pa[0:PP, :], lhsT=poolT[:, g, j * PP:(j + 1) * PP], rhs=svp[:, g, :],
                                                                        start=False, stop=(g == 3)), [b_poolT, bslp], [bpa])
                    Vop(lambda e, pa=pa, j=j, f=f: e.tensor_tensor(out=xh[0:PP, j, f * 512:(f + 1) * 512], in0=pa[0:PP, :], in1=xh[0:PP, j, f * 512:(f + 1) * 512], op=ALU.add),
                        [bpa, b_xh], [b_xh])
            stage('wout')
            P.inherit(F_bufs, A_bufs)
            norm_T(g2)
            hg = hist_g; hv = hist_v
            for i in range(11):
                sl, bsl = wget(base + 8 + i)
                sv = slotv(sl, 8)
                for (col0, kind, a) in ((0, 'g', 2 * i), (256, 'v', 2 * i), (128, 'g', 2 * i + 1), (384, 'v', 2 * i + 1)):
                    pa, bpa = nextpa()
                    for c in range(8):
                        Top(lambda e, pa=pa, c=c, col0=col0, sv=sv: e.matmul(pa[:, 0:NT], lhsT=sv[:, c, col0:col0 + 128], rhs=actT[:, c, 0:NT],
                                                                            start=(c == 0), stop=(c == 7)), [b_actT, bsl], [bpa])
                    ki = 0 if kind == 'g' else 1
                    cn = ccnt[kind]; ccnt[kind] += 1
                    nacc = len(acc[ki])
                    acf = acc[ki][cn % nacc]; bac = b_acc[ki][cn % nacc]
                    ac = acf[:, 0:NT].rearrange("p (s l) -> p s l", l=L)
                    cw = (cpg if kind == 'g' else cpv)
                    par = 0 if smp else (m % 2)
                    hbig = (hg if kind == 'g' else hv); hsml = (hist2_g if kind == 'g' else hist2_v)
                    hold = (hbig[:, a, 0:nseg * 2] if par == 0 else hsml[:, a, 0:2]).rearrange("p (s e) -> p s e", e=2)
                    hnew = (hsml[:, a, 0:2] if par == 0 else hbig[:, a, 0:2]).rearrange("p (s e) -> p s e", e=2)
                    bho = b_histd[(kind, a, par)]; bhn = b_histd[(kind, a, 1 - par)]
                    pav = pa[:, 0:NT].rearrange("p (s l) -> p s l", l=L)
                    if not smp:
                        Aop(lambda e, pav=pav, hnew=hnew: e.activation(out=hnew, in_=pav[:, :, L - 2:L], func=AF.Copy), [bpa], [bhn])
                    Aop(lambda e, pav=pav, ac=ac, cw=cw, a=a: e.activation(out=ac, in_=pav, func=AF.Identity, scale=cw[:, a, 2:3], bias=cw[:, a, 3:4]),
                        [bpa, b_const], [bac])
                    Vop(lambda e, pav=pav, ac=ac, cw=cw, a=a: e.scalar_tensor_tensor(out=ac[:, :, 1:L], in0=pav[:, :, 0:L - 1], scalar=cw[:, a, 1:2], in1=ac[:, :, 1:L],
                                                                                   op0=ALU.mult, op1=ALU.add), [bpa, b_const, bac], [bac])
                    Vop(lambda e, pav=pav, ac=ac, cw=cw, a=a: e.scalar_tensor_tensor(out=ac[:, :, 2:L], in0=pav[:, :, 0:L - 2], scalar=cw[:, a, 0:1], in1=ac[:, :, 2:L],
                                                                                   op0=ALU.mult, op1=ALU.add), [bpa, b_const, bac], [bac])
                    Vop(lambda e, hold=hold, ac=ac, cw=cw, a=a: e.scalar_tensor_tensor(out=ac[:, :, 0:2], in0=hold[:, :, 0:2], scalar=cw[:, a, 0:1], in1=ac[:, :, 0:2],
                                                                                     op0=ALU.mult, op1=ALU.add), [bho, b_const, bac], [bac])
                    Vop(lambda e, hold=hold, ac=ac, cw=cw, a=a: e.scalar_tensor_tensor(out=ac[:, :, 0:1], in0=hold[:, :, 1:2], scalar=cw[:, a, 1:2], in1=ac[:, :, 0:1],
                                                                                     op0=ALU.mult, op1=ALU.add), [bho, b_const, bac], [bac])
                    if kind == 'g':
                        sgi = cn % 2
                        pend_silu[0] = (acf, sgi, bac)
                    else:
                        sgi = cn % 2
                        acf_g, sgi_g, bac_g = pend_silu[0]
                        Aop(lambda e, acf_g=acf_g, sgi_g=sgi_g: e.activation(out=sg[sgi_g][:, 0:NT], in_=acf_g[:, 0:NT], func=AF.Silu), [bac_g], [b_sg[sgi_g]])
                        Gop(lambda e, acf=acf, a=a, sgi=sgi: e.tensor_tensor(out=gT[:, a, 0:NT], in0=sg[sgi][:, 0:NT], in1=acf[:, 0:NT], op=ALU.mult), [b_sg[sgi], bac], [b_gT])
                if last:
                    pa, bpa = nextpa()
                    for c in range(8):
                        Top(lambda e, pa=pa, c=c, sv=sv: e.matmul(pa[0:PP, :], lhsT=actT[:, c, NT - PP:NT], rhs=sv[:, c, :], start=(c == 0), stop=(c == 7)),
                            [b_actT, bsl], [bpa])
                    Aop(lambda e, pa=pa: e.activation(out=upst[0:PP, :], in_=pa[0:PP, :], func=AF.Copy), [bpa], [b_upst])
                    for (co, fo) in ((0, 256 * i), (256, 2816 + 256 * i)):
                        if smp:
                            for jj in range(2):
                                Dop(lambda e, co=co, fo=fo, jj=jj: e.dma_start(out=nfs[:, jj, fo:fo + 256], in_=upst[2 + jj:64:4, co:co + 256]), [b_upst], [], cho())
                        else:
                            Dop(lambda e, co=co, fo=fo: e.dma_start(out=nfp[s, :, fo:fo + 256], in_=upst[126:128, co:co + 256]), [b_upst], [], cho())
            stage('ffn_up')
            accs = [(pA[0], b_pA[0]), (pA[1], b_pA[1]), (pS[0], b_pS[0]), (pS[1], b_pS[1])]
            for f in range(2):
                for pc in range(3):
                    sl, bsl = wget(base + 19 + 3 * f + pc)
                    sv = slotv(sl, 8)
                    for kk in range(8 if pc < 2 else 6):
                        kc = pc * 8 + kk
                        for j in range(nsub):
                            Top(lambda e, j=j, kc=kc, kk=kk, sv=sv: e.matmul(accs[j][0][0:PP, :], lhsT=gT[:, kc, j * PP:(j + 1) * PP], rhs=sv[:, kk, :],
                                                                            start=(kc == 0), stop=(kc == 21)), [b_gT, bsl], [accs[j][1]])
                for j in range(nsub):
                    Vop(lambda e, j=j, f=f: e.tensor_tensor(out=xh[0:PP, j, f * 512:(f + 1) * 512], in0=accs[j][0][0:PP, :], in1=xh[0:PP, j, f * 512:(f + 1) * 512], op=ALU.add),
                        [accs[j][1], b_xh], [b_xh])
            if smp:
                ydst = ys.rearrange("(j p) d -> p j d", p=64)
            else:
                ydst = yp[s, T0:T0 + 512, :].rearrange("(j p) d -> p j d", p=128)
            Dop(lambda e: e.dma_start(out=ydst, in_=xh[0:PP, 0:nsub, :]), [b_xh], [], ch_y)

        qbd = sb("qbd", [128, 4, 16, 8], BF16); b_qbd = Buf()
        kTp = sb("kTp", [128, 4, 64], BF16); b_kTp = Buf()
        wt = sb("wt", [128, 224], F32); wtn = sb("wtn", [64, 16, 32], F32)
        vgrp = [None]

        def sample_prep():
            cload(wt[:], c_wt); cload(wtn[:], c_wtn)
            Vop(lambda e: e.memset(qbd[:].rearrange("p a b t -> p (a b t)"), 0.0), [], [b_qbd])
            sph = V16[0:120, 0:2, :]
            Dop(lambda e: e.dma_start(out=sph, in_=spool.rearrange("b i c -> (b i) c").rearrange("(two r) c -> r two c", r=120)), [], [b_V16], ch_s[0], q='pool')
            Dop(lambda e: e.dma_start(out=nps[:, 0:11, :], in_=spool[:, 4:15, :]), [], [], cho())
            for two in range(2):
                pt, bpt = nextpt()
                for g in range(4):
                    Top(lambda e, pt=pt, g=g, two=two: e.transpose(out=pt[:, g * 120:(g + 1) * 120], in_=V16[0:120, two, g * 128:(g + 1) * 128],
                                                                  identity=identb[0:120, 0:120]), [b_V16, b_const], [bpt])
                for g in range(4):
                    ue = uT[:, g, 0:320].rearrange("p (s e) -> p s e", e=20)
                    Aop(lambda e, pt=pt, g=g, two=two, ue=ue: e.activation(out=ue[:, 8 * two:8 * two + 8, 1:16], in_=pt[:, g * 120:(g + 1) * 120].rearrange("p (s i) -> p s i", i=15),
                                                                          func=AF.Copy), [bpt], [b_uT[g]])
            sfh = V16[0:32, 2:13, :].rearrange("p a t -> p (a t)")
            Dop(lambda e: e.dma_start(out=sfh[:, 0:5632], in_=sffn.rearrange("b j c -> (b j) c")), [], [b_V16], ch_s[1], q='pool')
            for kind in range(2):
                hs_ = hist_g if kind == 0 else hist_v
                for a0 in (0, 8, 16):
                    na = 8 if a0 < 16 else 6
                    pt, bpt = nextpt()
                    for a in range(a0, a0 + na):
                        f0 = a * 128 + (2816 if kind else 0)
                        Top(lambda e, pt=pt, a=a, a0=a0, f0=f0: e.transpose(out=pt[:, (a - a0) * 32:(a - a0 + 1) * 32], in_=sfh[0:32, f0:f0 + 128],
                                                                           identity=identb[0:32, 0:32]), [b_V16, b_const], [bpt])
                    Aop(lambda e, pt=pt, a0=a0, na=na, hs_=hs_: e.activation(out=hs_[:, a0:a0 + na, :], in_=pt[:, 0:na * 32].rearrange("p (a x) -> p a x", x=32),
                                                                             func=AF.Copy), [bpt], b_hist_all)

        pf = sb("pf", [128, 256], F32); b_pf = Buf()
        pts = sb("pts", [128, 256], BF16); b_pts = Buf()
        KcT = kTflat[:, 7168:10752].rearrange("p (a r) -> p a r", a=4); b_KcT = Buf()

        def sample_attention():
            P.inherit([b_Kc[0], b_Vc[0], b_KcT], [b_kT])
            Vop(lambda e: e.memset(pO[0:64, :], 0.0), [], [b_pO])
            Vop(lambda e: e.memset(pL[0:64, :], 0.0), [], [b_pL])
            for b in range(16):
                kc, bkc = Kc[b % 2], b_Kc[b % 2]
                vc, bvc = Vc[b % 2], b_Vc[b % 2]
                for (src, dstt, bd, chh) in ((ck, kc, bkc, ch_s[0]), (cv, vc, bvc, ch_s[1])):
                    grp = None
                    for r in range(4):
                        grp = Dop(lambda e, src=src, dstt=dstt, r=r, b=b: e.dma_start(out=dstt[r:128:4, 0:3, :],
                                                                                      in_=src[b, r:1536:16, :].rearrange("(tau a) c -> a tau c", a=32)),
                                  [], [bd], chh, group=grp, q='pool')
                    grp = Dop(lambda e, src=src, dstt=dstt, b=b: e.dma_start(out=dstt[:, 3:7, :], in_=src[b, 1536:2048, :].rearrange("(tau p) c -> p tau c", p=128)),
                              [], [bd], chh, group=grp, q='pool')
                for tau in range(7):
                    pt, bpt = nextpt()
                    for hp in range(4):
                        Top(lambda e, pt=pt, hp=hp, tau=tau, kc=kc: e.transpose(out=pt[:, hp * 128:(hp + 1) * 128], in_=kc[:, tau, hp * 128:(hp + 1) * 128], identity=identb[:]),
                            [bkc, b_const], [bpt])
                    Aop(lambda e, pt=pt, tau=tau: e.activation(out=KcT[:, :, tau * 128:(tau + 1) * 128], in_=pt[:, 0:512].rearrange("p (a r) -> p a r", a=4), func=AF.Copy),
                        [bpt], [b_KcT])
                ps, bps = pS[b % 2], b_pS[b % 2]
                for tau in range(7):
                    for hp in range(4):
                        Top(lambda e, ps=ps, tau=tau, hp=hp, b=b: e.matmul(ps[:, tau * 32 + hp * 8:tau * 32 + hp * 8 + 8], lhsT=KcT[:, hp, tau * 128:(tau + 1) * 128],
                                                                          rhs=qbd[:, hp, b, :], start=True, stop=True), [b_KcT, b_qbd], [bps])
                for hp in range(4):
                    Top(lambda e, ps=ps, hp=hp, b=b: e.matmul(ps[0:64, 224 + hp * 8:224 + hp * 8 + 8], lhsT=kTp[:, hp, 0:64], rhs=qbd[:, hp, b, :], start=True, stop=True),
                        [b_kTp, b_qbd], [bps])
                Aop(lambda e, ps=ps: e.activation(out=pf[:, 0:224], in_=ps[:, 0:224], func=AF.Exp, bias=negM[:, 0:1], scale=1.0), [bps, b_const], [b_pf])
                Aop(lambda e, ps=ps: e.activation(out=pf[0:64, 224:256], in_=ps[0:64, 224:256], func=AF.Exp, bias=negM[0:64, 0:1], scale=1.0), [bps, b_const], [b_pf])
                Vop(lambda e: e.tensor_tensor(out=pts[:, 0:224], in0=pf[:, 0:224], in1=wt[:, :], op=ALU.mult), [b_pf, b_const], [b_pts])
                Vop(lambda e, b=b: e.tensor_tensor(out=pts[0:64, 224:256], in0=pf[0:64, 224:256], in1=wtn[0:64, b, :], op=ALU.mult), [b_pf, b_const], [b_pts])
                for h in range(8):
                    o = pO[0:64, h * 64 + b * 4:h * 64 + b * 4 + 4]
                    for tau in range(7):
                        Top(lambda e, o=o, tau=tau, h=h, vc=vc: e.matmul(o, lhsT=vc[:, tau, h * 64:(h + 1) * 64], rhs=pts[:, tau * 32 + h * 4:tau * 32 + h * 4 + 4],
                                                                        start=False, stop=False, skip_group_check=True), [bvc, b_pts], [b_pO])
                    Top(lambda e, o=o, h=h: e.matmul(o, lhsT=Vn[0:64, 0, h * 64:(h + 1) * 64], rhs=pts[0:64, 224 + h * 4:224 + h * 4 + 4],
                                                     start=False, stop=False, skip_group_check=True), [b_Vn[0], b_pts], [b_pO])
                for tau in range(7):
                    Top(lambda e, tau=tau, b=b: e.matmul(pL[0:64, b * 32:(b + 1) * 32], lhsT=onesb[:, 0:64], rhs=pts[:, tau * 32:(tau + 1) * 32],
                                                         start=False, stop=False, skip_group_check=True), [b_const, b_pts], [b_pL])
                Top(lambda e, b=b: e.matmul(pL[0:64, b * 32:(b + 1) * 32], lhsT=onesb[0:64, 0:64], rhs=pts[0:64, 224:256],
                                            start=False, stop=False, skip_group_check=True), [b_const, b_pts], [b_pL])
            Vop(lambda e: e.reciprocal(out=rlb[0:64, :], in_=pL[0:64, :]), [b_pL], [b_rlb])
            for h in range(8):
                Vop(lambda e, h=h: e.tensor_tensor(out=attnT[0:64, h, 0:64].rearrange("p (b t) -> p b t", t=4),
                                                   in0=pO[0:64, h * 64:(h + 1) * 64].rearrange("p (b t) -> p b t", t=4),
                                                   in1=rlb[0:64, :].rearrange("p (b h t) -> p h b t", h=8, t=4)[:, h, :, :], op=ALU.mult),
                    [b_pO, b_rlb], [b_attnT[h]])

        tidx = 0
        try:
          stage('const')
          for s in range(2):
            for g in range(4):
                Gop(lambda e, g=g: e.memset(uT[:, g, 0:16], 0.0), [], [b_uT[g]])
            Gop(lambda e: e.memset(hist_g[:].rearrange("p a x -> p (a x)"), 0.0), [], b_hist_all)
            Gop(lambda e: e.memset(hist_v[:].rearrange("p a x -> p (a x)"), 0.0), [], b_hist_all)
            for m in range(4):
                do_tile(tidx, False, s, m)
                tidx += 1
                stage('tile%d' % (tidx - 1))
          sample_prep()
          stage('sprep')
          do_tile(tidx, True, 0, 0)
        except StopBuild:
            pass

        if os.environ.get('KDBG'):
            print('SBUF remaining', nc.sbuf_bytes_remaining)
        P.finalize()
        with nc.Block() as block:
            @block.tensor
            def _(e): P.run('pe', e)

            @block.scalar
            def _(e): P.run('act', e)

            @block.vector
            def _(e): P.run('dve', e)

            @block.gpsimd
            def _(e): P.run('pool', e)

            @block.sync
            def _(e):
                P.run('sp', e)
                for c in P.chans:
                    if c.count:
                        e.wait_ge(c.sem, c.count)
    return nc


def _consts():
    c = {}
    c["c_id"] = np.eye(128, dtype=np.float32).astype(BF)
    k = np.arange(128)[:, None]; q = np.arange(128)[None, :]
    c["c_mcur"] = np.tile(np.where(k <= q, 0.0, -30000.0).astype(np.float32), (1, 4)).astype(BF)
    c["c_mprev"] = np.tile(np.where(k >= q, 0.0, -30000.0).astype(np.float32), (1, 4)).astype(BF)
    m16 = np.zeros((128, 4, 16, 32), np.float32)
    for m in range(4):
        m16[:, m, :, :] = np.where(np.arange(128)[:, None] <= 32 * m + np.arange(32)[None, :], 0.0, -30000.0).astype(np.float32)[:, None, :]
    c["c_m16"] = m16.reshape(128, 4, 512).astype(BF)
    slopes = 2.0 ** (-np.arange(1, 9, dtype=np.float64))
    t = np.arange(2048)
    kaug = np.zeros((4, 8, 2048), np.float64)
    kaug[0] = (-64 * slopes)[:, None]; kaug[1] = (-slopes)[:, None]
    kaug[2] = 64 * slopes[:, None] * (t // 64)[None, :]; kaug[3] = slopes[:, None] * (t % 64)[None, :]
    c["c_kaug"] = kaug.astype(np.float32).astype(BF)
    qaug = np.zeros((4, 8, 2048), np.float32)
    qaug[0] = (t // 64)[None, :]; qaug[1] = (t % 64)[None, :]; qaug[2] = 1; qaug[3] = 1
    c["c_qaug"] = qaug.astype(BF)
    inv = np.zeros((128, 4, 16), np.float32)
    for g, w in enumerate((2, 4, 8, 16)):
        inv[:, g, :] = 1.0 / np.minimum(w, np.arange(16) + 1)
    c["c_invcnt"] = inv

    def cnt(d):
        d = np.asarray(d)
        return ((d >= 0) & (d <= 128)).astype(np.float64) + ((d >= 0) & (d % 4 == 0) & (d <= 512)) + ((d >= 0) & (d % 16 == 0) & (d <= 2048))
    p = np.arange(128)
    wt = np.zeros((128, 7, 8, 4), np.float64)
    for tau in range(7):
        row = 16 * (32 * tau + p // 4) + p % 4 if tau < 3 else 1536 + 128 * (tau - 3) + p
        for tt in range(4):
            d = 2048 + tt - row
            for h in range(8):
                wt[:, tau, h, tt] = cnt(d) * np.exp(-slopes[h] * d)
    c["c_wt"] = wt.reshape(128, 224).astype(np.float32)
    wtn = np.zeros((64, 16, 8, 4), np.float64)
    for b in range(16):
        for t1 in range(4):
            for tt in range(4):
                d = tt - t1
                if d >= 0:
                    wtn[4 * b + t1, b, :, tt] = cnt(d) * np.exp(-slopes * d)
    c["c_wtn"] = wtn.reshape(64, 16, 32).astype(np.float32)
    return c


_NC = [None]


def kernel(x_prompt, x_sample, cache_k, cache_v, state_pool, state_ffn_conv, g_attn_norm, w_in, g_q, g_k,
           w_pool, pool_scale, w_out, g_ffn_norm, w_up, conv_w, conv_b, w_down):
    f = lambda a: np.ascontiguousarray(np.asarray(a, dtype=np.float32))
    nc = build()
    consts = _consts()
    shared = dict(
        g_attn=f(g_attn_norm).reshape(1, 1024), w_in=f(w_in)[0], gq=np.ascontiguousarray(np.broadcast_to(f(g_q).reshape(1, 512), (128, 512))),
        gk=np.ascontiguousarray(np.broadcast_to(f(g_k).reshape(1, 512), (128, 512))), w_pool=f(w_pool)[0], pool_scale=f(pool_scale).reshape(1, 512),
        w_out=f(w_out)[0], g_ffn=f(g_ffn_norm).reshape(1, 1024), w_up=f(w_up)[0], conv_w=f(conv_w)[0], conv_b=f(conv_b).reshape(1, 5632),
        w_down=f(w_down)[0], **consts)
    xp_ = f(x_prompt); xs_ = f(x_sample); ck_ = f(cache_k)[0]; cv_ = f(cache_v)[0]; sp_ = f(state_pool)[0]; sf_ = f(state_ffn_conv)[0]
    in_maps = []
    for i in range(8):
        d = dict(shared)
        d["xp"] = xp_[2 * i:2 * i + 2]
        d["xs"] = xs_[16 * i:16 * i + 16].reshape(64, 1024)
        d["ck"] = ck_[16 * i:16 * i + 16].reshape(16, 2048, 512)
        d["cv"] = cv_[16 * i:16 * i + 16].reshape(16, 2048, 512)
        d["spool"] = sp_[16 * i:16 * i + 16]
        d["sffn"] = sf_[16 * i:16 * i + 16]
        in_maps.append(d)
    res = run_bass_kernel_spmd(nc, in_maps, core_ids=list(range(8)))
    R = res.results
    cat = lambda k: np.concatenate([np.asarray(r[k], dtype=np.float32) for r in R], axis=0)
    y_p = cat("yp"); y_s = cat("ys").reshape(128, 4, 1024)
    nk_p = cat("nkp").reshape(1, 16, 2048, 8, 64); nv_p = cat("nvp").reshape(1, 16, 2048, 8, 64)
    np_p = cat("npp").reshape(1, 16, 15, 512); nf_p = cat("nfp").reshape(1, 16, 2, 5632)
    nk_s = cat("nks").reshape(1, 128, 4, 8, 64); nv_s = cat("nvs").reshape(1, 128, 4, 8, 64)
    np_s = cat("nps").reshape(1, 128, 15, 512); nf_s = cat("nfs").reshape(1, 128, 2, 5632)
    return (y_p, y_s, nk_p, nv_p, np_p, nf_p, nk_s, nv_s, np_s, nf_s)
```

```python
import os
import numpy as np
import ml_dtypes
from contextlib import ExitStack
import concourse.bass as bass
import concourse.mybir as mybir
from concourse.bass_utils import run_bass_kernel_spmd

F32 = mybir.dt.float32
BF16 = mybir.dt.bfloat16
AF = mybir.ActivationFunctionType
ALU = mybir.AluOpType
AX = mybir.AxisListType
EPS = 1e-6
NSLOT = 3
BF = ml_dtypes.bfloat16


class Buf:
    __slots__ = ('w', 'r')

    def __init__(s):
        s.w = None
        s.r = {}


class Group:
    def __init__(s, chan):
        s.chan = chan
        s.total = None


class Chan:
    def __init__(s, sem):
        s.sem = sem
        s.count = 0
        s.last = None


class Prog:
    ENG = ['pe', 'act', 'dve', 'pool', 'sp']

    def __init__(s, nc, es):
        s.nc = nc
        s.q = {e: [] for e in s.ENG}
        s.sem = {e: es.enter_context(nc.semaphore('sem_' + e)) for e in s.ENG if e != 'sp'}
        s.es = es
        s.chans = []

    def chan(s):
        c = Chan(s.es.enter_context(s.nc.semaphore('ch%d' % len(s.chans))))
        s.chans.append(c)
        return c

    def op(s, eng, fn, reads=(), writes=(), chan=None, group=None):
        q = s.q[eng]
        idx = len(q)
        deps = set()
        if chan is not None:
            if group is None:
                group = Group(chan)
            if chan.last is not group:
                if chan.last is not None:
                    deps.add(('dma', chan.last))
                chan.last = group
            chan.count += 16
            group.total = chan.count
            me = ('dma', group)
            mekey = ('dma', id(chan))
        else:
            me = (eng, idx)
            mekey = eng
        isdma = chan is not None
        for b in reads:
            if b.w is not None and b.w != me:
                if not (b.w[0] == 'pe' and eng == 'pe' and not isdma):
                    deps.add(b.w)
        for b in writes:
            if b.w is not None and b.w != me:
                if b.w[0] == 'dma' or isdma or b.w[0] != eng:
                    deps.add(b.w)
            for k, v in b.r.items():
                if v == me:
                    continue
                if v[0] == 'dma' or isdma or v[0] != eng:
                    deps.add(v)
        for b in reads:
            b.r[mekey] = me
        for b in writes:
            b.w = me
            b.r = {}
        q.append((fn, deps, chan))
        return group

    def inherit(s, dsts, srcs):
        for d in dsts:
            for b in srcs:
                if b.w is not None:
                    d.r[('w', id(b))] = b.w
                for k, v in b.r.items():
                    d.r[(k, id(b))] = v

    def finalize(s):
        sig = {e: set() for e in s.ENG}
        for e in s.ENG:
            for (fn, deps, chan) in s.q[e]:
                for d in deps:
                    if d[0] != 'dma':
                        sig[d[0]].add(d[1])
        s.rank = {}
        for e in s.ENG:
            for i, idx in enumerate(sorted(sig[e])):
                s.rank[(e, idx)] = i + 1

    def run(s, e, engobj):
        waited = {}
        for idx, (fn, deps, chan) in enumerate(s.q[e]):
            need = {}
            for d in deps:
                if d[0] == 'dma':
                    sm, val = d[1].chan.sem, d[1].total
                else:
                    sm, val = s.sem[d[0]], s.rank[d]
                k = id(sm)
                if need.get(k, (None, 0))[1] < val:
                    need[k] = (sm, val)
            for k, (sm, val) in need.items():
                if waited.get(k, 0) < val:
                    engobj.wait_ge(sm, val)
                    waited[k] = val
            ins = fn(engobj)
            if (e, idx) in s.rank:
                ins.then_inc(s.sem[e], 1)
            if chan is not None:
                ins.then_inc(chan.sem, 16)


class StopBuild(Exception):
    pass


def build(stop=None):
    def stage(name):
        if stop == name:
            raise StopBuild()
    nc = bass.Bass("TRN2", target_bir_lowering=False)

    def din(name, shape, dtype=F32):
        return nc.dram_tensor(name, shape, dtype, kind="ExternalInput").ap()

    def dout(name, shape):
        return nc.dram_tensor(name, shape, F32, kind="ExternalOutput").ap()

    def dscr(name, shape):
        return nc.dram_tensor(name, shape, BF16, kind="Internal").ap()

    xp = din("xp", [2, 2048, 1024]); xs = din("xs", [64, 1024])
    ck = din("ck", [16, 2048, 512]); cv = din("cv", [16, 2048, 512])
    spool = din("spool", [16, 15, 512]); sffn = din("sffn", [16, 2, 5632])
    g_attn = din("g_attn", [1, 1024]); w_in = din("w_in", [1024, 2048])
    gq_in = din("gq", [128, 512]); gk_in = din("gk", [128, 512])
    w_pool = din("w_pool", [4, 128, 128]); pool_scale = din("pool_scale", [1, 512])
    w_out = din("w_out", [1024, 1024]); g_ffn = din("g_ffn", [1, 1024])
    w_up = din("w_up", [1024, 5632]); conv_w = din("conv_w", [3, 5632]); conv_b = din("conv_b", [1, 5632])
    w_down = din("w_down", [2816, 1024])
    c_id = din("c_id", [128, 128], BF16); c_mcur = din("c_mcur", [128, 512], BF16)
    c_mprev = din("c_mprev", [128, 512], BF16); c_m16 = din("c_m16", [128, 4, 512], BF16)
    c_kaug = din("c_kaug", [4, 8, 2048], BF16); c_qaug = din("c_qaug", [4, 8, 2048], BF16)
    c_invcnt = din("c_invcnt", [128, 4, 16]); c_wt = din("c_wt", [128, 224]); c_wtn = din("c_wtn", [64, 16, 32])

    yp = dout("yp", [2, 2048, 1024]); ys = dout("ys", [64, 1024])
    nkp = dout("nkp", [2, 2048, 512]); nvp = dout("nvp", [2, 2048, 512])
    npp = dout("npp", [2, 15, 512]); nfp = dout("nfp", [2, 2, 5632])
    nks = dout("nks", [64, 512]); nvs = dout("nvs", [64, 512])
    nps = dout("nps", [16, 15, 512]); nfs = dout("nfs", [16, 2, 5632])

    win_s = dscr("win_s", [4, 128, 8, 512]); woa_s = dscr("woa_s", [2, 64, 8, 512]); wop_s = dscr("wop_s", [2, 128, 4, 512])
    wup_s = dscr("wup_s", [11, 128, 8, 512]); wdn_s = dscr("wdn_s", [2, 3, 128, 8, 512])

    with ExitStack() as es:
        def sb(name, shape, dtype):
            return es.enter_context(nc.sbuf_tensor(name, shape, dtype))

        P = Prog(nc, es)
        xh = sb("xh", [128, 4, 1024], F32); b_xhs = [Buf() for _ in range(4)]; b_xh = b_xhs[0]
        xn = sb("xn", [128, 4, 1024], BF16); b_xn = Buf()
        actT = sb("actT", [128, 8, 512], BF16); b_actT = Buf()
        ss = sb("ss", [128, 16], F32); b_ss = Buf()
        ssqs = [sb("ssq", [128, 32], F32), sb("ssqb", [128, 32], F32)]; b_ssqs = [Buf(), Buf()]; sqc = [0]
        kT = sb("kT", [128, 8, 2048], BF16); b_kT = Buf()
        Vn = sb("Vn", [128, 5, 512], BF16); b_Vn = [Buf() for _ in range(5)]
        V4 = sb("V4", [128, 8, 512], BF16); b_V4 = Buf()
        V16 = sb("V16", [128, 16, 512], BF16); b_V16 = Buf()
        uT = sb("uT", [128, 4, 528], F32); b_uT = [Buf() for _ in range(4)]
        attnT = sb("attnT", [128, 8, 512], BF16); b_attnT = [Buf() for _ in range(8)]
        poolT = sb("poolT", [128, 4, 512], BF16); b_poolT = Buf()
        ring = sb("ring", [128, NSLOT, 4096], BF16); b_ring = [Buf() for _ in range(NSLOT)]
        identb = sb("identb", [128, 128], BF16); mcur = sb("mcur", [128, 512], BF16); mprev = sb("mprev", [128, 512], BF16)
        m16 = sb("m16", [128, 4, 512], BF16); onesb = sb("onesb", [128, 64], BF16)
        g1 = sb("g1", [128, 8], F32); g2 = sb("g2", [128, 8], F32)
        gqb = sb("gqb", [128, 512], F32); gkb = sb("gkb", [128, 512], F32); negM = sb("negM", [128, 4], F32)
        cpg = sb("cpg", [128, 22, 4], F32); cpv = sb("cpv", [128, 22, 4], F32)
        hist_g = sb("hist_g", [128, 22, 32], F32); hist_v = sb("hist_v", [128, 22, 32], F32)
        b_hist = Buf()
        hist2_g = sb("hist2_g", [128, 22, 2], F32); hist2_v = sb("hist2_v", [128, 22, 2], F32)
        pscale = sb("pscale", [128, 4], F32); wpool = sb("wpool", [128, 4, 128], BF16); invcnt = sb("invcnt", [128, 4, 16], F32)
        b_const = Buf()
        arena = sb("arena", [128, 21696], BF16)
        junk = sb("junk", [128, 1024], BF16); b_junk = Buf()
        off = [0]

        def carve(nbf16, reset=False):
            if reset:
                off[0] = 0
            a = arena[:, off[0]:off[0] + nbf16]
            off[0] += nbf16
            assert off[0] <= 21696
            return a
        qT = carve(8 * 512, True).rearrange("p (h t) -> p h t", h=8); b_qT = Buf()
        tmpn = carve(1024).bitcast(F32); b_tmpn = Buf()
        ksts = [carve(1024).bitcast(F32), carve(1024).bitcast(F32)]; b_ksts = [Buf(), Buf()]
        vsts = [carve(1024).bitcast(F32), carve(1024).bitcast(F32)]; b_vsts = [Buf(), Buf()]
        kvc = [0]
        qn_bfs = [carve(512), carve(512)]; b_qns = [Buf(), Buf()]
        kn_bfs = [carve(512), carve(512)]; b_kns = [Buf(), Buf()]
        nbc = [0]
        PT = [carve(512) for _ in range(3)]; b_PT = [Buf() for _ in range(3)]
        dT = carve(4 * 512).rearrange("p (g t) -> p g t", g=4); b_dT = Buf()
        sa = carve(1056).bitcast(F32); sbb = carve(1056).bitcast(F32); b_s = Buf()
        rlb = carve(1024).bitcast(F32); b_rlb = Buf()
        rlb2 = carve(1024).bitcast(F32); b_rlb2 = Buf()
        rlbs = [rlb, rlb2]; b_rlbs = [b_rlb, b_rlb2]
        tmp16 = carve(32).bitcast(F32)
        A_bufs = [b_qT, b_tmpn, b_dT, b_s, b_rlb, b_rlb2] + b_PT + b_qns + b_kns + b_ksts + b_vsts
        gT = carve(22 * 512, True).rearrange("p (k t) -> p k t", k=22); b_gT = Buf()
        ext = [[carve(1056).bitcast(F32) for _ in range(2)] for _ in range(2)]; b_ext = [[Buf(), Buf()], [Buf(), Buf()]]
        acc = [[carve(1024).bitcast(F32)], [carve(1024).bitcast(F32) for _ in range(2)]]; b_acc = [[Buf()], [Buf(), Buf()]]
        sg = [carve(1024).bitcast(F32) for _ in range(2)]; b_sg = [Buf(), Buf()]
        b_histd = {(k_, a_, p_): Buf() for k_ in 'gv' for a_ in range(22) for p_ in range(2)}
        b_hist_all = list(b_histd.values())
        ccnt = {'g': 0, 'v': 0}
        pend_silu = [None]
        pend_evac = [None]
        pend_tr = [None]
        upst = carve(1024).bitcast(F32); b_upst = Buf()
        F_bufs = [b_gT, b_upst] + b_sg + b_ext[0] + b_ext[1] + b_acc[0] + b_acc[1]
        kTflat = kT[:].rearrange("p h t -> p (h t)")
        Kc = [kTflat[:, 0:3584].rearrange("p (a c) -> p a c", a=7)] * 2; b_Kc = [Buf()] * 2
        Vc = [kTflat[:, 3584:7168].rearrange("p (a c) -> p a c", a=7)] * 2; b_Vc = [Buf()] * 2
        pbank = [es.enter_context(nc.psum_tensor("pb%d" % i, [128, 512], F32)) for i in range(8)]
        pA = [pbank[0], pbank[1]]; b_pA = [Buf(), Buf()]
        pTb = [pbank[2][:].bitcast(BF16), pbank[7][:].bitcast(BF16)]; b_pT = [Buf(), Buf()]
        pS = [pbank[3], pbank[4]]; b_pS = [Buf(), Buf()]
        pO = pbank[5]; b_pO = Buf()
        pL = pbank[6]; b_pL = Buf()

        def Aop(fn, r, w): P.op('act', fn, r, w)
        def Vop(fn, r, w): P.op('dve', fn, r, w)
        def Gop(fn, r, w): P.op('pool', fn, r, w)
        def Top(fn, r, w): P.op('pe', fn, r, w)
        def Dop(fn, r, w, ch, group=None, q='sp'): return P.op(q, fn, r, w, chan=ch, group=group)

        ch_c = [P.chan() for _ in range(4)]
        ch_xs = [P.chan() for _ in range(4)]; ch_ys = [P.chan() for _ in range(4)]; ch_k = P.chan(); ch_vo = P.chan(); ch_v = P.chan(); ch_q = P.chan()
        ch_w = [P.chan() for _ in range(NSLOT)]
        ch_pro = [P.chan() for _ in range(14)]
        ch_ol = [P.chan() for _ in range(4)]; ch_oc = [0]; ch_s = [P.chan(), P.chan()]

        def cho():
            ch_oc[0] += 1
            return ch_ol[ch_oc[0] % 4]

        cc = [0]

        def cload(out, in_, slow=False, q='sp'):
            c = ch_c[cc[0] % 4]; cc[0] += 1
            if slow:
                Dop(lambda e: e.dma_start(out=out, in_=in_, allow_slow_non_contiguous=True), [], [b_const], c, q=q)
            else:
                Dop(lambda e: e.dma_start(out=out, in_=in_), [], [b_const], c, q=q)
        cload(identb[:], c_id); cload(mcur[:], c_mcur); cload(mprev[:], c_mprev); cload(m16[:], c_m16)
        cload(gqb[:], gq_in); cload(gkb[:], gk_in); cload(invcnt[:], c_invcnt)
        cload(g1[:], g_attn[0, :].rearrange("(c p) -> p c", p=128), slow=True)
        cload(g2[:], g_ffn[0, :].rearrange("(c p) -> p c", p=128), slow=True)
        cload(pscale[:], pool_scale[0, :].rearrange("(g p) -> p g", p=128), slow=True)
        for j in range(3):
            cload(cpg[:, :, j], conv_w[j, 0:2816].rearrange("(a p) -> p a", p=128), slow=True)
            cload(cpv[:, :, j], conv_w[j, 2816:5632].rearrange("(a p) -> p a", p=128), slow=True)
        cload(cpg[:, :, 3], conv_b[0, 0:2816].rearrange("(a p) -> p a", p=128), slow=True)
        cload(cpv[:, :, 3], conv_b[0, 2816:5632].rearrange("(a p) -> p a", p=128), slow=True)
        cload(wpool[:], w_pool.rearrange("g c d -> c g d"), q='pool')
        Dop(lambda e: e.dma_start(out=kT[64:68, :, :], in_=c_kaug), [], [b_kT], ch_c[0])
        Vop(lambda e: e.tensor_scalar(out=g1[:], in0=g1[:], scalar1=32.0, scalar2=None, op0=ALU.mult), [b_const], [b_const])
        Vop(lambda e: e.tensor_scalar(out=g2[:], in0=g2[:], scalar1=32.0, scalar2=None, op0=ALU.mult), [b_const], [b_const])
        Vop(lambda e: e.memset(onesb[:], 1.0), [], [b_const])
        Vop(lambda e: e.tensor_tensor(out=xh[:, 0, 0:512], in0=gqb[:], in1=gqb[:], op=ALU.mult), [b_const], [b_xh])
        Vop(lambda e: e.reduce_max(out=negM[:, 1:2], in_=xh[:, 0, 0:512], axis=AX.X), [b_xh], [b_const])
        Vop(lambda e: e.tensor_tensor(out=xh[:, 0, 0:512], in0=gkb[:], in1=gkb[:], op=ALU.mult), [b_const], [b_xh])
        Vop(lambda e: e.reduce_max(out=negM[:, 2:3], in_=xh[:, 0, 0:512], axis=AX.X), [b_xh], [b_const])
        Vop(lambda e: e.tensor_tensor(out=negM[:, 3:4], in0=negM[:, 1:2], in1=negM[:, 2:3], op=ALU.add), [b_const], [b_const])
        Vop(lambda e: e.tensor_scalar(out=negM[:, 0:1], in0=negM[:, 3:4], scalar1=-4.0, scalar2=None, op0=ALU.mult), [b_const], [b_const])
        Vop(lambda e: e.tensor_scalar(out=gkb[:], in0=gkb[:], scalar1=8.0, scalar2=None, op0=ALU.mult), [b_const], [b_const])

        b_scr = {}
        pc_ = [0]

        def pro(key, out, in_):
            c = ch_pro[pc_[0] % 14]; pc_[0] += 1
            b = b_scr.setdefault(key, Buf())
            Dop(lambda e: e.dma_start(out=out, in_=in_), [], [b], c, q='pool')
        for g in range(4):
            pro(('in', g), win_s[g], w_in[:, g * 512:(g + 1) * 512].rearrange("(c p) n -> p c n", p=128))
        for f in range(2):
            pro(('oa', f), woa_s[f], w_out[0:512, f * 512:(f + 1) * 512].rearrange("(h p) n -> p h n", p=64))
            pro(('op', f), wop_s[f], w_out[512:1024, f * 512:(f + 1) * 512].rearrange("(g p) n -> p g n", p=128))
        def late_pro():
            for i in range(11):
                pro(('up', i), wup_s[i, :, :, 0:256], w_up[:, 256 * i:256 * i + 256].rearrange("(c p) n -> p c n", p=128))
                pro(('up', i), wup_s[i, :, :, 256:512], w_up[:, 2816 + 256 * i:2816 + 256 * i + 256].rearrange("(c p) n -> p c n", p=128))
            for f in range(2):
                for pc in range(3):
                    nk = 8 if pc < 2 else 6
                    pro(('dn', f, pc), wdn_s[f, pc, :, 0:nk, :],
                        w_down[pc * 1024:pc * 1024 + nk * 128, f * 512:(f + 1) * 512].rearrange("(k p) n -> p k n", p=128))

        piece_list = []
        NT_TILES = 9
        for t in range(NT_TILES):
            for g in (2, 0, 1, 3):
                piece_list.append((('in', g), win_s[g], 128, 4096))
            for f in range(2):
                piece_list.append((('oa', f), woa_s[f], 64, 4096))
                piece_list.append((('op', f), wop_s[f], 128, 2048))
            for i in range(11):
                piece_list.append((('up', i), wup_s[i], 128, 4096))
            for f in range(2):
                for pc in range(3):
                    piece_list.append((('dn', f, pc), wdn_s[f, pc], 128, 4096))
        issued = [0]

        def wget(i, hold=0):
            while issued[0] < min(len(piece_list), i - hold + NSLOT):
                k = issued[0]; issued[0] += 1
                key, src, npart, nel = piece_list[k]
                sl = k % NSLOT
                o = ring[0:npart, sl, 0:nel]
                s2 = src.rearrange("p a n -> p (a n)")
                Dop(lambda e, o=o, s2=s2: e.dma_start(out=o, in_=s2), [b_scr[key]], [b_ring[sl]], ch_w[sl])
            return i % NSLOT, b_ring[i % NSLOT]

        def slotv(sl, a):
            return ring[:, sl, 0:a * 512].rearrange("p (a n) -> p a n", n=512)

        pac = [0]

        rot6 = [(pA[0], b_pA[0]), (pA[1], b_pA[1]), (pS[0], b_pS[0]), (pS[1], b_pS[1]), (pO, b_pO), (pL, b_pL)]

        def nextpa():
            i = pac[0] % 6; pac[0] += 1
            return rot6[i]
        ptc = [0]

        def nextpt():
            i = ptc[0] % 2; ptc[0] += 1
            return pTb[i], b_pT[i]

        def do_tile(tidx, smp, s, m):
            NT = 64 if smp else 512; nsub = 1 if smp else 4; PP = 64 if smp else 128
            nseg = 16 if smp else 1; L = 4 if smp else 512; E = 16 + L
            T0 = 0 if smp else 512 * m
            base = tidx * 25
            last = smp or m == 3
            P.inherit(A_bufs, F_bufs)
            vgrp[0] = None
            if smp:
                xsrc = xs.rearrange("(j p) d -> p j d", p=64)
            else:
                xsrc = xp[s, T0:T0 + 512, :].rearrange("(j p) d -> p j d", p=128)
            for j in range(nsub):
                Dop(lambda e, j=j: e.dma_start(out=xh[0:PP, j, :], in_=xsrc[:, j, :]), [], [b_xhs[j]], ch_xs[j])
            if not smp:
                Dop(lambda e: e.dma_start(out=qT[64:68, :, :], in_=c_qaug[:, :, T0:T0 + 512]), [], [b_qT], ch_q)

            def norm_T(g):
                Vop(lambda e: e.memset(ss[:, 0:4], 0.0), [], [b_ss])
                for j in range(nsub):
                    Aop(lambda e, j=j: e.activation(out=junk[0:PP, :], in_=xh[0:PP, j, :], func=AF.Square, accum_out=ss[0:PP, j:j + 1]),
                        [b_xhs[j], b_ss], [b_junk, b_ss])
                Vop(lambda e: e.tensor_scalar(out=ss[0:PP, 4:8], in0=ss[0:PP, 0:4], scalar1=1024 * EPS, scalar2=None, op0=ALU.add), [b_ss], [b_ss])
                Aop(lambda e: e.activation(out=ss[0:PP, 8:12], in_=ss[0:PP, 4:8], func=AF.Ln), [b_ss], [b_ss])
                Aop(lambda e: e.activation(out=ss[0:PP, 12:16], in_=ss[0:PP, 8:12], func=AF.Exp, scale=-0.5), [b_ss], [b_ss])
                for j in range(nsub):
                    Aop(lambda e, j=j: e.activation(out=xn[0:PP, j, :], in_=xh[0:PP, j, :], func=AF.Copy, scale=ss[0:PP, 12 + j:13 + j]),
                        [b_xhs[j], b_ss], [b_xn])
                for c in range(8):
                    pt, bpt = nextpt()
                    for j in range(nsub):
                        Top(lambda e, pt=pt, j=j, c=c: e.transpose(out=pt[:, j * PP:(j + 1) * PP], in_=xn[0:PP, j, c * 128:(c + 1) * 128],
                                                                  identity=identb[0:PP, 0:PP]), [b_xn, b_const], [bpt])
                    Vop(lambda e, pt=pt, c=c: e.tensor_scalar(out=actT[:, c, 0:NT], in0=pt[:, 0:NT], scalar1=g[:, c:c + 1], scalar2=None, op0=ALU.mult),
                        [bpt, b_const], [b_actT])
            norm_T(g1)
            stage('norm1')

            for pos_, gi in enumerate((2, 0, 1)):
                sl, bsl = wget(base + pos_)
                sv = slotv(sl, 8)
                for j in range(nsub):
                    pa, bpa = nextpa()
                    for c in range(8):
                        Top(lambda e, pa=pa, j=j, c=c, sv=sv: e.matmul(pa[0:PP, :], lhsT=actT[:, c, j * PP:(j + 1) * PP], rhs=sv[:, c, :],
                                                                      start=(c == 0), stop=(c == 7)), [b_actT, bsl], [bpa])
                    if pend_tr[0] is not None:
                        pend_tr[0](); pend_tr[0] = None
                    rows = slice(T0 + j * 128, T0 + j * 128 + 128)
                    if gi < 2:
                        ssq = ssqs[sqc[0] % 2]; b_ssq = b_ssqs[sqc[0] % 2]; sqc[0] += 1
                        Vop(lambda e, ssq=ssq: e.memset(ssq[:, 0:8], 0.0), [], [b_ssq])
                        for h in range(8):
                            Aop(lambda e, pa=pa, h=h, ssq=ssq: e.activation(out=junk[0:PP, 0:64], in_=pa[0:PP, h * 64:(h + 1) * 64], func=AF.Square,
                                                                   accum_out=ssq[0:PP, h:h + 1]), [bpa, b_ssq], [b_junk, b_ssq])
                        Vop(lambda e, ssq=ssq: e.tensor_scalar(out=ssq[0:PP, 8:16], in0=ssq[0:PP, 0:8], scalar1=64 * EPS, scalar2=None, op0=ALU.add), [b_ssq], [b_ssq])
                        Aop(lambda e, ssq=ssq: e.activation(out=ssq[0:PP, 16:24], in_=ssq[0:PP, 8:16], func=AF.Ln), [b_ssq], [b_ssq])
                        Aop(lambda e, ssq=ssq: e.activation(out=ssq[0:PP, 24:32], in_=ssq[0:PP, 16:24], func=AF.Exp, scale=-0.5), [b_ssq], [b_ssq])
                        if pend_evac[0] is not None:
                            pend_evac[0](); pend_evac[0] = None
                        for h in range(8):
                            Vop(lambda e, pa=pa, h=h, ssq=ssq: e.tensor_scalar(out=tmpn[0:PP, h * 64:(h + 1) * 64], in0=pa[0:PP, h * 64:(h + 1) * 64],
                                                                      scalar1=ssq[0:PP, 24 + h:25 + h], scalar2=None, op0=ALU.mult), [bpa, b_ssq], [b_tmpn])
                        nbi = nbc[0] % 2; nbc[0] += 1
                        if gi == 0:
                            nb, bnb = qn_bfs[nbi], b_qns[nbi]
                            Vop(lambda e, nb=nb: e.tensor_tensor(out=nb[0:PP, :], in0=tmpn[0:PP, :], in1=gqb[0:PP, :], op=ALU.mult), [b_tmpn, b_const], [bnb])
                        else:
                            nb, bnb = kn_bfs[nbi], b_kns[nbi]
                            kst = ksts[nbi]; b_kst = b_ksts[nbi]
                            Vop(lambda e, kst=kst: e.tensor_tensor(out=kst[0:PP, :], in0=tmpn[0:PP, :], in1=gkb[0:PP, :], op=ALU.mult), [b_tmpn, b_const], [b_kst])
                            Vop(lambda e, nb=nb, kst=kst: e.tensor_copy(out=nb[0:PP, :], in_=kst[0:PP, :]), [b_kst], [bnb])
                            dst = nks if smp else nkp[s, rows, :]
                            Dop(lambda e, dst=dst, kst=kst: e.dma_start(out=dst, in_=kst[0:PP, :]), [b_kst], [], ch_k)
                        if not smp:
                            def tr(nb=nb, bnb=bnb, gi=gi, j=j, rows=rows):
                                pt, bpt = nextpt()
                                for h in range(8):
                                    Top(lambda e, h=h: e.transpose(out=pt[0:64, h * 128:(h + 1) * 128], in_=nb[:, h * 64:(h + 1) * 64], identity=identb[:]),
                                        [bnb, b_const], [bpt])
                                src = pt[0:64, :].rearrange("p (h t) -> p h t", t=128)
                                if gi == 0:
                                    pend_evac[0] = (lambda: Aop(lambda e: e.activation(out=qT[0:64, :, j * 128:(j + 1) * 128], in_=src, func=AF.Copy), [bpt], [b_qT]))
                                else:
                                    pend_evac[0] = (lambda: Aop(lambda e: e.activation(out=kT[0:64, :, rows], in_=src, func=AF.Copy), [bpt], [b_kT]))
                            pend_tr[0] = tr
                        else:
                            pt, bpt = nextpt()
                            for hp in range(4):
                                Top(lambda e, pt=pt, hp=hp, nb=nb: e.transpose(out=pt[:, hp * 64:(hp + 1) * 64], in_=nb[0:64, hp * 128:(hp + 1) * 128],
                                                                              identity=identb[0:64, 0:64]), [bnb, b_const], [bpt])
                            if gi == 0:
                                for hh in range(2):
                                    Aop(lambda e, pt=pt, hh=hh: e.activation(
                                        out=qbd[hh * 64:(hh + 1) * 64, :, :, hh * 4:hh * 4 + 4],
                                        in_=pt[hh * 64:(hh + 1) * 64, 0:256].rearrange("p (a b t) -> p a b t", a=4, b=16), func=AF.Copy), [bpt], [b_qbd])
                            else:
                                Aop(lambda e, pt=pt: e.activation(out=kTp[:, :, :], in_=pt[:, 0:256].rearrange("p (a t) -> p a t", a=4), func=AF.Copy), [bpt], [b_kTp])
                    else:
                        B = (4 * m + j) if not smp else 0
                        vs = B % 5
                        vst = vsts[kvc[0] % 2]; b_vst = b_vsts[kvc[0] % 2]; kvc[0] += 1
                        Aop(lambda e, pa=pa, vst=vst: e.activation(out=vst[0:PP, :], in_=pa[0:PP, :], func=AF.Copy), [bpa], [b_vst])
                        dst = nvs if smp else nvp[s, rows, :]
                        Dop(lambda e, dst=dst, vst=vst: e.dma_start(out=dst, in_=vst[0:PP, :]), [b_vst], [], ch_vo)
                        Vop(lambda e, vs=vs, vst=vst: e.tensor_copy(out=Vn[0:PP, vs, :], in_=vst[0:PP, :]), [b_vst], [b_Vn[vs]])
                        if not smp and os.environ.get('NOVDMA') is None:
                            grp = vgrp[0]
                            for c4 in range(4):
                                grp = Dop(lambda e, c4=c4, vs=vs, j=j: e.dma_start(out=V4[32 * j:32 * j + 32, (m % 2) * 4 + c4, :], in_=Vn[c4:128:4, vs, :]),
                                          [b_Vn[vs]], [b_V4], ch_v, group=grp, q='pool')
                            for c16 in range(16):
                                grp = Dop(lambda e, c16=c16, vs=vs, j=j: e.dma_start(out=V16[32 * m + 8 * j:32 * m + 8 * j + 8, c16, :], in_=Vn[c16:128:16, vs, :]),
                                          [b_Vn[vs]], [b_V16], ch_v, group=grp, q='pool')
                            vgrp[0] = grp
            if pend_tr[0] is not None:
                pend_tr[0](); pend_tr[0] = None
            if pend_evac[0] is not None:
                pend_evac[0](); pend_evac[0] = None
            stage('qkv')
            sl, bsl = wget(base + 3)
            sv = slotv(sl, 8)
            for g in range(4):
                pa, bpa = nextpa()
                for c in range(8):
                    Top(lambda e, pa=pa, g=g, c=c, sv=sv: e.matmul(pa[:, 0:NT], lhsT=sv[:, c, g * 128:(g + 1) * 128], rhs=actT[:, c, 0:NT],
                                                                  start=(c == 0), stop=(c == 7)), [b_actT, bsl], [bpa])
                ue = uT[:, g, 0:nseg * E].rearrange("p (s e) -> p s e", e=E)
                Aop(lambda e, pa=pa, ue=ue: e.activation(out=ue[:, :, 16:E], in_=pa[:, 0:NT].rearrange("p (s l) -> p s l", l=L), func=AF.Copy), [bpa], [b_uT[g]])
            if last:
                pa, bpa = nextpa()
                for c in range(8):
                    Top(lambda e, pa=pa, c=c, sv=sv: e.matmul(pa[0:PP, :], lhsT=actT[:, c, NT - PP:NT], rhs=sv[:, c, :], start=(c == 0), stop=(c == 7)),
                        [b_actT, bsl], [bpa])
                Aop(lambda e, pa=pa: e.activation(out=tmpn[0:PP, :], in_=pa[0:PP, :], func=AF.Copy), [bpa], [b_tmpn])
                if smp:
                    for t4 in range(4):
                        Dop(lambda e, t4=t4: e.dma_start(out=nps[:, 11 + t4, :], in_=tmpn[t4:64:4, :]), [b_tmpn], [], cho())
                else:
                    Dop(lambda e: e.dma_start(out=npp[s, :, :], in_=tmpn[113:128, :]), [b_tmpn], [], cho())
            stage('u')
            for g in range(4):
                w = 2 << g
                ue = uT[:, g, 0:nseg * E].rearrange("p (s e) -> p s e", e=E)
                s1 = sa[:, 0:nseg * E].rearrange("p (s e) -> p s e", e=E)
                s2 = sbb[:, 0:nseg * E].rearrange("p (s e) -> p s e", e=E)
                Gop(lambda e, ue=ue, s1=s1: e.tensor_tensor(out=s1[:, :, 1:E], in0=ue[:, :, 1:E], in1=ue[:, :, 0:E - 1], op=ALU.add), [b_uT[g]], [b_s])
                fin = s1
                if w >= 4:
                    Gop(lambda e, s1=s1, s2=s2: e.tensor_tensor(out=s2[:, :, 3:E], in0=s1[:, :, 3:E], in1=s1[:, :, 1:E - 2], op=ALU.add), [b_s], [b_s]); fin = s2
                if w >= 8:
                    Gop(lambda e, s1=s1, s2=s2: e.tensor_tensor(out=s1[:, :, 7:E], in0=s2[:, :, 7:E], in1=s2[:, :, 3:E - 4], op=ALU.add), [b_s], [b_s]); fin = s1
                if w >= 16:
                    Gop(lambda e, s1=s1, s2=s2: e.tensor_tensor(out=s2[:, :, 15:E], in0=s1[:, :, 15:E], in1=s1[:, :, 7:E - 8], op=ALU.add), [b_s], [b_s]); fin = s2
                dv = dT[:, g, 0:NT].rearrange("p (s l) -> p s l", l=L)
                Vop(lambda e, fin=fin, ue=ue, dv=dv, w=w: e.scalar_tensor_tensor(out=dv, in0=fin[:, :, 16:E], scalar=1.0 / w, in1=ue[:, :, 16:E],
                                                                                 op0=ALU.mult, op1=ALU.subtract), [b_s, b_uT[g]], [b_dT])
                if (not smp) and m == 0:
                    Gop(lambda e, fin=fin, g=g: e.tensor_tensor(out=tmp16[:, 0:16], in0=fin[:, 0, 16:32], in1=invcnt[:, g, :], op=ALU.mult), [b_s, b_const], [b_s])
                    Gop(lambda e, ue=ue, g=g: e.tensor_tensor(out=dT[:, g, 0:16], in0=tmp16[:, 0:16], in1=ue[:, 0, 16:32], op=ALU.subtract), [b_s, b_uT[g]], [b_dT])
                pa, bpa = nextpa()
                Top(lambda e, pa=pa, g=g: e.matmul(pa[:, 0:NT], lhsT=wpool[:, g, :], rhs=dT[:, g, 0:NT], start=True, stop=True), [b_dT, b_const], [bpa])
                Vop(lambda e, pa=pa, g=g: e.tensor_scalar(out=poolT[:, g, 0:NT], in0=pa[:, 0:NT], scalar1=pscale[:, g:g + 1], scalar2=None, op0=ALU.mult),
                    [bpa, b_const], [b_poolT])
                if not smp:
                    Gop(lambda e, g=g: e.tensor_copy(out=uT[:, g, 0:16], in_=uT[:, g, 512:528]), [b_uT[g]], [b_uT[g]])

            if tidx == 0:
                late_pro()
            stage('pool')
            if not smp:
                glist = []
                for h in range(8):
                    kinds = [k_ for k_ in ('1c', '1p', '4c', '4p', '16') if not (k_ == '4p' and m == 0)]
                    for ki_, kind in enumerate(kinds):
                        glist.append((h, kind, ki_ == 0, ki_ == len(kinds) - 1))

                def g_params(gi_):
                    h, kind, first, lastk = glist[gi_]
                    ps, bps = pS[gi_ % 2], b_pS[gi_ % 2]
                    pt_, bpt_ = PT[gi_ % 3], b_PT[gi_ % 3]
                    R = 128; c0 = 0
                    if kind == '16':
                        R = 32 * (m + 1); mask = m16[:, m, :]
                    else:
                        mask = mcur if kind in ('1c', '4c') else mprev
                        if kind == '1p' and m == 0:
                            c0 = 128
                    return h, kind, first, lastk, ps, bps, pt_, bpt_, R, c0, mask

                def emit_scores(gi_):
                    h, kind, first, lastk, ps, bps, pt_, bpt_, R, c0, mask = g_params(gi_)
                    Top(lambda e: e.matmul(ps[0:R, c0:512], lhsT=identb[0:R, 0:R], rhs=mask[0:R, c0:512], start=True, stop=False), [b_const], [bps])
                    if kind in ('1c', '1p'):
                        for n in range(c0 // 128, 4):
                            kb = T0 + n * 128 - (128 if kind == '1p' else 0)
                            Top(lambda e, n=n, kb=kb: e.matmul(ps[:, n * 128:(n + 1) * 128], lhsT=kT[0:68, h, kb:kb + 128],
                                                              rhs=qT[0:68, h, n * 128:(n + 1) * 128], start=False, stop=True, skip_group_check=True), [b_kT, b_qT], [bps])
                    elif kind in ('4c', '4p'):
                        for c4 in range(4):
                            kb = T0 + c4 - (512 if kind == '4p' else 0)
                            Top(lambda e, c4=c4, kb=kb: e.matmul(ps[:, c4 * 128:(c4 + 1) * 128], lhsT=kT[0:68, h, kb:kb + 509:4],
                                                                rhs=qT[0:68, h, c4:512:4], start=False, stop=True, skip_group_check=True), [b_kT, b_qT], [bps])
                    else:
                        for c16 in range(16):
                            Top(lambda e, c16=c16: e.matmul(ps[0:R, c16 * 32:(c16 + 1) * 32], lhsT=kT[0:68, h, c16:T0 + 512:16],
                                                           rhs=qT[0:68, h, c16:512:16], start=False, stop=True, skip_group_check=True), [b_kT, b_qT], [bps])
                    Aop(lambda e: e.activation(out=pt_[0:R, c0:512], in_=ps[0:R, c0:512], func=AF.Exp, bias=negM[0:R, 0:1], scale=1.0), [bps, b_const], [bpt_])

                def emit_pv(gi_):
                    h, kind, first, lastk, ps, bps, pt_, bpt_, R, c0, mask = g_params(gi_)
                    (pO_, b_pO_), (pL_, b_pL_) = ((pO, b_pO), (pL, b_pL)) if h % 2 == 0 else ((pA[0], b_pA[0]), (pA[1], b_pA[1]))
                    rlb_, b_rlb_ = rlbs[h % 2], b_rlbs[h % 2]
                    if first:
                        Vop(lambda e: e.memset(pO_[0:64, :], 0.0), [], [b_pO_])
                        Vop(lambda e: e.memset(pL_[0:64, :], 0.0), [], [b_pL_])
                    hs = slice(h * 64, (h + 1) * 64)
                    kw = dict(start=False, stop=False, skip_group_check=True)
                    if kind in ('1c', '1p'):
                        for n in range(c0 // 128, 4):
                            vs = (4 * m + n - (1 if kind == '1p' else 0)) % 5
                            Top(lambda e, n=n, vs=vs: e.matmul(pO_[0:64, n * 128:(n + 1) * 128], lhsT=Vn[:, vs, hs], rhs=pt_[:, n * 128:(n + 1) * 128], **kw),
                                [b_Vn[vs], bpt_], [b_pO_])
                        Top(lambda e: e.matmul(pL_[0:64, c0:512], lhsT=onesb[:, 0:64], rhs=pt_[:, c0:512], **kw), [b_const, bpt_], [b_pL_])
                    elif kind in ('4c', '4p'):
                        for c4 in range(4):
                            vsl = ((m if kind == '4c' else m - 1) % 2) * 4 + c4
                            Top(lambda e, c4=c4, vsl=vsl: e.matmul(pO_[0:64, c4:512:4], lhsT=V4[:, vsl, hs], rhs=pt_[:, c4 * 128:(c4 + 1) * 128], **kw), [b_V4, bpt_], [b_pO_])
                        Top(lambda e: e.matmul(pL_[0:64, :].rearrange("p (i c) -> p c i", c=4), lhsT=onesb[:, 0:64], rhs=pt_[:, 0:512].rearrange("p (c i) -> p c i", c=4), **kw),
                            [b_const, bpt_], [b_pL_])
                    else:
                        for c16 in range(16):
                            Top(lambda e, c16=c16: e.matmul(pO_[0:64, c16:512:16], lhsT=V16[0:R, c16, hs], rhs=pt_[0:R, c16 * 32:(c16 + 1) * 32], **kw), [b_V16, bpt_], [b_pO_])
                        Top(lambda e: e.matmul(pL_[0:64, :].rearrange("p (i c) -> p c i", c=16), lhsT=onesb[0:R, 0:64], rhs=pt_[0:R, 0:512].rearrange("p (c i) -> p c i", c=16), **kw),
                            [b_const, bpt_], [b_pL_])
                    if lastk:
                        Aop(lambda e: e.activation(out=rlb_[0:64, :], in_=pL_[0:64, :], func=AF.Ln), [b_pL_], [b_rlb_])
                        Aop(lambda e: e.activation(out=rlb_[0:64, :], in_=rlb_[0:64, :], func=AF.Exp, scale=-1.0), [b_rlb_], [b_rlb_])
                        Vop(lambda e: e.tensor_tensor(out=attnT[0:64, h, :], in0=pO_[0:64, :], in1=rlb_[0:64, :], op=ALU.mult), [b_pO_, b_rlb_], [b_attnT[h]])

                emit_scores(0)
                for gi_ in range(len(glist)):
                    if gi_ + 1 < len(glist):
                        emit_scores(gi_ + 1)
                    emit_pv(gi_)
            else:
                sample_attention()

            stage('attn')
            for f in range(2):
                sla, bsla = wget(base + 4 + 2 * f)
                slp, bslp = wget(base + 5 + 2 * f, hold=1)
                sva = slotv(sla, 8); svp = slotv(slp, 4)
                for j in range(nsub):
                    pa, bpa = nextpa()
                    for hh in range(8):
                        Top(lambda e, pa=pa, hh=hh, j=j, sva=sva: e.matmul(pa[0:PP, :], lhsT=attnT[0:64, hh, j * PP:(j + 1) * PP], rhs=sva[0:64, hh, :],
                                                                          start=(hh == 0), stop=False), [b_attnT[hh], bsla], [bpa])
                    for g in range(4):
                        Top(lambda e, pa=pa, g=g, j=j, svp=svp: e.matmul(pa[0:PP, :], lhsT=poolT[:, g, j * PP:(j + 1) * PP], rhs=svp[:, g, :],
                                                                        start=False, stop=(g == 3)), [b_poolT, bslp], [bpa])
                    Vop(lambda e, pa=pa, j=j, f=f: e.tensor_tensor(out=xh[0:PP, j, f * 512:(f + 1) * 512], in0=pa[0:PP, :], in1=xh[0:PP, j, f * 512:(f + 1) * 512], op=ALU.add),
                        [bpa, b_xhs[j]], [b_xhs[j]])
            stage('wout')
            P.inherit(F_bufs, A_bufs)
            norm_T(g2)
            hg = hist_g; hv = hist_v
            for i in range(11):
                sl, bsl = wget(base + 8 + i)
                sv = slotv(sl, 8)
                for (col0, kind, a) in ((0, 'g', 2 * i), (256, 'v', 2 * i), (128, 'g', 2 * i + 1), (384, 'v', 2 * i + 1)):
                    pa, bpa = nextpa()
                    for c in range(8):
                        Top(lambda e, pa=pa, c=c, col0=col0, sv=sv: e.matmul(pa[:, 0:NT], lhsT=sv[:, c, col0:col0 + 128], rhs=actT[:, c, 0:NT],
                                                                            start=(c == 0), stop=(c == 7)), [b_actT, bsl], [bpa])
                    ki = 0 if kind == 'g' else 1
                    cn = ccnt[kind]; ccnt[kind] += 1
                    nacc = len(acc[ki])
                    acf = acc[ki][cn % nacc]; bac = b_acc[ki][cn % nacc]
                    ac = acf[:, 0:NT].rearrange("p (s l) -> p s l", l=L)
                    cw = (cpg if kind == 'g' else cpv)
                    par = 0 if smp else (m % 2)
                    hbig = (hg if kind == 'g' else hv); hsml = (hist2_g if kind == 'g' else hist2_v)
                    hold = (hbig[:, a, 0:nseg * 2] if par == 0 else hsml[:, a, 0:2]).rearrange("p (s e) -> p s e", e=2)
                    hnew = (hsml[:, a, 0:2] if par == 0 else hbig[:, a, 0:2]).rearrange("p (s e) -> p s e", e=2)
                    bho = b_histd[(kind, a, par)]; bhn = b_histd[(kind, a, 1 - par)]
                    pav = pa[:, 0:NT].rearrange("p (s l) -> p s l", l=L)
                    if not smp:
                        Aop(lambda e, pav=pav, hnew=hnew: e.activation(out=hnew, in_=pav[:, :, L - 2:L], func=AF.Copy), [bpa], [bhn])
                    Aop(lambda e, pav=pav, ac=ac, cw=cw, a=a: e.activation(out=ac, in_=pav, func=AF.Identity, scale=cw[:, a, 2:3], bias=cw[:, a, 3:4]),
                        [bpa, b_const], [bac])
                    Vop(lambda e, pav=pav, ac=ac, cw=cw, a=a: e.scalar_tensor_tensor(out=ac[:, :, 1:L], in0=pav[:, :, 0:L - 1], scalar=cw[:, a, 1:2], in1=ac[:, :, 1:L],
                                                                                   op0=ALU.mult, op1=ALU.add), [bpa, b_const, bac], [bac])
                    Vop(lambda e, pav=pav, ac=ac, cw=cw, a=a: e.scalar_tensor_tensor(out=ac[:, :, 2:L], in0=pav[:, :, 0:L - 2], scalar=cw[:, a, 0:1], in1=ac[:, :, 2:L],
                                                                                   op0=ALU.mult, op1=ALU.add), [bpa, b_const, bac], [bac])
                    Vop(lambda e, hold=hold, ac=ac, cw=cw, a=a: e.scalar_tensor_tensor(out=ac[:, :, 0:2], in0=hold[:, :, 0:2], scalar=cw[:, a, 0:1], in1=ac[:, :, 0:2],
                                                                                     op0=ALU.mult, op1=ALU.add), [bho, b_const, bac], [bac])
                    Vop(lambda e, hold=hold, ac=ac, cw=cw, a=a: e.scalar_tensor_tensor(out=ac[:, :, 0:1], in0=hold[:, :, 1:2], scalar=cw[:, a, 1:2], in1=ac[:, :, 0:1],
                                                                                     op0=ALU.mult, op1=ALU.add), [bho, b_const, bac], [bac])
                    if kind == 'g':
                        sgi = cn % 2
                        pend_silu[0] = (acf, sgi, bac)
                    else:
                        sgi = cn % 2
                        acf_g, sgi_g, bac_g = pend_silu[0]
                        Aop(lambda e, acf_g=acf_g, sgi_g=sgi_g: e.activation(out=sg[sgi_g][:, 0:NT], in_=acf_g[:, 0:NT], func=AF.Silu), [bac_g], [b_sg[sgi_g]])
                        Gop(lambda e, acf=acf, a=a, sgi=sgi: e.tensor_tensor(out=gT[:, a, 0:NT], in0=sg[sgi][:, 0:NT], in1=acf[:, 0:NT], op=ALU.mult), [b_sg[sgi], bac], [b_gT])
                if last:
                    pa, bpa = nextpa()
                    for c in range(8):
                        Top(lambda e, pa=pa, c=c, sv=sv: e.matmul(pa[0:PP, :], lhsT=actT[:, c, NT - PP:NT], rhs=sv[:, c, :], start=(c == 0), stop=(c == 7)),
                            [b_actT, bsl], [bpa])
                    Aop(lambda e, pa=pa: e.activation(out=upst[0:PP, :], in_=pa[0:PP, :], func=AF.Copy), [bpa], [b_upst])
                    for (co, fo) in ((0, 256 * i), (256, 2816 + 256 * i)):
                        if smp:
                            for jj in range(2):
                                Dop(lambda e, co=co, fo=fo, jj=jj: e.dma_start(out=nfs[:, jj, fo:fo + 256], in_=upst[2 + jj:64:4, co:co + 256]), [b_upst], [], cho())
                        else:
                            Dop(lambda e, co=co, fo=fo: e.dma_start(out=nfp[s, :, fo:fo + 256], in_=upst[126:128, co:co + 256]), [b_upst], [], cho())
            stage('ffn_up')
            accs = [(pA[0], b_pA[0]), (pA[1], b_pA[1]), (pS[0], b_pS[0]), (pS[1], b_pS[1])]
            for f in range(2):
                for pc in range(3):
                    sl, bsl = wget(base + 19 + 3 * f + pc)
                    sv = slotv(sl, 8)
                    for kk in range(8 if pc < 2 else 6):
                        kc = pc * 8 + kk
                        for j in range(nsub):
                            Top(lambda e, j=j, kc=kc, kk=kk, sv=sv: e.matmul(accs[j][0][0:PP, :], lhsT=gT[:, kc, j * PP:(j + 1) * PP], rhs=sv[:, kk, :],
                                                                            start=(kc == 0), stop=(kc == 21)), [b_gT, bsl], [accs[j][1]])
                for j in range(nsub):
                    Vop(lambda e, j=j, f=f: e.tensor_tensor(out=xh[0:PP, j, f * 512:(f + 1) * 512], in0=accs[j][0][0:PP, :], in1=xh[0:PP, j, f * 512:(f + 1) * 512], op=ALU.add),
                        [accs[j][1], b_xhs[j]], [b_xhs[j]])
            if smp:
                ydst = ys.rearrange("(j p) d -> p j d", p=64)
            else:
                ydst = yp[s, T0:T0 + 512, :].rearrange("(j p) d -> p j d", p=128)
            for j in range(nsub):
                Dop(lambda e, j=j: e.dma_start(out=ydst[:, j, :], in_=xh[0:PP, j, :]), [b_xhs[j]], [], ch_ys[j])

        qbd = sb("qbd", [128, 4, 16, 8], BF16); b_qbd = Buf()
        kTp = sb("kTp", [128, 4, 64], BF16); b_kTp = Buf()
        wt = sb("wt", [128, 224], F32); wtn = sb("wtn", [64, 16, 32], F32)
        vgrp = [None]

        def sample_prep():
            cload(wt[:], c_wt); cload(wtn[:], c_wtn)
            Vop(lambda e: e.memset(qbd[:].rearrange("p a b t -> p (a b t)"), 0.0), [], [b_qbd])
            sph = V16[0:120, 0:2, :]
            Dop(lambda e: e.dma_start(out=sph, in_=spool.rearrange("b i c -> (b i) c").rearrange("(two r) c -> r two c", r=120)), [], [b_V16], ch_s[0], q='pool')
            Dop(lambda e: e.dma_start(out=nps[:, 0:11, :], in_=spool[:, 4:15, :]), [], [], cho())
            for two in range(2):
                pt, bpt = nextpt()
                for g in range(4):
                    Top(lambda e, pt=pt, g=g, two=two: e.transpose(out=pt[:, g * 120:(g + 1) * 120], in_=V16[0:120, two, g * 128:(g + 1) * 128],
                                                                  identity=identb[0:120, 0:120]), [b_V16, b_const], [bpt])
                for g in range(4):
                    ue = uT[:, g, 0:320].rearrange("p (s e) -> p s e", e=20)
                    Aop(lambda e, pt=pt, g=g, two=two, ue=ue: e.activation(out=ue[:, 8 * two:8 * two + 8, 1:16], in_=pt[:, g * 120:(g + 1) * 120].rearrange("p (s i) -> p s i", i=15),
                                                                          func=AF.Copy), [bpt], [b_uT[g]])
            sfh = V16[0:32, 2:13, :].rearrange("p a t -> p (a t)")
            Dop(lambda e: e.dma_start(out=sfh[:, 0:5632], in_=sffn.rearrange("b j c -> (b j) c")), [], [b_V16], ch_s[1], q='pool')
            for kind in range(2):
                hs_ = hist_g if kind == 0 else hist_v
                for a0 in (0, 8, 16):
                    na = 8 if a0 < 16 else 6
                    pt, bpt = nextpt()
                    for a in range(a0, a0 + na):
                        f0 = a * 128 + (2816 if kind else 0)
                        Top(lambda e, pt=pt, a=a, a0=a0, f0=f0: e.transpose(out=pt[:, (a - a0) * 32:(a - a0 + 1) * 32], in_=sfh[0:32, f0:f0 + 128],
                                                                           identity=identb[0:32, 0:32]), [b_V16, b_const], [bpt])
                    Aop(lambda e, pt=pt, a0=a0, na=na, hs_=hs_: e.activation(out=hs_[:, a0:a0 + na, :], in_=pt[:, 0:na * 32].rearrange("p (a x) -> p a x", x=32),
                                                                             func=AF.Copy), [bpt], b_hist_all)

        pf = sb("pf", [128, 256], F32); b_pf = Buf()
        pts = sb("pts", [128, 256], BF16); b_pts = Buf()
        KcT = kTflat[:, 7168:10752].rearrange("p (a r) -> p a r", a=4); b_KcT = Buf()

        def sample_attention():
            P.inherit([b_Kc[0], b_Vc[0], b_KcT], [b_kT])
            Vop(lambda e: e.memset(pO[0:64, :], 0.0), [], [b_pO])
            Vop(lambda e: e.memset(pL[0:64, :], 0.0), [], [b_pL])
            for b in range(16):
                kc, bkc = Kc[b % 2], b_Kc[b % 2]
                vc, bvc = Vc[b % 2], b_Vc[b % 2]
                for (src, dstt, bd, chh) in ((ck, kc, bkc, ch_s[0]), (cv, vc, bvc, ch_s[1])):
                    grp = None
                    for r in range(4):
                        grp = Dop(lambda e, src=src, dstt=dstt, r=r, b=b: e.dma_start(out=dstt[r:128:4, 0:3, :],
                                                                                      in_=src[b, r:1536:16, :].rearrange("(tau a) c -> a tau c", a=32)),
                                  [], [bd], chh, group=grp, q='pool')
                    grp = Dop(lambda e, src=src, dstt=dstt, b=b: e.dma_start(out=dstt[:, 3:7, :], in_=src[b, 1536:2048, :].rearrange("(tau p) c -> p tau c", p=128)),
                              [], [bd], chh, group=grp, q='pool')
                for tau in range(7):
                    pt, bpt = nextpt()
                    for hp in range(4):
                        Top(lambda e, pt=pt, hp=hp, tau=tau, kc=kc: e.transpose(out=pt[:, hp * 128:(hp + 1) * 128], in_=kc[:, tau, hp * 128:(hp + 1) * 128], identity=identb[:]),
                            [bkc, b_const], [bpt])
                    Aop(lambda e, pt=pt, tau=tau: e.activation(out=KcT[:, :, tau * 128:(tau + 1) * 128], in_=pt[:, 0:512].rearrange("p (a r) -> p a r", a=4), func=AF.Copy),
                        [bpt], [b_KcT])
                ps, bps = pS[b % 2], b_pS[b % 2]
                for tau in range(7):
                    for hp in range(4):
                        Top(lambda e, ps=ps, tau=tau, hp=hp, b=b: e.matmul(ps[:, tau * 32 + hp * 8:tau * 32 + hp * 8 + 8], lhsT=KcT[:, hp, tau * 128:(tau + 1) * 128],
                                                                          rhs=qbd[:, hp, b, :], start=True, stop=True), [b_KcT, b_qbd], [bps])
                for hp in range(4):
                    Top(lambda e, ps=ps, hp=hp, b=b: e.matmul(ps[0:64, 224 + hp * 8:224 + hp * 8 + 8], lhsT=kTp[:, hp, 0:64], rhs=qbd[:, hp, b, :], start=True, stop=True),
                        [b_kTp, b_qbd], [bps])
                Aop(lambda e, ps=ps: e.activation(out=pf[:, 0:224], in_=ps[:, 0:224], func=AF.Exp, bias=negM[:, 0:1], scale=1.0), [bps, b_const], [b_pf])
                Aop(lambda e, ps=ps: e.activation(out=pf[0:64, 224:256], in_=ps[0:64, 224:256], func=AF.Exp, bias=negM[0:64, 0:1], scale=1.0), [bps, b_const], [b_pf])
                Vop(lambda e: e.tensor_tensor(out=pts[:, 0:224], in0=pf[:, 0:224], in1=wt[:, :], op=ALU.mult), [b_pf, b_const], [b_pts])
                Vop(lambda e, b=b: e.tensor_tensor(out=pts[0:64, 224:256], in0=pf[0:64, 224:256], in1=wtn[0:64, b, :], op=ALU.mult), [b_pf, b_const], [b_pts])
                for h in range(8):
                    o = pO[0:64, h * 64 + b * 4:h * 64 + b * 4 + 4]
                    for tau in range(7):
                        Top(lambda e, o=o, tau=tau, h=h, vc=vc: e.matmul(o, lhsT=vc[:, tau, h * 64:(h + 1) * 64], rhs=pts[:, tau * 32 + h * 4:tau * 32 + h * 4 + 4],
                                                                        start=False, stop=False, skip_group_check=True), [bvc, b_pts], [b_pO])
                    Top(lambda e, o=o, h=h: e.matmul(o, lhsT=Vn[0:64, 0, h * 64:(h + 1) * 64], rhs=pts[0:64, 224 + h * 4:224 + h * 4 + 4],
                                                     start=False, stop=False, skip_group_check=True), [b_Vn[0], b_pts], [b_pO])
                for tau in range(7):
                    Top(lambda e, tau=tau, b=b: e.matmul(pL[0:64, b * 32:(b + 1) * 32], lhsT=onesb[:, 0:64], rhs=pts[:, tau * 32:(tau + 1) * 32],
                                                         start=False, stop=False, skip_group_check=True), [b_const, b_pts], [b_pL])
                Top(lambda e, b=b: e.matmul(pL[0:64, b * 32:(b + 1) * 32], lhsT=onesb[0:64, 0:64], rhs=pts[0:64, 224:256],
                                            start=False, stop=False, skip_group_check=True), [b_const, b_pts], [b_pL])
            Vop(lambda e: e.reciprocal(out=rlb[0:64, :], in_=pL[0:64, :]), [b_pL], [b_rlb])
            for h in range(8):
                Vop(lambda e, h=h: e.tensor_tensor(out=attnT[0:64, h, 0:64].rearrange("p (b t) -> p b t", t=4),
                                                   in0=pO[0:64, h * 64:(h + 1) * 64].rearrange("p (b t) -> p b t", t=4),
                                                   in1=rlb[0:64, :].rearrange("p (b h t) -> p h b t", h=8, t=4)[:, h, :, :], op=ALU.mult),
                    [b_pO, b_rlb], [b_attnT[h]])

        tidx = 0
        try:
          stage('const')
          for s in range(2):
            for g in range(4):
                Gop(lambda e, g=g: e.memset(uT[:, g, 0:16], 0.0), [], [b_uT[g]])
            Gop(lambda e: e.memset(hist_g[:].rearrange("p a x -> p (a x)"), 0.0), [], b_hist_all)
            Gop(lambda e: e.memset(hist_v[:].rearrange("p a x -> p (a x)"), 0.0), [], b_hist_all)
            for m in range(4):
                do_tile(tidx, False, s, m)
                tidx += 1
                stage('tile%d' % (tidx - 1))
          sample_prep()
          stage('sprep')
          do_tile(tidx, True, 0, 0)
        except StopBuild:
            pass

        if os.environ.get('KDBG'):
            print('SBUF remaining', nc.sbuf_bytes_remaining)
        P.finalize()
        with nc.Block() as block:
            @block.tensor
            def _(e): P.run('pe', e)

            @block.scalar
            def _(e): P.run('act', e)

            @block.vector
            def _(e): P.run('dve', e)

            @block.gpsimd
            def _(e): P.run('pool', e)

            @block.sync
            def _(e):
                P.run('sp', e)
                for c in P.chans:
                    if c.count:
                        e.wait_ge(c.sem, c.count)
    return nc


def _consts():
    c = {}
    c["c_id"] = np.eye(128, dtype=np.float32).astype(BF)
    k = np.arange(128)[:, None]; q = np.arange(128)[None, :]
    c["c_mcur"] = np.tile(np.where(k <= q, 0.0, -30000.0).astype(np.float32), (1, 4)).astype(BF)
    c["c_mprev"] = np.tile(np.where(k >= q, 0.0, -30000.0).astype(np.float32), (1, 4)).astype(BF)
    m16 = np.zeros((128, 4, 16, 32), np.float32)
    for m in range(4):
        m16[:, m, :, :] = np.where(np.arange(128)[:, None] <= 32 * m + np.arange(32)[None, :], 0.0, -30000.0).astype(np.float32)[:, None, :]
    c["c_m16"] = m16.reshape(128, 4, 512).astype(BF)
    slopes = 2.0 ** (-np.arange(1, 9, dtype=np.float64))
    t = np.arange(2048)
    kaug = np.zeros((4, 8, 2048), np.float64)
    kaug[0] = (-64 * slopes)[:, None]; kaug[1] = (-slopes)[:, None]
    kaug[2] = 64 * slopes[:, None] * (t // 64)[None, :]; kaug[3] = slopes[:, None] * (t % 64)[None, :]
    c["c_kaug"] = kaug.astype(np.float32).astype(BF)
    qaug = np.zeros((4, 8, 2048), np.float32)
    qaug[0] = (t // 64)[None, :]; qaug[1] = (t % 64)[None, :]; qaug[2] = 1; qaug[3] = 1
    c["c_qaug"] = qaug.astype(BF)
    inv = np.zeros((128, 4, 16), np.float32)
    for g, w in enumerate((2, 4, 8, 16)):
        inv[:, g, :] = 1.0 / np.minimum(w, np.arange(16) + 1)
    c["c_invcnt"] = inv

    def cnt(d):
        d = np.asarray(d)
        return ((d >= 0) & (d <= 128)).astype(np.float64) + ((d >= 0) & (d % 4 == 0) & (d <= 512)) + ((d >= 0) & (d % 16 == 0) & (d <= 2048))
    p = np.arange(128)
    wt = np.zeros((128, 7, 8, 4), np.float64)
    for tau in range(7):
        row = 16 * (32 * tau + p // 4) + p % 4 if tau < 3 else 1536 + 128 * (tau - 3) + p
        for tt in range(4):
            d = 2048 + tt - row
            for h in range(8):
                wt[:, tau, h, tt] = cnt(d) * np.exp(-slopes[h] * d)
    c["c_wt"] = wt.reshape(128, 224).astype(np.float32)
    wtn = np.zeros((64, 16, 8, 4), np.float64)
    for b in range(16):
        for t1 in range(4):
            for tt in range(4):
                d = tt - t1
                if d >= 0:
                    wtn[4 * b + t1, b, :, tt] = cnt(d) * np.exp(-slopes * d)
    c["c_wtn"] = wtn.reshape(64, 16, 32).astype(np.float32)
    return c


_NC = [None]


def kernel(x_prompt, x_sample, cache_k, cache_v, state_pool, state_ffn_conv, g_attn_norm, w_in, g_q, g_k,
           w_pool, pool_scale, w_out, g_ffn_norm, w_up, conv_w, conv_b, w_down):
    f = lambda a: np.ascontiguousarray(np.asarray(a, dtype=np.float32))
    nc = build()
    consts = _consts()
    shared = dict(
        g_attn=f(g_attn_norm).reshape(1, 1024), w_in=f(w_in)[0], gq=np.ascontiguousarray(np.broadcast_to(f(g_q).reshape(1, 512), (128, 512))),
        gk=np.ascontiguousarray(np.broadcast_to(f(g_k).reshape(1, 512), (128, 512))), w_pool=f(w_pool)[0], pool_scale=f(pool_scale).reshape(1, 512),
        w_out=f(w_out)[0], g_ffn=f(g_ffn_norm).reshape(1, 1024), w_up=f(w_up)[0], conv_w=f(conv_w)[0], conv_b=f(conv_b).reshape(1, 5632),
        w_down=f(w_down)[0], **consts)
    xp_ = f(x_prompt); xs_ = f(x_sample); ck_ = f(cache_k)[0]; cv_ = f(cache_v)[0]; sp_ = f(state_pool)[0]; sf_ = f(state_ffn_conv)[0]
    in_maps = []
    for i in range(8):
        d = dict(shared)
        d["xp"] = xp_[2 * i:2 * i + 2]
        d["xs"] = xs_[16 * i:16 * i + 16].reshape(64, 1024)
        d["ck"] = ck_[16 * i:16 * i + 16].reshape(16, 2048, 512)
        d["cv"] = cv_[16 * i:16 * i + 16].reshape(16, 2048, 512)
        d["spool"] = sp_[16 * i:16 * i + 16]
        d["sffn"] = sf_[16 * i:16 * i + 16]
        in_maps.append(d)
    res = run_bass_kernel_spmd(nc, in_maps, core_ids=list(range(8)))
    R = res.results
    cat = lambda k: np.concatenate([np.asarray(r[k], dtype=np.float32) for r in R], axis=0)
    y_p = cat("yp"); y_s = cat("ys").reshape(128, 4, 1024)
    nk_p = cat("nkp").reshape(1, 16, 2048, 8, 64); nv_p = cat("nvp").reshape(1, 16, 2048, 8, 64)
    np_p = cat("npp").reshape(1, 16, 15, 512); nf_p = cat("nfp").reshape(1, 16, 2, 5632)
    nk_s = cat("nks").reshape(1, 128, 4, 8, 64); nv_s = cat("nvs").reshape(1, 128, 4, 8, 64)
    np_s = cat("nps").reshape(1, 128, 15, 512); nf_s = cat("nfs").reshape(1, 128, 2, 5632)
    return (y_p, y_s, nk_p, nv_p, np_p, nf_p, nk_s, nv_s, np_s, nf_s)
```

```python
import os
import numpy as np
import ml_dtypes
from contextlib import ExitStack
import concourse.bass as bass
import concourse.mybir as mybir
from concourse.bass_utils import run_bass_kernel_spmd

F32 = mybir.dt.float32
BF16 = mybir.dt.bfloat16
AF = mybir.ActivationFunctionType
ALU = mybir.AluOpType
AX = mybir.AxisListType
EPS = 1e-6
NSLOT = 3
BF = ml_dtypes.bfloat16


class Buf:
    __slots__ = ('w', 'r')

    def __init__(s):
        s.w = None
        s.r = {}


class Group:
    def __init__(s, chan):
        s.chan = chan
        s.total = None


class Chan:
    def __init__(s, sem):
        s.sem = sem
        s.count = 0
        s.last = None


class Prog:
    ENG = ['pe', 'act', 'dve', 'pool', 'sp']

    def __init__(s, nc, es):
        s.nc = nc
        s.q = {e: [] for e in s.ENG}
        s.sem = {e: es.enter_context(nc.semaphore('sem_' + e)) for e in s.ENG if e != 'sp'}
        s.es = es
        s.chans = []

    def chan(s):
        c = Chan(s.es.enter_context(s.nc.semaphore('ch%d' % len(s.chans))))
        s.chans.append(c)
        return c

    def op(s, eng, fn, reads=(), writes=(), chan=None, group=None):
        q = s.q[eng]
        idx = len(q)
        deps = set()
        if chan is not None:
            if group is None:
                group = Group(chan)
            if chan.last is not group:
                if chan.last is not None:
                    deps.add(('dma', chan.last))
                chan.last = group
            chan.count += 16
            group.total = chan.count
            me = ('dma', group)
            mekey = ('dma', id(chan))
        else:
            me = (eng, idx)
            mekey = eng
        isdma = chan is not None
        for b in reads:
            if b.w is not None and b.w != me:
                if not (b.w[0] == 'pe' and eng == 'pe' and not isdma):
                    deps.add(b.w)
        for b in writes:
            if b.w is not None and b.w != me:
                if b.w[0] == 'dma' or isdma or b.w[0] != eng:
                    deps.add(b.w)
            for k, v in b.r.items():
                if v == me:
                    continue
                if v[0] == 'dma' or isdma or v[0] != eng:
                    deps.add(v)
        for b in reads:
            b.r[mekey] = me
        for b in writes:
            b.w = me
            b.r = {}
        q.append((fn, deps, chan))
        return group

    def inherit(s, dsts, srcs):
        for d in dsts:
            for b in srcs:
                if b.w is not None:
                    d.r[('w', id(b))] = b.w
                for k, v in b.r.items():
                    d.r[(k, id(b))] = v

    def finalize(s):
        sig = {e: set() for e in s.ENG}
        for e in s.ENG:
            for (fn, deps, chan) in s.q[e]:
                for d in deps:
                    if d[0] != 'dma':
                        sig[d[0]].add(d[1])
        s.rank = {}
        for e in s.ENG:
            for i, idx in enumerate(sorted(sig[e])):
                s.rank[(e, idx)] = i + 1

    def run(s, e, engobj):
        waited = {}
        for idx, (fn, deps, chan) in enumerate(s.q[e]):
            need = {}
            for d in deps:
                if d[0] == 'dma':
                    sm, val = d[1].chan.sem, d[1].total
                else:
                    sm, val = s.sem[d[0]], s.rank[d]
                k = id(sm)
                if need.get(k, (None, 0))[1] < val:
                    need[k] = (sm, val)
            for k, (sm, val) in need.items():
                if waited.get(k, 0) < val:
                    engobj.wait_ge(sm, val)
                    waited[k] = val
            ins = fn(engobj)
            if (e, idx) in s.rank:
                ins.then_inc(s.sem[e], 1)
            if chan is not None:
                ins.then_inc(chan.sem, 16)


class StopBuild(Exception):
    pass


def build(stop=None):
    def stage(name):
        if stop == name:
            raise StopBuild()
    nc = bass.Bass("TRN2", target_bir_lowering=False)

    def din(name, shape, dtype=F32):
        return nc.dram_tensor(name, shape, dtype, kind="ExternalInput").ap()

    def dout(name, shape):
        return nc.dram_tensor(name, shape, F32, kind="ExternalOutput").ap()

    def dscr(name, shape):
        return nc.dram_tensor(name, shape, BF16, kind="Internal").ap()

    xp = din("xp", [2, 2048, 1024]); xs = din("xs", [64, 1024])
    ck = din("ck", [16, 2048, 512]); cv = din("cv", [16, 2048, 512])
    spool = din("spool", [16, 15, 512]); sffn = din("sffn", [16, 2, 5632])
    g_attn = din("g_attn", [1, 1024]); w_in = din("w_in", [1024, 2048])
    gq_in = din("gq", [128, 512]); gk_in = din("gk", [128, 512])
    w_pool = din("w_pool", [4, 128, 128]); pool_scale = din("pool_scale", [1, 512])
    w_out = din("w_out", [1024, 1024]); g_ffn = din("g_ffn", [1, 1024])
    w_up = din("w_up", [1024, 5632]); conv_w = din("conv_w", [3, 5632]); conv_b = din("conv_b", [1, 5632])
    w_down = din("w_down", [2816, 1024])
    c_id = din("c_id", [128, 128], BF16); c_mcur = din("c_mcur", [128, 512], BF16)
    c_mprev = din("c_mprev", [128, 512], BF16); c_m16 = din("c_m16", [128, 4, 512], BF16)
    c_kaug = din("c_kaug", [4, 8, 2048], BF16); c_qaug = din("c_qaug", [4, 8, 2048], BF16)
    c_invcnt = din("c_invcnt", [128, 4, 16]); c_wt = din("c_wt", [128, 224]); c_wtn = din("c_wtn", [64, 16, 32])

    yp = dout("yp", [2, 2048, 1024]); ys = dout("ys", [64, 1024])
    nkp = dout("nkp", [2, 2048, 512]); nvp = dout("nvp", [2, 2048, 512])
    npp = dout("npp", [2, 15, 512]); nfp = dout("nfp", [2, 2, 5632])
    nks = dout("nks", [64, 512]); nvs = dout("nvs", [64, 512])
    nps = dout("nps", [16, 15, 512]); nfs = dout("nfs", [16, 2, 5632])

    win_s = dscr("win_s", [4, 128, 8, 512]); woa_s = dscr("woa_s", [2, 64, 8, 512]); wop_s = dscr("wop_s", [2, 128, 4, 512])
    wup_s = dscr("wup_s", [11, 128, 8, 512]); wdn_s = dscr("wdn_s", [2, 3, 128, 8, 512])

    with ExitStack() as es:
        def sb(name, shape, dtype):
            return es.enter_context(nc.sbuf_tensor(name, shape, dtype))

        P = Prog(nc, es)
        xh = sb("xh", [128, 4, 1024], F32); b_xhs = [Buf() for _ in range(4)]; b_xh = b_xhs[0]
        xn = sb("xn", [128, 4, 1024], BF16); b_xn = Buf()
        actT = sb("actT", [128, 8, 512], BF16); b_actT = Buf()
        ss = sb("ss", [128, 16], F32); b_ss = Buf()
        ssqs = [sb("ssq", [128, 32], F32), sb("ssqb", [128, 32], F32)]; b_ssqs = [Buf(), Buf()]; sqc = [0]
        kT = sb("kT", [128, 8, 2048], BF16); b_kT = Buf()
        Vn = sb("Vn", [128, 5, 512], BF16); b_Vn = [Buf() for _ in range(5)]
        V4 = sb("V4", [128, 8, 512], BF16); b_V4 = Buf()
        V16 = sb("V16", [128, 16, 512], BF16); b_V16 = Buf()
        uT = sb("uT", [128, 4, 528], F32); b_uT = [Buf() for _ in range(4)]
        attnT = sb("attnT", [128, 8, 512], BF16); b_attnT = [Buf() for _ in range(8)]
        poolT = sb("poolT", [128, 4, 512], BF16); b_poolT = Buf()
        ring = sb("ring", [128, NSLOT, 4096], BF16); b_ring = [Buf() for _ in range(NSLOT)]
        identb = sb("identb", [128, 128], BF16); mcur = sb("mcur", [128, 512], BF16); mprev = sb("mprev", [128, 512], BF16)
        m16 = sb("m16", [128, 4, 512], BF16); onesb = sb("onesb", [128, 64], BF16)
        g1 = sb("g1", [128, 8], F32); g2 = sb("g2", [128, 8], F32)
        gqb = sb("gqb", [128, 512], F32); gkb = sb("gkb", [128, 512], F32); negM = sb("negM", [128, 4], F32)
        cpg = sb("cpg", [128, 22, 4], F32); cpv = sb("cpv", [128, 22, 4], F32)
        hist_g = sb("hist_g", [128, 22, 32], F32); hist_v = sb("hist_v", [128, 22, 32], F32)
        b_hist = Buf()
        hist2_g = sb("hist2_g", [128, 22, 2], F32); hist2_v = sb("hist2_v", [128, 22, 2], F32)
        pscale = sb("pscale", [128, 4], F32); wpool = sb("wpool", [128, 4, 128], BF16); invcnt = sb("invcnt", [128, 4, 16], F32)
        b_const = Buf()
        arena = sb("arena", [128, 21696], BF16)
        junk = sb("junk", [128, 1024], BF16); b_junk = Buf()
        off = [0]

        def carve(nbf16, reset=False):
            if reset:
                off[0] = 0
            a = arena[:, off[0]:off[0] + nbf16]
            off[0] += nbf16
            assert off[0] <= 21696
            return a
        qT = carve(8 * 512, True).rearrange("p (h t) -> p h t", h=8); b_qT = Buf()
        tmpn = carve(1024).bitcast(F32); b_tmpn = Buf()
        ksts = [carve(1024).bitcast(F32), carve(1024).bitcast(F32)]; b_ksts = [Buf(), Buf()]
        vsts = [carve(1024).bitcast(F32), carve(1024).bitcast(F32)]; b_vsts = [Buf(), Buf()]
        kvc = [0]
        qn_bfs = [carve(512), carve(512)]; b_qns = [Buf(), Buf()]
        kn_bfs = [carve(512), carve(512)]; b_kns = [Buf(), Buf()]
        nbc = [0]
        PT = [carve(512) for _ in range(3)]; b_PT = [Buf() for _ in range(3)]
        dT = carve(4 * 512).rearrange("p (g t) -> p g t", g=4); b_dT = Buf()
        sa = carve(1056).bitcast(F32); sbb = carve(1056).bitcast(F32); b_s = Buf()
        rlb = carve(1024).bitcast(F32); b_rlb = Buf()
        rlb2 = carve(1024).bitcast(F32); b_rlb2 = Buf()
        rlbs = [rlb, rlb2]; b_rlbs = [b_rlb, b_rlb2]
        tmp16 = carve(32).bitcast(F32)
        A_bufs = [b_qT, b_tmpn, b_dT, b_s, b_rlb, b_rlb2] + b_PT + b_qns + b_kns + b_ksts + b_vsts
        gT = carve(22 * 512, True).rearrange("p (k t) -> p k t", k=22); b_gT = Buf()
        ext = [[carve(1056).bitcast(F32) for _ in range(2)] for _ in range(2)]; b_ext = [[Buf(), Buf()], [Buf(), Buf()]]
        acc = [[carve(1024).bitcast(F32)], [carve(1024).bitcast(F32) for _ in range(2)]]; b_acc = [[Buf()], [Buf(), Buf()]]
        sg = [carve(1024).bitcast(F32) for _ in range(2)]; b_sg = [Buf(), Buf()]
        b_histd = {(k_, a_, p_): Buf() for k_ in 'gv' for a_ in range(22) for p_ in range(2)}
        b_hist_all = list(b_histd.values())
        ccnt = {'g': 0, 'v': 0}
        pend_silu = [None]
        pend_evac = [None]
        pend_tr = [None]
        upst = carve(1024).bitcast(F32); b_upst = Buf()
        F_bufs = [b_gT, b_upst] + b_sg + b_ext[0] + b_ext[1] + b_acc[0] + b_acc[1]
        kTflat = kT[:].rearrange("p h t -> p (h t)")
        Kc = [kTflat[:, 0:3584].rearrange("p (a c) -> p a c", a=7)] * 2; b_Kc = [Buf()] * 2
        Vc = [kTflat[:, 3584:7168].rearrange("p (a c) -> p a c", a=7)] * 2; b_Vc = [Buf()] * 2
        pbank = [es.enter_context(nc.psum_tensor("pb%d" % i, [128, 512], F32)) for i in range(8)]
        pA = [pbank[0], pbank[1]]; b_pA = [Buf(), Buf()]
        pTb = [pbank[2][:].bitcast(BF16), pbank[7][:].bitcast(BF16)]; b_pT = [Buf(), Buf()]
        pS = [pbank[3], pbank[4]]; b_pS = [Buf(), Buf()]
        pO = pbank[5]; b_pO = Buf()
        pL = pbank[6]; b_pL = Buf()

        def Aop(fn, r, w): P.op('act', fn, r, w)
        def Vop(fn, r, w): P.op('dve', fn, r, w)
        def Gop(fn, r, w): P.op('pool', fn, r, w)
        def Top(fn, r, w): P.op('pe', fn, r, w)
        def Dop(fn, r, w, ch, group=None, q='sp'): return P.op(q, fn, r, w, chan=ch, group=group)

        ch_c = [P.chan() for _ in range(4)]
        ch_xs = [P.chan() for _ in range(4)]; ch_ys = [P.chan() for _ in range(4)]; ch_k = P.chan(); ch_vo = P.chan(); ch_v = P.chan(); ch_q = P.chan()
        ch_w = [P.chan() for _ in range(NSLOT)]
        ch_pro = [P.chan() for _ in range(14)]
        ch_ol = [P.chan() for _ in range(4)]; ch_oc = [0]; ch_s = [P.chan(), P.chan()]

        def cho():
            ch_oc[0] += 1
            return ch_ol[ch_oc[0] % 4]

        cc = [0]

        def cload(out, in_, slow=False, q='sp'):
            c = ch_c[cc[0] % 4]; cc[0] += 1
            if slow:
                Dop(lambda e: e.dma_start(out=out, in_=in_, allow_slow_non_contiguous=True), [], [b_const], c, q=q)
            else:
                Dop(lambda e: e.dma_start(out=out, in_=in_), [], [b_const], c, q=q)
        cload(identb[:], c_id); cload(mcur[:], c_mcur); cload(mprev[:], c_mprev); cload(m16[:], c_m16)
        cload(gqb[:], gq_in); cload(gkb[:], gk_in); cload(invcnt[:], c_invcnt)
        cload(g1[:], g_attn[0, :].rearrange("(c p) -> p c", p=128), slow=True)
        cload(g2[:], g_ffn[0, :].rearrange("(c p) -> p c", p=128), slow=True)
        cload(pscale[:], pool_scale[0, :].rearrange("(g p) -> p g", p=128), slow=True)
        for j in range(3):
            cload(cpg[:, :, j], conv_w[j, 0:2816].rearrange("(a p) -> p a", p=128), slow=True)
            cload(cpv[:, :, j], conv_w[j, 2816:5632].rearrange("(a p) -> p a", p=128), slow=True)
        cload(cpg[:, :, 3], conv_b[0, 0:2816].rearrange("(a p) -> p a", p=128), slow=True)
        cload(cpv[:, :, 3], conv_b[0, 2816:5632].rearrange("(a p) -> p a", p=128), slow=True)
        cload(wpool[:], w_pool.rearrange("g c d -> c g d"), q='pool')
        Dop(lambda e: e.dma_start(out=kT[64:68, :, :], in_=c_kaug), [], [b_kT], ch_c[0])
        Vop(lambda e: e.tensor_scalar(out=g1[:], in0=g1[:], scalar1=32.0, scalar2=None, op0=ALU.mult), [b_const], [b_const])
        Vop(lambda e: e.tensor_scalar(out=g2[:], in0=g2[:], scalar1=32.0, scalar2=None, op0=ALU.mult), [b_const], [b_const])
        Vop(lambda e: e.memset(onesb[:], 1.0), [], [b_const])
        Vop(lambda e: e.tensor_tensor(out=xh[:, 0, 0:512], in0=gqb[:], in1=gqb[:], op=ALU.mult), [b_const], [b_xh])
        Vop(lambda e: e.reduce_max(out=negM[:, 1:2], in_=xh[:, 0, 0:512], axis=AX.X), [b_xh], [b_const])
        Vop(lambda e: e.tensor_tensor(out=xh[:, 0, 0:512], in0=gkb[:], in1=gkb[:], op=ALU.mult), [b_const], [b_xh])
        Vop(lambda e: e.reduce_max(out=negM[:, 2:3], in_=xh[:, 0, 0:512], axis=AX.X), [b_xh], [b_const])
        Vop(lambda e: e.tensor_tensor(out=negM[:, 3:4], in0=negM[:, 1:2], in1=negM[:, 2:3], op=ALU.add), [b_const], [b_const])
        Vop(lambda e: e.tensor_scalar(out=negM[:, 0:1], in0=negM[:, 3:4], scalar1=-4.0, scalar2=None, op0=ALU.mult), [b_const], [b_const])
        Vop(lambda e: e.tensor_scalar(out=gkb[:], in0=gkb[:], scalar1=8.0, scalar2=None, op0=ALU.mult), [b_const], [b_const])

        b_scr = {}
        pc_ = [0]

        def pro(key, out, in_):
            c = ch_pro[pc_[0] % 14]; pc_[0] += 1
            b = b_scr.setdefault(key, Buf())
            Dop(lambda e: e.dma_start(out=out, in_=in_), [], [b], c, q='pool')
        for g in range(4):
            pro(('in', g), win_s[g], w_in[:, g * 512:(g + 1) * 512].rearrange("(c p) n -> p c n", p=128))
        for f in range(2):
            pro(('oa', f), woa_s[f], w_out[0:512, f * 512:(f + 1) * 512].rearrange("(h p) n -> p h n", p=64))
            pro(('op', f), wop_s[f], w_out[512:1024, f * 512:(f + 1) * 512].rearrange("(g p) n -> p g n", p=128))
        def late_pro():
            for i in range(11):
                pro(('up', i), wup_s[i, :, :, 0:256], w_up[:, 256 * i:256 * i + 256].rearrange("(c p) n -> p c n", p=128))
                pro(('up', i), wup_s[i, :, :, 256:512], w_up[:, 2816 + 256 * i:2816 + 256 * i + 256].rearrange("(c p) n -> p c n", p=128))
            for f in range(2):
                for pc in range(3):
                    nk = 8 if pc < 2 else 6
                    pro(('dn', f, pc), wdn_s[f, pc, :, 0:nk, :],
                        w_down[pc * 1024:pc * 1024 + nk * 128, f * 512:(f + 1) * 512].rearrange("(k p) n -> p k n", p=128))

        piece_list = []
        NT_TILES = 9
        for t in range(NT_TILES):
            for g in (2, 0, 1, 3):
                piece_list.append((('in', g), win_s[g], 128, 4096))
            for f in range(2):
                piece_list.append((('oa', f), woa_s[f], 64, 4096))
                piece_list.append((('op', f), wop_s[f], 128, 2048))
            for i in range(11):
                piece_list.append((('up', i), wup_s[i], 128, 4096))
            for f in range(2):
                for pc in range(3):
                    piece_list.append((('dn', f, pc), wdn_s[f, pc], 128, 4096))
        issued = [0]

        def wget(i, hold=0):
            while issued[0] < min(len(piece_list), i - hold + NSLOT):
                k = issued[0]; issued[0] += 1
                key, src, npart, nel = piece_list[k]
                sl = k % NSLOT
                o = ring[0:npart, sl, 0:nel]
                s2 = src.rearrange("p a n -> p (a n)")
                Dop(lambda e, o=o, s2=s2: e.dma_start(out=o, in_=s2), [b_scr[key]], [b_ring[sl]], ch_w[sl])
            return i % NSLOT, b_ring[i % NSLOT]

        def slotv(sl, a):
            return ring[:, sl, 0:a * 512].rearrange("p (a n) -> p a n", n=512)

        pac = [0]

        rot6 = [(pA[0], b_pA[0]), (pA[1], b_pA[1]), (pS[0], b_pS[0]), (pS[1], b_pS[1]), (pO, b_pO), (pL, b_pL)]

        def nextpa():
            i = pac[0] % 6; pac[0] += 1
            return rot6[i]
        ptc = [0]

        def nextpt():
            i = ptc[0] % 2; ptc[0] += 1
            return pTb[i], b_pT[i]

        def do_tile(tidx, smp, s, m):
            NT = 64 if smp else 512; nsub = 1 if smp else 4; PP = 64 if smp else 128
            nseg = 16 if smp else 1; L = 4 if smp else 512; E = 16 + L
            T0 = 0 if smp else 512 * m
            base = tidx * 25
            last = smp or m == 3
            P.inherit(A_bufs, F_bufs)
            vgrp[0] = None
            if smp:
                xsrc = xs.rearrange("(j p) d -> p j d", p=64)
            else:
                xsrc = xp[s, T0:T0 + 512, :].rearrange("(j p) d -> p j d", p=128)
            for j in range(nsub):
                Dop(lambda e, j=j: e.dma_start(out=xh[0:PP, j, :], in_=xsrc[:, j, :]), [], [b_xhs[j]], ch_xs[j])
            if not smp:
                Dop(lambda e: e.dma_start(out=qT[64:68, :, :], in_=c_qaug[:, :, T0:T0 + 512]), [], [b_qT], ch_q)

            def norm_T(g):
                Vop(lambda e: e.memset(ss[:, 0:4], 0.0), [], [b_ss])
                for j in range(nsub):
                    Aop(lambda e, j=j: e.activation(out=junk[0:PP, :], in_=xh[0:PP, j, :], func=AF.Square, accum_out=ss[0:PP, j:j + 1]),
                        [b_xhs[j], b_ss], [b_junk, b_ss])
                Vop(lambda e: e.tensor_scalar(out=ss[0:PP, 4:8], in0=ss[0:PP, 0:4], scalar1=1024 * EPS, scalar2=None, op0=ALU.add), [b_ss], [b_ss])
                Aop(lambda e: e.activation(out=ss[0:PP, 8:12], in_=ss[0:PP, 4:8], func=AF.Ln), [b_ss], [b_ss])
                Aop(lambda e: e.activation(out=ss[0:PP, 12:16], in_=ss[0:PP, 8:12], func=AF.Exp, scale=-0.5), [b_ss], [b_ss])
                for j in range(nsub):
                    Aop(lambda e, j=j: e.activation(out=xn[0:PP, j, :], in_=xh[0:PP, j, :], func=AF.Copy, scale=ss[0:PP, 12 + j:13 + j]),
                        [b_xhs[j], b_ss], [b_xn])
                for c in range(8):
                    pt, bpt = nextpt()
                    for j in range(nsub):
                        Top(lambda e, pt=pt, j=j, c=c: e.transpose(out=pt[:, j * PP:(j + 1) * PP], in_=xn[0:PP, j, c * 128:(c + 1) * 128],
                                                                  identity=identb[0:PP, 0:PP]), [b_xn, b_const], [bpt])
                    Vop(lambda e, pt=pt, c=c: e.tensor_scalar(out=actT[:, c, 0:NT], in0=pt[:, 0:NT], scalar1=g[:, c:c + 1], scalar2=None, op0=ALU.mult),
                        [bpt, b_const], [b_actT])
            norm_T(g1)
            stage('norm1')

            for pos_, gi in enumerate((2, 0, 1)):
                sl, bsl = wget(base + pos_)
                sv = slotv(sl, 8)
                for j in range(nsub):
                    pa, bpa = nextpa()
                    for c in range(8):
                        Top(lambda e, pa=pa, j=j, c=c, sv=sv: e.matmul(pa[0:PP, :], lhsT=actT[:, c, j * PP:(j + 1) * PP], rhs=sv[:, c, :],
                                                                      start=(c == 0), stop=(c == 7)), [b_actT, bsl], [bpa])
                    if pend_tr[0] is not None:
                        pend_tr[0](); pend_tr[0] = None
                    rows = slice(T0 + j * 128, T0 + j * 128 + 128)
                    if gi < 2:
                        ssq = ssqs[sqc[0] % 2]; b_ssq = b_ssqs[sqc[0] % 2]; sqc[0] += 1
                        Vop(lambda e, ssq=ssq: e.memset(ssq[:, 0:8], 0.0), [], [b_ssq])
                        for h in range(8):
                            Aop(lambda e, pa=pa, h=h, ssq=ssq: e.activation(out=junk[0:PP, 0:64], in_=pa[0:PP, h * 64:(h + 1) * 64], func=AF.Square,
                                                                   accum_out=ssq[0:PP, h:h + 1]), [bpa, b_ssq], [b_junk, b_ssq])
                        Vop(lambda e, ssq=ssq: e.tensor_scalar(out=ssq[0:PP, 8:16], in0=ssq[0:PP, 0:8], scalar1=64 * EPS, scalar2=None, op0=ALU.add), [b_ssq], [b_ssq])
                        Aop(lambda e, ssq=ssq: e.activation(out=ssq[0:PP, 16:24], in_=ssq[0:PP, 8:16], func=AF.Ln), [b_ssq], [b_ssq])
                        Aop(lambda e, ssq=ssq: e.activation(out=ssq[0:PP, 24:32], in_=ssq[0:PP, 16:24], func=AF.Exp, scale=-0.5), [b_ssq], [b_ssq])
                        if pend_evac[0] is not None:
                            pend_evac[0](); pend_evac[0] = None
                        for h in range(8):
                            Vop(lambda e, pa=pa, h=h, ssq=ssq: e.tensor_scalar(out=tmpn[0:PP, h * 64:(h + 1) * 64], in0=pa[0:PP, h * 64:(h + 1) * 64],
                                                                      scalar1=ssq[0:PP, 24 + h:25 + h], scalar2=None, op0=ALU.mult), [bpa, b_ssq], [b_tmpn])
                        nbi = nbc[0] % 2; nbc[0] += 1
                        if gi == 0:
                            nb, bnb = qn_bfs[nbi], b_qns[nbi]
                            Vop(lambda e, nb=nb: e.tensor_tensor(out=nb[0:PP, :], in0=tmpn[0:PP, :], in1=gqb[0:PP, :], op=ALU.mult), [b_tmpn, b_const], [bnb])
                        else:
                            nb, bnb = kn_bfs[nbi], b_kns[nbi]
                            kst = ksts[nbi]; b_kst = b_ksts[nbi]
                            Vop(lambda e, kst=kst: e.tensor_tensor(out=kst[0:PP, :], in0=tmpn[0:PP, :], in1=gkb[0:PP, :], op=ALU.mult), [b_tmpn, b_const], [b_kst])
                            Vop(lambda e, nb=nb, kst=kst: e.tensor_copy(out=nb[0:PP, :], in_=kst[0:PP, :]), [b_kst], [bnb])
                            dst = nks if smp else nkp[s, rows, :]
                            Dop(lambda e, dst=dst, kst=kst: e.dma_start(out=dst, in_=kst[0:PP, :]), [b_kst], [], ch_k)
                        if not smp:
                            def tr(nb=nb, bnb=bnb, gi=gi, j=j, rows=rows):
                                pt, bpt = nextpt()
                                for h in range(8):
                                    Top(lambda e, h=h: e.transpose(out=pt[0:64, h * 128:(h + 1) * 128], in_=nb[:, h * 64:(h + 1) * 64], identity=identb[:]),
                                        [bnb, b_const], [bpt])
                                src = pt[0:64, :].rearrange("p (h t) -> p h t", t=128)
                                if gi == 0:
                                    pend_evac[0] = (lambda: Aop(lambda e: e.activation(out=qT[0:64, :, j * 128:(j + 1) * 128], in_=src, func=AF.Copy), [bpt], [b_qT]))
                                else:
                                    pend_evac[0] = (lambda: Aop(lambda e: e.activation(out=kT[0:64, :, rows], in_=src, func=AF.Copy), [bpt], [b_kT]))
                            pend_tr[0] = tr
                        else:
                            pt, bpt = nextpt()
                            for hp in range(4):
                                Top(lambda e, pt=pt, hp=hp, nb=nb: e.transpose(out=pt[:, hp * 64:(hp + 1) * 64], in_=nb[0:64, hp * 128:(hp + 1) * 128],
                                                                              identity=identb[0:64, 0:64]), [bnb, b_const], [bpt])
                            if gi == 0:
                                for hh in range(2):
                                    Aop(lambda e, pt=pt, hh=hh: e.activation(
                                        out=qbd[hh * 64:(hh + 1) * 64, :, :, hh * 4:hh * 4 + 4],
                                        in_=pt[hh * 64:(hh + 1) * 64, 0:256].rearrange("p (a b t) -> p a b t", a=4, b=16), func=AF.Copy), [bpt], [b_qbd])
                            else:
                                Aop(lambda e, pt=pt: e.activation(out=kTp[:, :, :], in_=pt[:, 0:256].rearrange("p (a t) -> p a t", a=4), func=AF.Copy), [bpt], [b_kTp])
                    else:
                        B = (4 * m + j) if not smp else 0
                        vs = B % 5
                        vst = vsts[kvc[0] % 2]; b_vst = b_vsts[kvc[0] % 2]; kvc[0] += 1
                        Aop(lambda e, pa=pa, vst=vst: e.activation(out=vst[0:PP, :], in_=pa[0:PP, :], func=AF.Copy), [bpa], [b_vst])
                        dst = nvs if smp else nvp[s, rows, :]
                        Dop(lambda e, dst=dst, vst=vst: e.dma_start(out=dst, in_=vst[0:PP, :]), [b_vst], [], ch_vo)
                        Vop(lambda e, vs=vs, vst=vst: e.tensor_copy(out=Vn[0:PP, vs, :], in_=vst[0:PP, :]), [b_vst], [b_Vn[vs]])
                        if not smp and os.environ.get('NOVDMA') is None:
                            grp = vgrp[0]
                            for c4 in range(4):
                                grp = Dop(lambda e, c4=c4, vs=vs, j=j: e.dma_start(out=V4[32 * j:32 * j + 32, (m % 2) * 4 + c4, :], in_=Vn[c4:128:4, vs, :]),
                                          [b_Vn[vs]], [b_V4], ch_v, group=grp, q='pool')
                            for c16 in range(16):
                                grp = Dop(lambda e, c16=c16, vs=vs, j=j: e.dma_start(out=V16[32 * m + 8 * j:32 * m + 8 * j + 8, c16, :], in_=Vn[c16:128:16, vs, :]),
                                          [b_Vn[vs]], [b_V16], ch_v, group=grp, q='pool')
                            vgrp[0] = grp
            if pend_tr[0] is not None:
                pend_tr[0](); pend_tr[0] = None
            if pend_evac[0] is not None:
                pend_evac[0](); pend_evac[0] = None
            stage('qkv')
            sl, bsl = wget(base + 3)
            sv = slotv(sl, 8)
            for g in range(4):
                pa, bpa = nextpa()
                for c in range(8):
                    Top(lambda e, pa=pa, g=g, c=c, sv=sv: e.matmul(pa[:, 0:NT], lhsT=sv[:, c, g * 128:(g + 1) * 128], rhs=actT[:, c, 0:NT],
                                                                  start=(c == 0), stop=(c == 7)), [b_actT, bsl], [bpa])
                ue = uT[:, g, 0:nseg * E].rearrange("p (s e) -> p s e", e=E)
                Aop(lambda e, pa=pa, ue=ue: e.activation(out=ue[:, :, 16:E], in_=pa[:, 0:NT].rearrange("p (s l) -> p s l", l=L), func=AF.Copy), [bpa], [b_uT[g]])
            if last:
                pa, bpa = nextpa()
                for c in range(8):
                    Top(lambda e, pa=pa, c=c, sv=sv: e.matmul(pa[0:PP, :], lhsT=actT[:, c, NT - PP:NT], rhs=sv[:, c, :], start=(c == 0), stop=(c == 7)),
                        [b_actT, bsl], [bpa])
                Aop(lambda e, pa=pa: e.activation(out=tmpn[0:PP, :], in_=pa[0:PP, :], func=AF.Copy), [bpa], [b_tmpn])
                if smp:
                    for t4 in range(4):
                        Dop(lambda e, t4=t4: e.dma_start(out=nps[:, 11 + t4, :], in_=tmpn[t4:64:4, :]), [b_tmpn], [], cho())
                else:
                    Dop(lambda e: e.dma_start(out=npp[s, :, :], in_=tmpn[113:128, :]), [b_tmpn], [], cho())
            stage('u')
            for g in range(4):
                w = 2 << g
                ue = uT[:, g, 0:nseg * E].rearrange("p (s e) -> p s e", e=E)
                s1 = sa[:, 0:nseg * E].rearrange("p (s e) -> p s e", e=E)
                s2 = sbb[:, 0:nseg * E].rearrange("p (s e) -> p s e", e=E)
                Gop(lambda e, ue=ue, s1=s1: e.tensor_tensor(out=s1[:, :, 1:E], in0=ue[:, :, 1:E], in1=ue[:, :, 0:E - 1], op=ALU.add), [b_uT[g]], [b_s])
                fin = s1
                if w >= 4:
                    Gop(lambda e, s1=s1, s2=s2: e.tensor_tensor(out=s2[:, :, 3:E], in0=s1[:, :, 3:E], in1=s1[:, :, 1:E - 2], op=ALU.add), [b_s], [b_s]); fin = s2
                if w >= 8:
                    Gop(lambda e, s1=s1, s2=s2: e.tensor_tensor(out=s1[:, :, 7:E], in0=s2[:, :, 7:E], in1=s2[:, :, 3:E - 4], op=ALU.add), [b_s], [b_s]); fin = s1
                if w >= 16:
                    Gop(lambda e, s1=s1, s2=s2: e.tensor_tensor(out=s2[:, :, 15:E], in0=s1[:, :, 15:E], in1=s1[:, :, 7:E - 8], op=ALU.add), [b_s], [b_s]); fin = s2
                dv = dT[:, g, 0:NT].rearrange("p (s l) -> p s l", l=L)
                Vop(lambda e, fin=fin, ue=ue, dv=dv, w=w: e.scalar_tensor_tensor(out=dv, in0=fin[:, :, 16:E], scalar=1.0 / w, in1=ue[:, :, 16:E],
                                                                                 op0=ALU.mult, op1=ALU.subtract), [b_s, b_uT[g]], [b_dT])
                if (not smp) and m == 0:
                    Gop(lambda e, fin=fin, g=g: e.tensor_tensor(out=tmp16[:, 0:16], in0=fin[:, 0, 16:32], in1=invcnt[:, g, :], op=ALU.mult), [b_s, b_const], [b_s])
                    Gop(lambda e, ue=ue, g=g: e.tensor_tensor(out=dT[:, g, 0:16], in0=tmp16[:, 0:16], in1=ue[:, 0, 16:32], op=ALU.subtract), [b_s, b_uT[g]], [b_dT])
                pa, bpa = nextpa()
                Top(lambda e, pa=pa, g=g: e.matmul(pa[:, 0:NT], lhsT=wpool[:, g, :], rhs=dT[:, g, 0:NT], start=True, stop=True), [b_dT, b_const], [bpa])
                Vop(lambda e, pa=pa, g=g: e.tensor_scalar(out=poolT[:, g, 0:NT], in0=pa[:, 0:NT], scalar1=pscale[:, g:g + 1], scalar2=None, op0=ALU.mult),
                    [bpa, b_const], [b_poolT])
                if not smp:
                    Gop(lambda e, g=g: e.tensor_copy(out=uT[:, g, 0:16], in_=uT[:, g, 512:528]), [b_uT[g]], [b_uT[g]])

            if tidx == 0:
                late_pro()
            stage('pool')
            if not smp:
                glist = []
                for h in range(8):
                    kinds = [k_ for k_ in ('1c', '1p', '4c', '4p', '16') if not (k_ == '4p' and m == 0)]
                    for ki_, kind in enumerate(kinds):
                        glist.append((h, kind, ki_ == 0, ki_ == len(kinds) - 1))

                def g_params(gi_):
                    h, kind, first, lastk = glist[gi_]
                    ps, bps = pS[gi_ % 2], b_pS[gi_ % 2]
                    pt_, bpt_ = PT[gi_ % 3], b_PT[gi_ % 3]
                    R = 128; c0 = 0
                    if kind == '16':
                        R = 32 * (m + 1); mask = m16[:, m, :]
                    else:
                        mask = mcur if kind in ('1c', '4c') else mprev
                        if kind == '1p' and m == 0:
                            c0 = 128
                    return h, kind, first, lastk, ps, bps, pt_, bpt_, R, c0, mask

                def emit_scores(gi_):
                    h, kind, first, lastk, ps, bps, pt_, bpt_, R, c0, mask = g_params(gi_)
                    Top(lambda e: e.matmul(ps[0:R, c0:512], lhsT=identb[0:R, 0:R], rhs=mask[0:R, c0:512], start=True, stop=False), [b_const], [bps])
                    if kind in ('1c', '1p'):
                        for n in range(c0 // 128, 4):
                            kb = T0 + n * 128 - (128 if kind == '1p' else 0)
                            Top(lambda e, n=n, kb=kb: e.matmul(ps[:, n * 128:(n + 1) * 128], lhsT=kT[0:68, h, kb:kb + 128],
                                                              rhs=qT[0:68, h, n * 128:(n + 1) * 128], start=False, stop=True, skip_group_check=True), [b_kT, b_qT], [bps])
                    elif kind in ('4c', '4p'):
                        for c4 in range(4):
                            kb = T0 + c4 - (512 if kind == '4p' else 0)
                            Top(lambda e, c4=c4, kb=kb: e.matmul(ps[:, c4 * 128:(c4 + 1) * 128], lhsT=kT[0:68, h, kb:kb + 509:4],
                                                                rhs=qT[0:68, h, c4:512:4], start=False, stop=True, skip_group_check=True), [b_kT, b_qT], [bps])
                    else:
                        for c16 in range(16):
                            Top(lambda e, c16=c16: e.matmul(ps[0:R, c16 * 32:(c16 + 1) * 32], lhsT=kT[0:68, h, c16:T0 + 512:16],
                                                           rhs=qT[0:68, h, c16:512:16], start=False, stop=True, skip_group_check=True), [b_kT, b_qT], [bps])
                    Aop(lambda e: e.activation(out=pt_[0:R, c0:512], in_=ps[0:R, c0:512], func=AF.Exp, bias=negM[0:R, 0:1], scale=1.0), [bps, b_const], [bpt_])

                def emit_pv(gi_):
                    h, kind, first, lastk, ps, bps, pt_, bpt_, R, c0, mask = g_params(gi_)
                    (pO_, b_pO_), (pL_, b_pL_) = ((pO, b_pO), (pL, b_pL)) if h % 2 == 0 else ((pA[0], b_pA[0]), (pA[1], b_pA[1]))
                    rlb_, b_rlb_ = rlbs[h % 2], b_rlbs[h % 2]
                    if first:
                        Vop(lambda e: e.memset(pO_[0:64, :], 0.0), [], [b_pO_])
                        Vop(lambda e: e.memset(pL_[0:64, :], 0.0), [], [b_pL_])
                    hs = slice(h * 64, (h + 1) * 64)
                    kw = dict(start=False, stop=False, skip_group_check=True)
                    if kind in ('1c', '1p'):
                        for n in range(c0 // 128, 4):
                            vs = (4 * m + n - (1 if kind == '1p' else 0)) % 5
                            Top(lambda e, n=n, vs=vs: e.matmul(pO_[0:64, n * 128:(n + 1) * 128], lhsT=Vn[:, vs, hs], rhs=pt_[:, n * 128:(n + 1) * 128], **kw),
                                [b_Vn[vs], bpt_], [b_pO_])
                        Top(lambda e: e.matmul(pL_[0:64, c0:512], lhsT=onesb[:, 0:64], rhs=pt_[:, c0:512], **kw), [b_const, bpt_], [b_pL_])
                    elif kind in ('4c', '4p'):
                        for c4 in range(4):
                            vsl = ((m if kind == '4c' else m - 1) % 2) * 4 + c4
                            Top(lambda e, c4=c4, vsl=vsl: e.matmul(pO_[0:64, c4:512:4], lhsT=V4[:, vsl, hs], rhs=pt_[:, c4 * 128:(c4 + 1) * 128], **kw), [b_V4, bpt_], [b_pO_])
                        Top(lambda e: e.matmul(pL_[0:64, :].rearrange("p (i c) -> p c i", c=4), lhsT=onesb[:, 0:64], rhs=pt_[:, 0:512].rearrange("p (c i) -> p c i", c=4), **kw),
                            [b_const, bpt_], [b_pL_])
                    else:
                        for c16 in range(16):
                            Top(lambda e, c16=c16: e.matmul(pO_[0:64, c16:512:16], lhsT=V16[0:R, c16, hs], rhs=pt_[0:R, c16 * 32:(c16 + 1) * 32], **kw), [b_V16, bpt_], [b_pO_])
                        Top(lambda e: e.matmul(pL_[0:64, :].rearrange("p (i c) -> p c i", c=16), lhsT=onesb[0:R, 0:64], rhs=pt_[0:R, 0:512].rearrange("p (c i) -> p c i", c=16), **kw),
                            [b_const, bpt_], [b_pL_])
                    if lastk:
                        Aop(lambda e: e.activation(out=rlb_[0:64, :], in_=pL_[0:64, :], func=AF.Ln), [b_pL_], [b_rlb_])
                        Aop(lambda e: e.activation(out=rlb_[0:64, :], in_=rlb_[0:64, :], func=AF.Exp, scale=-1.0), [b_rlb_], [b_rlb_])
                        Vop(lambda e: e.tensor_tensor(out=attnT[0:64, h, :], in0=pO_[0:64, :], in1=rlb_[0:64, :], op=ALU.mult), [b_pO_, b_rlb_], [b_attnT[h]])

                emit_scores(0)
                for gi_ in range(len(glist)):
                    if gi_ + 1 < len(glist):
                        emit_scores(gi_ + 1)
                    emit_pv(gi_)
            else:
                sample_attention()

            stage('attn')
            for f in range(2):
                sla, bsla = wget(base + 4 + 2 * f)
                slp, bslp = wget(base + 5 + 2 * f, hold=1)
                sva = slotv(sla, 8); svp = slotv(slp, 4)
                for j in range(nsub):
                    pa, bpa = nextpa()
                    for hh in range(8):
                        Top(lambda e, pa=pa, hh=hh, j=j, sva=sva: e.matmul(pa[0:PP, :], lhsT=attnT[0:64, hh, j * PP:(j + 1) * PP], rhs=sva[0:64, hh, :],
                                                                          start=(hh == 0), stop=False), [b_attnT[hh], bsla], [bpa])
                    for g in range(4):
                        Top(lambda e, pa=pa, g=g, j=j, svp=svp: e.matmul(pa[0:PP, :], lhsT=poolT[:, g, j * PP:(j + 1) * PP], rhs=svp[:, g, :],
                                                                        start=False, stop=(g == 3)), [b_poolT, bslp], [bpa])
                    Vop(lambda e, pa=pa, j=j, f=f: e.tensor_tensor(out=xh[0:PP, j, f * 512:(f + 1) * 512], in0=pa[0:PP, :], in1=xh[0:PP, j, f * 512:(f + 1) * 512], op=ALU.add),
                        [bpa, b_xhs[j]], [b_xhs[j]])
            stage('wout')
            P.inherit(F_bufs, A_bufs)
            norm_T(g2)
            hg = hist_g; hv = hist_v
            for i in range(11):
                sl, bsl = wget(base + 8 + i)
                sv = slotv(sl, 8)
                for (col0, kind, a) in ((0, 'g', 2 * i), (256, 'v', 2 * i), (128, 'g', 2 * i + 1), (384, 'v', 2 * i + 1)):
                    pa, bpa = nextpa()
                    for c in range(8):
                        Top(lambda e, pa=pa, c=c, col0=col0, sv=sv: e.matmul(pa[:, 0:NT], lhsT=sv[:, c, col0:col0 + 128], rhs=actT[:, c, 0:NT],
                                                                            start=(c == 0), stop=(c == 7)), [b_actT, bsl], [bpa])
                    ki = 0 if kind == 'g' else 1
                    cn = ccnt[kind]; ccnt[kind] += 1
                    nacc = len(acc[ki])
                    acf = acc[ki][cn % nacc]; bac = b_acc[ki][cn % nacc]
                    ac = acf[:, 0:NT].rearrange("p (s l) -> p s l", l=L)
                    cw = (cpg if kind == 'g' else cpv)
                    par = 0 if smp else (m % 2)
                    hbig = (hg if kind == 'g' else hv); hsml = (hist2_g if kind == 'g' else hist2_v)
                    hold = (hbig[:, a, 0:nseg * 2] if par == 0 else hsml[:, a, 0:2]).rearrange("p (s e) -> p s e", e=2)
                    hnew = (hsml[:, a, 0:2] if par == 0 else hbig[:, a, 0:2]).rearrange("p (s e) -> p s e", e=2)
                    bho = b_histd[(kind, a, par)]; bhn = b_histd[(kind, a, 1 - par)]
                    pav = pa[:, 0:NT].rearrange("p (s l) -> p s l", l=L)
                    if not smp:
                        Aop(lambda e, pav=pav, hnew=hnew: e.activation(out=hnew, in_=pav[:, :, L - 2:L], func=AF.Copy), [bpa], [bhn])
                    Aop(lambda e, pav=pav, ac=ac, cw=cw, a=a: e.activation(out=ac, in_=pav, func=AF.Identity, scale=cw[:, a, 2:3], bias=cw[:, a, 3:4]),
                        [bpa, b_const], [bac])
                    Vop(lambda e, pav=pav, ac=ac, cw=cw, a=a: e.scalar_tensor_tensor(out=ac[:, :, 1:L], in0=pav[:, :, 0:L - 1], scalar=cw[:, a, 1:2], in1=ac[:, :, 1:L],
                                                                                   op0=ALU.mult, op1=ALU.add), [bpa, b_const, bac], [bac])
                    Vop(lambda e, pav=pav, ac=ac, cw=cw, a=a: e.scalar_tensor_tensor(out=ac[:, :, 2:L], in0=pav[:, :, 0:L - 2], scalar=cw[:, a, 0:1], in1=ac[:, :, 2:L],
                                                                                   op0=ALU.mult, op1=ALU.add), [bpa, b_const, bac], [bac])
                    Vop(lambda e, hold=hold, ac=ac, cw=cw, a=a: e.scalar_tensor_tensor(out=ac[:, :, 0:2], in0=hold[:, :, 0:2], scalar=cw[:, a, 0:1], in1=ac[:, :, 0:2],
                                                                                     op0=ALU.mult, op1=ALU.add), [bho, b_const, bac], [bac])
                    Vop(lambda e, hold=hold, ac=ac, cw=cw, a=a: e.scalar_tensor_tensor(out=ac[:, :, 0:1], in0=hold[:, :, 1:2], scalar=cw[:, a, 1:2], in1=ac[:, :, 0:1],
                                                                                     op0=ALU.mult, op1=ALU.add), [bho, b_const, bac], [bac])
                    if kind == 'g':
                        sgi = cn % 2
                        pend_silu[0] = (acf, sgi, bac)
                    else:
                        sgi = cn % 2
                        acf_g, sgi_g, bac_g = pend_silu[0]
                        Aop(lambda e, acf_g=acf_g, sgi_g=sgi_g: e.activation(out=sg[sgi_g][:, 0:NT], in_=acf_g[:, 0:NT], func=AF.Silu), [bac_g], [b_sg[sgi_g]])
                        Gop(lambda e, acf=acf, a=a, sgi=sgi: e.tensor_tensor(out=gT[:, a, 0:NT], in0=sg[sgi][:, 0:NT], in1=acf[:, 0:NT], op=ALU.mult), [b_sg[sgi], bac], [b_gT])
                if last:
                    pa, bpa = nextpa()
                    for c in range(8):
                        Top(lambda e, pa=pa, c=c, sv=sv: e.matmul(pa[0:PP, :], lhsT=actT[:, c, NT - PP:NT], rhs=sv[:, c, :], start=(c == 0), stop=(c == 7)),
                            [b_actT, bsl], [bpa])
                    Aop(lambda e, pa=pa: e.activation(out=upst[0:PP, :], in_=pa[0:PP, :], func=AF.Copy), [bpa], [b_upst])
                    for (co, fo) in ((0, 256 * i), (256, 2816 + 256 * i)):
                        if smp:
                            for jj in range(2):
                                Dop(lambda e, co=co, fo=fo, jj=jj: e.dma_start(out=nfs[:, jj, fo:fo + 256], in_=upst[2 + jj:64:4, co:co + 256]), [b_upst], [], cho())
                        else:
                            Dop(lambda e, co=co, fo=fo: e.dma_start(out=nfp[s, :, fo:fo + 256], in_=upst[126:128, co:co + 256]), [b_upst], [], cho())
            stage('ffn_up')
            accs = [(pA[0], b_pA[0]), (pA[1], b_pA[1]), (pS[0], b_pS[0]), (pS[1], b_pS[1])]
            f = 0
            for pc in range(3):
                sl, bsl = wget(base + 19 + pc)
                sv = slotv(sl, 8)
                for kk in range(8 if pc < 2 else 6):
                    kc = pc * 8 + kk
                    for j in range(nsub):
                        Top(lambda e, j=j, kc=kc, kk=kk, sv=sv: e.matmul(accs[j][0][0:PP, :], lhsT=gT[:, kc, j * PP:(j + 1) * PP], rhs=sv[:, kk, :],
                                                                        start=(kc == 0), stop=(kc == 21)), [b_gT, bsl], [accs[j][1]])
            for j in range(nsub):
                Vop(lambda e, j=j: e.tensor_tensor(out=xh[0:PP, j, 0:512], in0=accs[j][0][0:PP, :], in1=xh[0:PP, j, 0:512], op=ALU.add),
                    [accs[j][1], b_xhs[j]], [b_xhs[j]])
            f = 1
            sls = [wget(base + 22, hold=0), wget(base + 23, hold=1), wget(base + 24, hold=2)]
            for j in range(nsub):
                for pc in range(3):
                    sl, bsl = sls[pc]
                    sv = slotv(sl, 8)
                    for kk in range(8 if pc < 2 else 6):
                        kc = pc * 8 + kk
                        Top(lambda e, j=j, kc=kc, kk=kk, sv=sv: e.matmul(accs[j][0][0:PP, :], lhsT=gT[:, kc, j * PP:(j + 1) * PP], rhs=sv[:, kk, :],
                                                                        start=(kc == 0), stop=(kc == 21)), [b_gT, bsl], [accs[j][1]])
                Vop(lambda e, j=j: e.tensor_tensor(out=xh[0:PP, j, 512:1024], in0=accs[j][0][0:PP, :], in1=xh[0:PP, j, 512:1024], op=ALU.add),
                    [accs[j][1], b_xhs[j]], [b_xhs[j]])
            if smp:
                ydst = ys.rearrange("(j p) d -> p j d", p=64)
            else:
                ydst = yp[s, T0:T0 + 512, :].rearrange("(j p) d -> p j d", p=128)
            for j in range(nsub):
                Dop(lambda e, j=j: e.dma_start(out=ydst[:, j, :], in_=xh[0:PP, j, :]), [b_xhs[j]], [], ch_ys[j])

        qbd = sb("qbd", [128, 4, 16, 8], BF16); b_qbd = Buf()
        kTp = sb("kTp", [128, 4, 64], BF16); b_kTp = Buf()
        wt = sb("wt", [128, 224], F32); wtn = sb("wtn", [64, 16, 32], F32)
        vgrp = [None]

        def sample_prep():
            cload(wt[:], c_wt); cload(wtn[:], c_wtn)
            Vop(lambda e: e.memset(qbd[:].rearrange("p a b t -> p (a b t)"), 0.0), [], [b_qbd])
            sph = V16[0:120, 0:2, :]
            Dop(lambda e: e.dma_start(out=sph, in_=spool.rearrange("b i c -> (b i) c").rearrange("(two r) c -> r two c", r=120)), [], [b_V16], ch_s[0], q='pool')
            Dop(lambda e: e.dma_start(out=nps[:, 0:11, :], in_=spool[:, 4:15, :]), [], [], cho())
            for two in range(2):
                pt, bpt = nextpt()
                for g in range(4):
                    Top(lambda e, pt=pt, g=g, two=two: e.transpose(out=pt[:, g * 120:(g + 1) * 120], in_=V16[0:120, two, g * 128:(g + 1) * 128],
                                                                  identity=identb[0:120, 0:120]), [b_V16, b_const], [bpt])
                for g in range(4):
                    ue = uT[:, g, 0:320].rearrange("p (s e) -> p s e", e=20)
                    Aop(lambda e, pt=pt, g=g, two=two, ue=ue: e.activation(out=ue[:, 8 * two:8 * two + 8, 1:16], in_=pt[:, g * 120:(g + 1) * 120].rearrange("p (s i) -> p s i", i=15),
                                                                          func=AF.Copy), [bpt], [b_uT[g]])
            sfh = V16[0:32, 2:13, :].rearrange("p a t -> p (a t)")
            Dop(lambda e: e.dma_start(out=sfh[:, 0:5632], in_=sffn.rearrange("b j c -> (b j) c")), [], [b_V16], ch_s[1], q='pool')
            for kind in range(2):
                hs_ = hist_g if kind == 0 else hist_v
                for a0 in (0, 8, 16):
                    na = 8 if a0 < 16 else 6
                    pt, bpt = nextpt()
                    for a in range(a0, a0 + na):
                        f0 = a * 128 + (2816 if kind else 0)
                        Top(lambda e, pt=pt, a=a, a0=a0, f0=f0: e.transpose(out=pt[:, (a - a0) * 32:(a - a0 + 1) * 32], in_=sfh[0:32, f0:f0 + 128],
                                                                           identity=identb[0:32, 0:32]), [b_V16, b_const], [bpt])
                    Aop(lambda e, pt=pt, a0=a0, na=na, hs_=hs_: e.activation(out=hs_[:, a0:a0 + na, :], in_=pt[:, 0:na * 32].rearrange("p (a x) -> p a x", x=32),
                                                                             func=AF.Copy), [bpt], b_hist_all)

        pf = sb("pf", [128, 256], F32); b_pf = Buf()
        pts = sb("pts", [128, 256], BF16); b_pts = Buf()
        KcT = kTflat[:, 7168:10752].rearrange("p (a r) -> p a r", a=4); b_KcT = Buf()

        def sample_attention():
            P.inherit([b_Kc[0], b_Vc[0], b_KcT], [b_kT])
            Vop(lambda e: e.memset(pO[0:64, :], 0.0), [], [b_pO])
            Vop(lambda e: e.memset(pL[0:64, :], 0.0), [], [b_pL])
            for b in range(16):
                kc, bkc = Kc[b % 2], b_Kc[b % 2]
                vc, bvc = Vc[b % 2], b_Vc[b % 2]
                for (src, dstt, bd, chh) in ((ck, kc, bkc, ch_s[0]), (cv, vc, bvc, ch_s[1])):
                    grp = None
                    for r in range(4):
                        grp = Dop(lambda e, src=src, dstt=dstt, r=r, b=b: e.dma_start(out=dstt[r:128:4, 0:3, :],
                                                                                      in_=src[b, r:1536:16, :].rearrange("(tau a) c -> a tau c", a=32)),
                                  [], [bd], chh, group=grp, q='pool')
                    grp = Dop(lambda e, src=src, dstt=dstt, b=b: e.dma_start(out=dstt[:, 3:7, :], in_=src[b, 1536:2048, :].rearrange("(tau p) c -> p tau c", p=128)),
                              [], [bd], chh, group=grp, q='pool')
                for tau in range(7):
                    pt, bpt = nextpt()
                    for hp in range(4):
                        Top(lambda e, pt=pt, hp=hp, tau=tau, kc=kc: e.transpose(out=pt[:, hp * 128:(hp + 1) * 128], in_=kc[:, tau, hp * 128:(hp + 1) * 128], identity=identb[:]),
                            [bkc, b_const], [bpt])
                    Aop(lambda e, pt=pt, tau=tau: e.activation(out=KcT[:, :, tau * 128:(tau + 1) * 128], in_=pt[:, 0:512].rearrange("p (a r) -> p a r", a=4), func=AF.Copy),
                        [bpt], [b_KcT])
                ps, bps = pS[b % 2], b_pS[b % 2]
                for tau in range(7):
                    for hp in range(4):
                        Top(lambda e, ps=ps, tau=tau, hp=hp, b=b: e.matmul(ps[:, tau * 32 + hp * 8:tau * 32 + hp * 8 + 8], lhsT=KcT[:, hp, tau * 128:(tau + 1) * 128],
                                                                          rhs=qbd[:, hp, b, :], start=True, stop=True), [b_KcT, b_qbd], [bps])
                for hp in range(4):
                    Top(lambda e, ps=ps, hp=hp, b=b: e.matmul(ps[0:64, 224 + hp * 8:224 + hp * 8 + 8], lhsT=kTp[:, hp, 0:64], rhs=qbd[:, hp, b, :], start=True, stop=True),
                        [b_kTp, b_qbd], [bps])
                Aop(lambda e, ps=ps: e.activation(out=pf[:, 0:224], in_=ps[:, 0:224], func=AF.Exp, bias=negM[:, 0:1], scale=1.0), [bps, b_const], [b_pf])
                Aop(lambda e, ps=ps: e.activation(out=pf[0:64, 224:256], in_=ps[0:64, 224:256], func=AF.Exp, bias=negM[0:64, 0:1], scale=1.0), [bps, b_const], [b_pf])
                Vop(lambda e: e.tensor_tensor(out=pts[:, 0:224], in0=pf[:, 0:224], in1=wt[:, :], op=ALU.mult), [b_pf, b_const], [b_pts])
                Vop(lambda e, b=b: e.tensor_tensor(out=pts[0:64, 224:256], in0=pf[0:64, 224:256], in1=wtn[0:64, b, :], op=ALU.mult), [b_pf, b_const], [b_pts])
                for h in range(8):
                    o = pO[0:64, h * 64 + b * 4:h * 64 + b * 4 + 4]
                    for tau in range(7):
                        Top(lambda e, o=o, tau=tau, h=h, vc=vc: e.matmul(o, lhsT=vc[:, tau, h * 64:(h + 1) * 64], rhs=pts[:, tau * 32 + h * 4:tau * 32 + h * 4 + 4],
                                                                        start=False, stop=False, skip_group_check=True), [bvc, b_pts], [b_pO])
                    Top(lambda e, o=o, h=h: e.matmul(o, lhsT=Vn[0:64, 0, h * 64:(h + 1) * 64], rhs=pts[0:64, 224 + h * 4:224 + h * 4 + 4],
                                                     start=False, stop=False, skip_group_check=True), [b_Vn[0], b_pts], [b_pO])
                for tau in range(7):
                    Top(lambda e, tau=tau, b=b: e.matmul(pL[0:64, b * 32:(b + 1) * 32], lhsT=onesb[:, 0:64], rhs=pts[:, tau * 32:(tau + 1) * 32],
                                                         start=False, stop=False, skip_group_check=True), [b_const, b_pts], [b_pL])
                Top(lambda e, b=b: e.matmul(pL[0:64, b * 32:(b + 1) * 32], lhsT=onesb[0:64, 0:64], rhs=pts[0:64, 224:256],
                                            start=False, stop=False, skip_group_check=True), [b_const, b_pts], [b_pL])
            Vop(lambda e: e.reciprocal(out=rlb[0:64, :], in_=pL[0:64, :]), [b_pL], [b_rlb])
            for h in range(8):
                Vop(lambda e, h=h: e.tensor_tensor(out=attnT[0:64, h, 0:64].rearrange("p (b t) -> p b t", t=4),
                                                   in0=pO[0:64, h * 64:(h + 1) * 64].rearrange("p (b t) -> p b t", t=4),
                                                   in1=rlb[0:64, :].rearrange("p (b h t) -> p h b t", h=8, t=4)[:, h, :, :], op=ALU.mult),
                    [b_pO, b_rlb], [b_attnT[h]])

        tidx = 0
        try:
          stage('const')
          for s in range(2):
            for g in range(4):
                Gop(lambda e, g=g: e.memset(uT[:, g, 0:16], 0.0), [], [b_uT[g]])
            Gop(lambda e: e.memset(hist_g[:].rearrange("p a x -> p (a x)"), 0.0), [], b_hist_all)
            Gop(lambda e: e.memset(hist_v[:].rearrange("p a x -> p (a x)"), 0.0), [], b_hist_all)
            for m in range(4):
                do_tile(tidx, False, s, m)
                tidx += 1
                stage('tile%d' % (tidx - 1))
          sample_prep()
          stage('sprep')
          do_tile(tidx, True, 0, 0)
        except StopBuild:
            pass

        if os.environ.get('KDBG'):
            print('SBUF remaining', nc.sbuf_bytes_remaining)
        P.finalize()
        with nc.Block() as block:
            @block.tensor
            def _(e): P.run('pe', e)

            @block.scalar
            def _(e): P.run('act', e)

            @block.vector
            def _(e): P.run('dve', e)

            @block.gpsimd
            def _(e): P.run('pool', e)

            @block.sync
            def _(e):
                P.run('sp', e)
                for c in P.chans:
                    if c.count:
                        e.wait_ge(c.sem, c.count)
    return nc


def _consts():
    c = {}
    c["c_id"] = np.eye(128, dtype=np.float32).astype(BF)
    k = np.arange(128)[:, None]; q = np.arange(128)[None, :]
    c["c_mcur"] = np.tile(np.where(k <= q, 0.0, -30000.0).astype(np.float32), (1, 4)).astype(BF)
    c["c_mprev"] = np.tile(np.where(k >= q, 0.0, -30000.0).astype(np.float32), (1, 4)).astype(BF)
    m16 = np.zeros((128, 4, 16, 32), np.float32)
    for m in range(4):
        m16[:, m, :, :] = np.where(np.arange(128)[:, None] <= 32 * m + np.arange(32)[None, :], 0.0, -30000.0).astype(np.float32)[:, None, :]
    c["c_m16"] = m16.reshape(128, 4, 512).astype(BF)
    slopes = 2.0 ** (-np.arange(1, 9, dtype=np.float64))
    t = np.arange(2048)
    kaug = np.zeros((4, 8, 2048), np.float64)
    kaug[0] = (-64 * slopes)[:, None]; kaug[1] = (-slopes)[:, None]
    kaug[2] = 64 * slopes[:, None] * (t // 64)[None, :]; kaug[3] = slopes[:, None] * (t % 64)[None, :]
    c["c_kaug"] = kaug.astype(np.float32).astype(BF)
    qaug = np.zeros((4, 8, 2048), np.float32)
    qaug[0] = (t // 64)[None, :]; qaug[1] = (t % 64)[None, :]; qaug[2] = 1; qaug[3] = 1
    c["c_qaug"] = qaug.astype(BF)
    inv = np.zeros((128, 4, 16), np.float32)
    for g, w in enumerate((2, 4, 8, 16)):
        inv[:, g, :] = 1.0 / np.minimum(w, np.arange(16) + 1)
    c["c_invcnt"] = inv

    def cnt(d):
        d = np.asarray(d)
        return ((d >= 0) & (d <= 128)).astype(np.float64) + ((d >= 0) & (d % 4 == 0) & (d <= 512)) + ((d >= 0) & (d % 16 == 0) & (d <= 2048))
    p = np.arange(128)
    wt = np.zeros((128, 7, 8, 4), np.float64)
    for tau in range(7):
        row = 16 * (32 * tau + p // 4) + p % 4 if tau < 3 else 1536 + 128 * (tau - 3) + p
        for tt in range(4):
            d = 2048 + tt - row
            for h in range(8):
                wt[:, tau, h, tt] = cnt(d) * np.exp(-slopes[h] * d)
    c["c_wt"] = wt.reshape(128, 224).astype(np.float32)
    wtn = np.zeros((64, 16, 8, 4), np.float64)
    for b in range(16):
        for t1 in range(4):
            for tt in range(4):
                d = tt - t1
                if d >= 0:
                    wtn[4 * b + t1, b, :, tt] = cnt(d) * np.exp(-slopes * d)
    c["c_wtn"] = wtn.reshape(64, 16, 32).astype(np.float32)
    return c


_NC = [None]


def kernel(x_prompt, x_sample, cache_k, cache_v, state_pool, state_ffn_conv, g_attn_norm, w_in, g_q, g_k,
           w_pool, pool_scale, w_out, g_ffn_norm, w_up, conv_w, conv_b, w_down):
    f = lambda a: np.ascontiguousarray(np.asarray(a, dtype=np.float32))
    nc = build()
    consts = _consts()
    shared = dict(
        g_attn=f(g_attn_norm).reshape(1, 1024), w_in=f(w_in)[0], gq=np.ascontiguousarray(np.broadcast_to(f(g_q).reshape(1, 512), (128, 512))),
        gk=np.ascontiguousarray(np.broadcast_to(f(g_k).reshape(1, 512), (128, 512))), w_pool=f(w_pool)[0], pool_scale=f(pool_scale).reshape(1, 512),
        w_out=f(w_out)[0], g_ffn=f(g_ffn_norm).reshape(1, 1024), w_up=f(w_up)[0], conv_w=f(conv_w)[0], conv_b=f(conv_b).reshape(1, 5632),
        w_down=f(w_down)[0], **consts)
    xp_ = f(x_prompt); xs_ = f(x_sample); ck_ = f(cache_k)[0]; cv_ = f(cache_v)[0]; sp_ = f(state_pool)[0]; sf_ = f(state_ffn_conv)[0]
    in_maps = []
    for i in range(8):
        d = dict(shared)
        d["xp"] = xp_[2 * i:2 * i + 2]
        d["xs"] = xs_[16 * i:16 * i + 16].reshape(64, 1024)
        d["ck"] = ck_[16 * i:16 * i + 16].reshape(16, 2048, 512)
        d["cv"] = cv_[16 * i:16 * i + 16].reshape(16, 2048, 512)
        d["spool"] = sp_[16 * i:16 * i + 16]
        d["sffn"] = sf_[16 * i:16 * i + 16]
        in_maps.append(d)
    res = run_bass_kernel_spmd(nc, in_maps, core_ids=list(range(8)))
    R = res.results
    cat = lambda k: np.concatenate([np.asarray(r[k], dtype=np.float32) for r in R], axis=0)
    y_p = cat("yp"); y_s = cat("ys").reshape(128, 4, 1024)
    nk_p = cat("nkp").reshape(1, 16, 2048, 8, 64); nv_p = cat("nvp").reshape(1, 16, 2048, 8, 64)
    np_p = cat("npp").reshape(1, 16, 15, 512); nf_p = cat("nfp").reshape(1, 16, 2, 5632)
    nk_s = cat("nks").reshape(1, 128, 4, 8, 64); nv_s = cat("nvs").reshape(1, 128, 4, 8, 64)
    np_s = cat("nps").reshape(1, 128, 15, 512); nf_s = cat("nfs").reshape(1, 128, 2, 5632)
    return (y_p, y_s, nk_p, nv_p, np_p, nf_p, nk_s, nv_s, np_s, nf_s)
```

```python
import os
import numpy as np
import ml_dtypes
from contextlib import ExitStack
import concourse.bass as bass
import concourse.mybir as mybir
from concourse.bass_utils import run_bass_kernel_spmd

F32 = mybir.dt.float32
BF16 = mybir.dt.bfloat16
AF = mybir.ActivationFunctionType
ALU = mybir.AluOpType
AX = mybir.AxisListType
EPS = 1e-6
NSLOT = 3
BF = ml_dtypes.bfloat16


class Buf:
    __slots__ = ('w', 'r')

    def __init__(s):
        s.w = None
        s.r = {}


class Group:
    def __init__(s, chan):
        s.chan = chan
        s.total = None


class Chan:
    def __init__(s, sem):
        s.sem = sem
        s.count = 0
        s.last = None


class Prog:
    ENG = ['pe', 'act', 'dve', 'pool', 'sp']

    def __init__(s, nc, es):
        s.nc = nc
        s.q = {e: [] for e in s.ENG}
        s.sem = {e: es.enter_context(nc.semaphore('sem_' + e)) for e in s.ENG if e != 'sp'}
        s.es = es
        s.chans = []

    def chan(s):
        c = Chan(s.es.enter_context(s.nc.semaphore('ch%d' % len(s.chans))))
        s.chans.append(c)
        return c

    def op(s, eng, fn, reads=(), writes=(), chan=None, group=None):
        q = s.q[eng]
        idx = len(q)
        deps = set()
        if chan is not None:
            if group is None:
                group = Group(chan)
            if chan.last is not group:
                if chan.last is not None:
                    deps.add(('dma', chan.last))
                chan.last = group
            chan.count += 16
            group.total = chan.count
            me = ('dma', group)
            mekey = ('dma', id(chan))
        else:
            me = (eng, idx)
            mekey = eng
        isdma = chan is not None
        for b in reads:
            if b.w is not None and b.w != me:
                if not (b.w[0] == 'pe' and eng == 'pe' and not isdma):
                    deps.add(b.w)
        for b in writes:
            if b.w is not None and b.w != me:
                if b.w[0] == 'dma' or isdma or b.w[0] != eng:
                    deps.add(b.w)
            for k, v in b.r.items():
                if v == me:
                    continue
                if v[0] == 'dma' or isdma or v[0] != eng:
                    deps.add(v)
        for b in reads:
            b.r[mekey] = me
        for b in writes:
            b.w = me
            b.r = {}
        q.append((fn, deps, chan))
        return group

    def inherit(s, dsts, srcs):
        for d in dsts:
            for b in srcs:
                if b.w is not None:
                    d.r[('w', id(b))] = b.w
                for k, v in b.r.items():
                    d.r[(k, id(b))] = v

    def finalize(s):
        sig = {e: set() for e in s.ENG}
        for e in s.ENG:
            for (fn, deps, chan) in s.q[e]:
                for d in deps:
                    if d[0] != 'dma':
                        sig[d[0]].add(d[1])
        s.rank = {}
        for e in s.ENG:
            for i, idx in enumerate(sorted(sig[e])):
                s.rank[(e, idx)] = i + 1

    def run(s, e, engobj):
        waited = {}
        for idx, (fn, deps, chan) in enumerate(s.q[e]):
            need = {}
            for d in deps:
                if d[0] == 'dma':
                    sm, val = d[1].chan.sem, d[1].total
                else:
                    sm, val = s.sem[d[0]], s.rank[d]
                k = id(sm)
                if need.get(k, (None, 0))[1] < val:
                    need[k] = (sm, val)
            for k, (sm, val) in need.items():
                if waited.get(k, 0) < val:
                    engobj.wait_ge(sm, val)
                    waited[k] = val
            ins = fn(engobj)
            if (e, idx) in s.rank:
                ins.then_inc(s.sem[e], 1)
            if chan is not None:
                ins.then_inc(chan.sem, 16)


class StopBuild(Exception):
    pass


def build(stop=None):
    def stage(name):
        if stop == name:
            raise StopBuild()
    nc = bass.Bass("TRN2", target_bir_lowering=False)

    def din(name, shape, dtype=F32):
        return nc.dram_tensor(name, shape, dtype, kind="ExternalInput").ap()

    def dout(name, shape):
        return nc.dram_tensor(name, shape, F32, kind="ExternalOutput").ap()

    def dscr(name, shape):
        return nc.dram_tensor(name, shape, BF16, kind="Internal").ap()

    xp = din("xp", [2, 2048, 1024]); xs = din("xs", [64, 1024])
    ck = din("ck", [16, 2048, 512]); cv = din("cv", [16, 2048, 512])
    spool = din("spool", [16, 15, 512]); sffn = din("sffn", [16, 2, 5632])
    g_attn = din("g_attn", [1, 1024]); w_in = din("w_in", [1024, 2048])
    gq_in = din("gq", [128, 512]); gk_in = din("gk", [128, 512])
    w_pool = din("w_pool", [4, 128, 128]); pool_scale = din("pool_scale", [1, 512])
    w_out = din("w_out", [1024, 1024]); g_ffn = din("g_ffn", [1, 1024])
    w_up = din("w_up", [1024, 5632]); conv_w = din("conv_w", [3, 5632]); conv_b = din("conv_b", [1, 5632])
    w_down = din("w_down", [2816, 1024])
    c_id = din("c_id", [128, 128], BF16); c_mcur = din("c_mcur", [128, 512], BF16)
    c_mprev = din("c_mprev", [128, 512], BF16); c_m16 = din("c_m16", [128, 4, 512], BF16)
    c_kaug = din("c_kaug", [4, 8, 2048], BF16); c_qaug = din("c_qaug", [4, 8, 2048], BF16)
    c_invcnt = din("c_invcnt", [128, 4, 16]); c_wt = din("c_wt", [128, 224]); c_wtn = din("c_wtn", [64, 16, 32])

    yp = dout("yp", [2, 2048, 1024]); ys = dout("ys", [64, 1024])
    nkp = dout("nkp", [2, 2048, 512]); nvp = dout("nvp", [2, 2048, 512])
    npp = dout("npp", [2, 15, 512]); nfp = dout("nfp", [2, 2, 5632])
    nks = dout("nks", [64, 512]); nvs = dout("nvs", [64, 512])
    nps = dout("nps", [16, 15, 512]); nfs = dout("nfs", [16, 2, 5632])

    win_s = dscr("win_s", [4, 128, 8, 512]); woa_s = dscr("woa_s", [2, 64, 8, 512]); wop_s = dscr("wop_s", [2, 128, 4, 512])
    wup_s = dscr("wup_s", [11, 128, 8, 512]); wdn_s = dscr("wdn_s", [2, 3, 128, 8, 512])

    with ExitStack() as es:
        def sb(name, shape, dtype):
            return es.enter_context(nc.sbuf_tensor(name, shape, dtype))

        P = Prog(nc, es)
        xh = sb("xh", [128, 4, 1024], F32); b_xhs = [Buf() for _ in range(4)]; b_xh = b_xhs[0]
        xn = sb("xn", [128, 4, 1024], BF16); b_xns = [Buf() for _ in range(4)]
        actT = sb("actT", [128, 8, 512], BF16); b_actT = Buf()
        ss = sb("ss", [128, 16], F32); b_ss = Buf()
        ssqs = [sb("ssq", [128, 32], F32), sb("ssqb", [128, 32], F32)]; b_ssqs = [Buf(), Buf()]; sqc = [0]
        kT = sb("kT", [128, 8, 2048], BF16); b_kT = Buf()
        Vn = sb("Vn", [128, 5, 512], BF16); b_Vn = [Buf() for _ in range(5)]
        V4 = sb("V4", [128, 8, 512], BF16); b_V4 = Buf()
        V16 = sb("V16", [128, 16, 512], BF16); b_V16 = Buf()
        uT = sb("uT", [128, 4, 528], F32); b_uT = [Buf() for _ in range(4)]
        attnT = sb("attnT", [128, 8, 512], BF16); b_attnT = [Buf() for _ in range(8)]
        poolT = sb("poolT", [128, 4, 512], BF16); b_poolT = Buf()
        ring = sb("ring", [128, NSLOT, 4096], BF16); b_ring = [Buf() for _ in range(NSLOT)]
        identb = sb("identb", [128, 128], BF16); mcur = sb("mcur", [128, 512], BF16); mprev = sb("mprev", [128, 512], BF16)
        m16 = sb("m16", [128, 4, 512], BF16); onesb = sb("onesb", [128, 64], BF16)
        g1 = sb("g1", [128, 8], F32); g2 = sb("g2", [128, 8], F32)
        gqb = sb("gqb", [128, 512], F32); gkb = sb("gkb", [128, 512], F32); negM = sb("negM", [128, 4], F32)
        cpg = sb("cpg", [128, 22, 4], F32); cpv = sb("cpv", [128, 22, 4], F32)
        hist_g = sb("hist_g", [128, 22, 32], F32); hist_v = sb("hist_v", [128, 22, 32], F32)
        b_hist = Buf()
        hist2_g = sb("hist2_g", [128, 22, 2], F32); hist2_v = sb("hist2_v", [128, 22, 2], F32)
        pscale = sb("pscale", [128, 4], F32); wpool = sb("wpool", [128, 4, 128], BF16); invcnt = sb("invcnt", [128, 4, 16], F32)
        b_const = Buf()
        arena = sb("arena", [128, 21696], BF16)
        junk = sb("junk", [128, 1024], BF16); b_junk = Buf()
        off = [0]

        def carve(nbf16, reset=False):
            if reset:
                off[0] = 0
            a = arena[:, off[0]:off[0] + nbf16]
            off[0] += nbf16
            assert off[0] <= 21696
            return a
        qT = carve(8 * 512, True).rearrange("p (h t) -> p h t", h=8); b_qT = Buf()
        tmpn = carve(1024).bitcast(F32); b_tmpn = Buf()
        ksts = [carve(1024).bitcast(F32), carve(1024).bitcast(F32)]; b_ksts = [Buf(), Buf()]
        vsts = [carve(1024).bitcast(F32), carve(1024).bitcast(F32)]; b_vsts = [Buf(), Buf()]
        kvc = [0]
        qn_bfs = [carve(512), carve(512)]; b_qns = [Buf(), Buf()]
        kn_bfs = [carve(512), carve(512)]; b_kns = [Buf(), Buf()]
        nbc = [0]
        PT = [carve(512) for _ in range(3)]; b_PT = [Buf() for _ in range(3)]
        dT = carve(4 * 512).rearrange("p (g t) -> p g t", g=4); b_dT = Buf()
        sa = carve(1056).bitcast(F32); sbb = carve(1056).bitcast(F32); b_s = Buf()
        rlb = carve(1024).bitcast(F32); b_rlb = Buf()
        rlb2 = carve(1024).bitcast(F32); b_rlb2 = Buf()
        rlbs = [rlb, rlb2]; b_rlbs = [b_rlb, b_rlb2]
        tmp16 = carve(32).bitcast(F32)
        A_bufs = [b_qT, b_tmpn, b_dT, b_s, b_rlb, b_rlb2] + b_PT + b_qns + b_kns + b_ksts + b_vsts
        gT = carve(22 * 512, True).rearrange("p (k t) -> p k t", k=22); b_gT = Buf()
        ext = [[carve(1056).bitcast(F32) for _ in range(2)] for _ in range(2)]; b_ext = [[Buf(), Buf()], [Buf(), Buf()]]
        acc = [[carve(1024).bitcast(F32)], [carve(1024).bitcast(F32) for _ in range(2)]]; b_acc = [[Buf()], [Buf(), Buf()]]
        sg = [carve(1024).bitcast(F32) for _ in range(2)]; b_sg = [Buf(), Buf()]
        b_histd = {(k_, a_, p_): Buf() for k_ in 'gv' for a_ in range(22) for p_ in range(2)}
        b_hist_all = list(b_histd.values())
        ccnt = {'g': 0, 'v': 0}
        pend_silu = [None]
        pend_evac = [None]
        pend_tr = [None]
        upst = carve(1024).bitcast(F32); b_upst = Buf()
        F_bufs = [b_gT, b_upst] + b_sg + b_ext[0] + b_ext[1] + b_acc[0] + b_acc[1]
        kTflat = kT[:].rearrange("p h t -> p (h t)")
        Kc = [kTflat[:, 0:3584].rearrange("p (a c) -> p a c", a=7)] * 2; b_Kc = [Buf()] * 2
        Vc = [kTflat[:, 3584:7168].rearrange("p (a c) -> p a c", a=7)] * 2; b_Vc = [Buf()] * 2
        pbank = [es.enter_context(nc.psum_tensor("pb%d" % i, [128, 512], F32)) for i in range(8)]
        pA = [pbank[0], pbank[1]]; b_pA = [Buf(), Buf()]
        pTb = [pbank[2][:].bitcast(BF16), pbank[7][:].bitcast(BF16)]; b_pT = [Buf(), Buf()]
        pS = [pbank[3], pbank[4]]; b_pS = [Buf(), Buf()]
        pO = pbank[5]; b_pO = Buf()
        pL = pbank[6]; b_pL = Buf()

        def Aop(fn, r, w): P.op('act', fn, r, w)
        def Vop(fn, r, w): P.op('dve', fn, r, w)
        def Gop(fn, r, w): P.op('pool', fn, r, w)
        def Top(fn, r, w): P.op('pe', fn, r, w)
        def Dop(fn, r, w, ch, group=None, q='sp'): return P.op(q, fn, r, w, chan=ch, group=group)

        ch_c = [P.chan() for _ in range(4)]
        ch_xs = [P.chan() for _ in range(4)]; ch_ys = [P.chan() for _ in range(4)]; ch_k = P.chan(); ch_vo = P.chan(); ch_v = P.chan(); ch_q = P.chan()
        ch_w = [P.chan() for _ in range(NSLOT)]
        ch_pro = [P.chan() for _ in range(14)]
        ch_ol = [P.chan() for _ in range(4)]; ch_oc = [0]; ch_s = [P.chan(), P.chan()]

        def cho():
            ch_oc[0] += 1
            return ch_ol[ch_oc[0] % 4]

        cc = [0]

        def cload(out, in_, slow=False, q='sp'):
            c = ch_c[cc[0] % 4]; cc[0] += 1
            if slow:
                Dop(lambda e: e.dma_start(out=out, in_=in_, allow_slow_non_contiguous=True), [], [b_const], c, q=q)
            else:
                Dop(lambda e: e.dma_start(out=out, in_=in_), [], [b_const], c, q=q)
        cload(identb[:], c_id); cload(mcur[:], c_mcur); cload(mprev[:], c_mprev); cload(m16[:], c_m16)
        cload(gqb[:], gq_in); cload(gkb[:], gk_in); cload(invcnt[:], c_invcnt)
        cload(g1[:], g_attn[0, :].rearrange("(c p) -> p c", p=128), slow=True)
        cload(g2[:], g_ffn[0, :].rearrange("(c p) -> p c", p=128), slow=True)
        cload(pscale[:], pool_scale[0, :].rearrange("(g p) -> p g", p=128), slow=True)
        for j in range(3):
            cload(cpg[:, :, j], conv_w[j, 0:2816].rearrange("(a p) -> p a", p=128), slow=True)
            cload(cpv[:, :, j], conv_w[j, 2816:5632].rearrange("(a p) -> p a", p=128), slow=True)
        cload(cpg[:, :, 3], conv_b[0, 0:2816].rearrange("(a p) -> p a", p=128), slow=True)
        cload(cpv[:, :, 3], conv_b[0, 2816:5632].rearrange("(a p) -> p a", p=128), slow=True)
        cload(wpool[:], w_pool.rearrange("g c d -> c g d"), q='pool')
        Dop(lambda e: e.dma_start(out=kT[64:68, :, :], in_=c_kaug), [], [b_kT], ch_c[0])
        Vop(lambda e: e.tensor_scalar(out=g1[:], in0=g1[:], scalar1=32.0, scalar2=None, op0=ALU.mult), [b_const], [b_const])
        Vop(lambda e: e.tensor_scalar(out=g2[:], in0=g2[:], scalar1=32.0, scalar2=None, op0=ALU.mult), [b_const], [b_const])
        Vop(lambda e: e.memset(onesb[:], 1.0), [], [b_const])
        Vop(lambda e: e.tensor_tensor(out=xh[:, 0, 0:512], in0=gqb[:], in1=gqb[:], op=ALU.mult), [b_const], [b_xh])
        Vop(lambda e: e.reduce_max(out=negM[:, 1:2], in_=xh[:, 0, 0:512], axis=AX.X), [b_xh], [b_const])
        Vop(lambda e: e.tensor_tensor(out=xh[:, 0, 0:512], in0=gkb[:], in1=gkb[:], op=ALU.mult), [b_const], [b_xh])
        Vop(lambda e: e.reduce_max(out=negM[:, 2:3], in_=xh[:, 0, 0:512], axis=AX.X), [b_xh], [b_const])
        Vop(lambda e: e.tensor_tensor(out=negM[:, 3:4], in0=negM[:, 1:2], in1=negM[:, 2:3], op=ALU.add), [b_const], [b_const])
        Vop(lambda e: e.tensor_scalar(out=negM[:, 0:1], in0=negM[:, 3:4], scalar1=-4.0, scalar2=None, op0=ALU.mult), [b_const], [b_const])
        Vop(lambda e: e.tensor_scalar(out=gkb[:], in0=gkb[:], scalar1=8.0, scalar2=None, op0=ALU.mult), [b_const], [b_const])

        b_scr = {}
        pc_ = [0]

        def pro(key, out, in_):
            c = ch_pro[pc_[0] % 14]; pc_[0] += 1
            b = b_scr.setdefault(key, Buf())
            Dop(lambda e: e.dma_start(out=out, in_=in_), [], [b], c, q='pool')
        for g in range(4):
            pro(('in', g), win_s[g], w_in[:, g * 512:(g + 1) * 512].rearrange("(c p) n -> p c n", p=128))
        for f in range(2):
            pro(('oa', f), woa_s[f], w_out[0:512, f * 512:(f + 1) * 512].rearrange("(h p) n -> p h n", p=64))
            pro(('op', f), wop_s[f], w_out[512:1024, f * 512:(f + 1) * 512].rearrange("(g p) n -> p g n", p=128))
        def late_pro():
            for i in range(11):
                pro(('up', i), wup_s[i, :, :, 0:256], w_up[:, 256 * i:256 * i + 256].rearrange("(c p) n -> p c n", p=128))
                pro(('up', i), wup_s[i, :, :, 256:512], w_up[:, 2816 + 256 * i:2816 + 256 * i + 256].rearrange("(c p) n -> p c n", p=128))
            for f in range(2):
                for pc in range(3):
                    nk = 8 if pc < 2 else 6
                    pro(('dn', f, pc), wdn_s[f, pc, :, 0:nk, :],
                        w_down[pc * 1024:pc * 1024 + nk * 128, f * 512:(f + 1) * 512].rearrange("(k p) n -> p k n", p=128))

        piece_list = []
        NT_TILES = 9
        for t in range(NT_TILES):
            for g in (2, 0, 1, 3):
                piece_list.append((('in', g), win_s[g], 128, 4096))
            for f in range(2):
                piece_list.append((('oa', f), woa_s[f], 64, 4096))
                piece_list.append((('op', f), wop_s[f], 128, 2048))
            for i in range(11):
                piece_list.append((('up', i), wup_s[i], 128, 4096))
            for f in range(2):
                for pc in range(3):
                    piece_list.append((('dn', f, pc), wdn_s[f, pc], 128, 4096))
        issued = [0]

        def wget(i, hold=0):
            while issued[0] < min(len(piece_list), i - hold + NSLOT):
                k = issued[0]; issued[0] += 1
                key, src, npart, nel = piece_list[k]
                sl = k % NSLOT
                o = ring[0:npart, sl, 0:nel]
                s2 = src.rearrange("p a n -> p (a n)")
                Dop(lambda e, o=o, s2=s2: e.dma_start(out=o, in_=s2), [b_scr[key]], [b_ring[sl]], ch_w[sl])
            return i % NSLOT, b_ring[i % NSLOT]

        def slotv(sl, a):
            return ring[:, sl, 0:a * 512].rearrange("p (a n) -> p a n", n=512)

        pac = [0]

        rot6 = [(pA[0], b_pA[0]), (pA[1], b_pA[1]), (pS[0], b_pS[0]), (pS[1], b_pS[1]), (pO, b_pO), (pL, b_pL)]

        def nextpa():
            i = pac[0] % 6; pac[0] += 1
            return rot6[i]
        ptc = [0]

        def nextpt():
            i = ptc[0] % 2; ptc[0] += 1
            return pTb[i], b_pT[i]

        def do_tile(tidx, smp, s, m):
            NT = 64 if smp else 512; nsub = 1 if smp else 4; PP = 64 if smp else 128
            nseg = 16 if smp else 1; L = 4 if smp else 512; E = 16 + L
            T0 = 0 if smp else 512 * m
            base = tidx * 25
            last = smp or m == 3
            P.inherit(A_bufs, F_bufs)
            vgrp[0] = None
            if smp:
                xsrc = xs.rearrange("(j p) d -> p j d", p=64)
            else:
                xsrc = xp[s, T0:T0 + 512, :].rearrange("(j p) d -> p j d", p=128)
            for j in range(nsub):
                Dop(lambda e, j=j: e.dma_start(out=xh[0:PP, j, :], in_=xsrc[:, j, :]), [], [b_xhs[j]], ch_xs[j])
            if not smp:
                Dop(lambda e: e.dma_start(out=qT[64:68, :, :], in_=c_qaug[:, :, T0:T0 + 512]), [], [b_qT], ch_q)

            def norm_T(g):
                Vop(lambda e: e.memset(ss[:, 0:4], 0.0), [], [b_ss])
                for j in range(nsub):
                    Aop(lambda e, j=j: e.activation(out=junk[0:PP, :], in_=xh[0:PP, j, :], func=AF.Square, accum_out=ss[0:PP, j:j + 1]),
                        [b_xhs[j], b_ss], [b_junk, b_ss])
                Vop(lambda e: e.tensor_scalar(out=ss[0:PP, 4:8], in0=ss[0:PP, 0:4], scalar1=1024 * EPS, scalar2=None, op0=ALU.add), [b_ss], [b_ss])
                Aop(lambda e: e.activation(out=ss[0:PP, 8:12], in_=ss[0:PP, 4:8], func=AF.Ln), [b_ss], [b_ss])
                Aop(lambda e: e.activation(out=ss[0:PP, 12:16], in_=ss[0:PP, 8:12], func=AF.Exp, scale=-0.5), [b_ss], [b_ss])
                for j in range(nsub):
                    if nsub == 4 and j >= 2:
                        Vop(lambda e, j=j: e.tensor_scalar(out=xn[0:PP, j, :], in0=xh[0:PP, j, :], scalar1=ss[0:PP, 12 + j:13 + j], scalar2=None, op0=ALU.mult),
                            [b_xhs[j], b_ss], [b_xns[j]])
                    else:
                        Aop(lambda e, j=j: e.activation(out=xn[0:PP, j, :], in_=xh[0:PP, j, :], func=AF.Copy, scale=ss[0:PP, 12 + j:13 + j]),
                            [b_xhs[j], b_ss], [b_xns[j]])
                for c in range(8):
                    pt, bpt = nextpt()
                    for j in range(nsub):
                        Top(lambda e, pt=pt, j=j, c=c: e.transpose(out=pt[:, j * PP:(j + 1) * PP], in_=xn[0:PP, j, c * 128:(c + 1) * 128],
                                                                  identity=identb[0:PP, 0:PP]), [b_xns[j], b_const], [bpt])
                    Vop(lambda e, pt=pt, c=c: e.tensor_scalar(out=actT[:, c, 0:NT], in0=pt[:, 0:NT], scalar1=g[:, c:c + 1], scalar2=None, op0=ALU.mult),
                        [bpt, b_const], [b_actT])
            norm_T(g1)
            stage('norm1')

            for pos_, gi in enumerate((2, 0, 1)):
                sl, bsl = wget(base + pos_)
                sv = slotv(sl, 8)
                for j in range(nsub):
                    pa, bpa = nextpa()
                    for c in range(8):
                        Top(lambda e, pa=pa, j=j, c=c, sv=sv: e.matmul(pa[0:PP, :], lhsT=actT[:, c, j * PP:(j + 1) * PP], rhs=sv[:, c, :],
                                                                      start=(c == 0), stop=(c == 7)), [b_actT, bsl], [bpa])
                    if pend_tr[0] is not None:
                        pend_tr[0](); pend_tr[0] = None
                    rows = slice(T0 + j * 128, T0 + j * 128 + 128)
                    if gi < 2:
                        ssq = ssqs[sqc[0] % 2]; b_ssq = b_ssqs[sqc[0] % 2]; sqc[0] += 1
                        Vop(lambda e, ssq=ssq: e.memset(ssq[:, 0:8], 0.0), [], [b_ssq])
                        for h in range(8):
                            Aop(lambda e, pa=pa, h=h, ssq=ssq: e.activation(out=junk[0:PP, 0:64], in_=pa[0:PP, h * 64:(h + 1) * 64], func=AF.Square,
                                                                   accum_out=ssq[0:PP, h:h + 1]), [bpa, b_ssq], [b_junk, b_ssq])
                        Vop(lambda e, ssq=ssq: e.tensor_scalar(out=ssq[0:PP, 8:16], in0=ssq[0:PP, 0:8], scalar1=64 * EPS, scalar2=None, op0=ALU.add), [b_ssq], [b_ssq])
                        Aop(lambda e, ssq=ssq: e.activation(out=ssq[0:PP, 16:24], in_=ssq[0:PP, 8:16], func=AF.Ln), [b_ssq], [b_ssq])
                        Aop(lambda e, ssq=ssq: e.activation(out=ssq[0:PP, 24:32], in_=ssq[0:PP, 16:24], func=AF.Exp, scale=-0.5), [b_ssq], [b_ssq])
                        if pend_evac[0] is not None:
                            pend_evac[0](); pend_evac[0] = None
                        for h in range(8):
                            Vop(lambda e, pa=pa, h=h, ssq=ssq: e.tensor_scalar(out=tmpn[0:PP, h * 64:(h + 1) * 64], in0=pa[0:PP, h * 64:(h + 1) * 64],
                                                                      scalar1=ssq[0:PP, 24 + h:25 + h], scalar2=None, op0=ALU.mult), [bpa, b_ssq], [b_tmpn])
                        nbi = nbc[0] % 2; nbc[0] += 1
                        if gi == 0:
                            nb, bnb = qn_bfs[nbi], b_qns[nbi]
                            Vop(lambda e, nb=nb: e.tensor_tensor(out=nb[0:PP, :], in0=tmpn[0:PP, :], in1=gqb[0:PP, :], op=ALU.mult), [b_tmpn, b_const], [bnb])
                        else:
                            nb, bnb = kn_bfs[nbi], b_kns[nbi]
                            kst = ksts[nbi]; b_kst = b_ksts[nbi]
                            Vop(lambda e, kst=kst: e.tensor_tensor(out=kst[0:PP, :], in0=tmpn[0:PP, :], in1=gkb[0:PP, :], op=ALU.mult), [b_tmpn, b_const], [b_kst])
                            Vop(lambda e, nb=nb, kst=kst: e.tensor_copy(out=nb[0:PP, :], in_=kst[0:PP, :]), [b_kst], [bnb])
                            dst = nks if smp else nkp[s, rows, :]
                            Dop(lambda e, dst=dst, kst=kst: e.dma_start(out=dst, in_=kst[0:PP, :]), [b_kst], [], ch_k)
                        if not smp:
                            def tr(nb=nb, bnb=bnb, gi=gi, j=j, rows=rows):
                                pt, bpt = nextpt()
                                for h in range(8):
                                    Top(lambda e, h=h: e.transpose(out=pt[0:64, h * 128:(h + 1) * 128], in_=nb[:, h * 64:(h + 1) * 64], identity=identb[:]),
                                        [bnb, b_const], [bpt])
                                src = pt[0:64, :].rearrange("p (h t) -> p h t", t=128)
                                if gi == 0:
                                    pend_evac[0] = (lambda: Aop(lambda e: e.activation(out=qT[0:64, :, j * 128:(j + 1) * 128], in_=src, func=AF.Copy), [bpt], [b_qT]))
                                else:
                                    pend_evac[0] = (lambda: Aop(lambda e: e.activation(out=kT[0:64, :, rows], in_=src, func=AF.Copy), [bpt], [b_kT]))
                            pend_tr[0] = tr
                        else:
                            pt, bpt = nextpt()
                            for hp in range(4):
                                Top(lambda e, pt=pt, hp=hp, nb=nb: e.transpose(out=pt[:, hp * 64:(hp + 1) * 64], in_=nb[0:64, hp * 128:(hp + 1) * 128],
                                                                              identity=identb[0:64, 0:64]), [bnb, b_const], [bpt])
                            if gi == 0:
                                for hh in range(2):
                                    Aop(lambda e, pt=pt, hh=hh: e.activation(
                                        out=qbd[hh * 64:(hh + 1) * 64, :, :, hh * 4:hh * 4 + 4],
                                        in_=pt[hh * 64:(hh + 1) * 64, 0:256].rearrange("p (a b t) -> p a b t", a=4, b=16), func=AF.Copy), [bpt], [b_qbd])
                            else:
                                Aop(lambda e, pt=pt: e.activation(out=kTp[:, :, :], in_=pt[:, 0:256].rearrange("p (a t) -> p a t", a=4), func=AF.Copy), [bpt], [b_kTp])
                    else:
                        B = (4 * m + j) if not smp else 0
                        vs = B % 5
                        vst = vsts[kvc[0] % 2]; b_vst = b_vsts[kvc[0] % 2]; kvc[0] += 1
                        Aop(lambda e, pa=pa, vst=vst: e.activation(out=vst[0:PP, :], in_=pa[0:PP, :], func=AF.Copy), [bpa], [b_vst])
                        dst = nvs if smp else nvp[s, rows, :]
                        Dop(lambda e, dst=dst, vst=vst: e.dma_start(out=dst, in_=vst[0:PP, :]), [b_vst], [], ch_vo)
                        Vop(lambda e, vs=vs, vst=vst: e.tensor_copy(out=Vn[0:PP, vs, :], in_=vst[0:PP, :]), [b_vst], [b_Vn[vs]])
                        if not smp and os.environ.get('NOVDMA') is None:
                            grp = vgrp[0]
                            for c4 in range(4):
                                grp = Dop(lambda e, c4=c4, vs=vs, j=j: e.dma_start(out=V4[32 * j:32 * j + 32, (m % 2) * 4 + c4, :], in_=Vn[c4:128:4, vs, :]),
                                          [b_Vn[vs]], [b_V4], ch_v, group=grp, q='pool')
                            for c16 in range(16):
                                grp = Dop(lambda e, c16=c16, vs=vs, j=j: e.dma_start(out=V16[32 * m + 8 * j:32 * m + 8 * j + 8, c16, :], in_=Vn[c16:128:16, vs, :]),
                                          [b_Vn[vs]], [b_V16], ch_v, group=grp, q='pool')
                            vgrp[0] = grp
            if pend_tr[0] is not None:
                pend_tr[0](); pend_tr[0] = None
            if pend_evac[0] is not None:
                pend_evac[0](); pend_evac[0] = None
            stage('qkv')
            sl, bsl = wget(base + 3)
            sv = slotv(sl, 8)
            for g in range(4):
                pa, bpa = nextpa()
                for c in range(8):
                    Top(lambda e, pa=pa, g=g, c=c, sv=sv: e.matmul(pa[:, 0:NT], lhsT=sv[:, c, g * 128:(g + 1) * 128], rhs=actT[:, c, 0:NT],
                                                                  start=(c == 0), stop=(c == 7)), [b_actT, bsl], [bpa])
                ue = uT[:, g, 0:nseg * E].rearrange("p (s e) -> p s e", e=E)
                Aop(lambda e, pa=pa, ue=ue: e.activation(out=ue[:, :, 16:E], in_=pa[:, 0:NT].rearrange("p (s l) -> p s l", l=L), func=AF.Copy), [bpa], [b_uT[g]])
            if last:
                pa, bpa = nextpa()
                for c in range(8):
                    Top(lambda e, pa=pa, c=c, sv=sv: e.matmul(pa[0:PP, :], lhsT=actT[:, c, NT - PP:NT], rhs=sv[:, c, :], start=(c == 0), stop=(c == 7)),
                        [b_actT, bsl], [bpa])
                Aop(lambda e, pa=pa: e.activation(out=tmpn[0:PP, :], in_=pa[0:PP, :], func=AF.Copy), [bpa], [b_tmpn])
                if smp:
                    for t4 in range(4):
                        Dop(lambda e, t4=t4: e.dma_start(out=nps[:, 11 + t4, :], in_=tmpn[t4:64:4, :]), [b_tmpn], [], cho())
                else:
                    Dop(lambda e: e.dma_start(out=npp[s, :, :], in_=tmpn[113:128, :]), [b_tmpn], [], cho())
            stage('u')
            for g in range(4):
                w = 2 << g
                ue = uT[:, g, 0:nseg * E].rearrange("p (s e) -> p s e", e=E)
                s1 = sa[:, 0:nseg * E].rearrange("p (s e) -> p s e", e=E)
                s2 = sbb[:, 0:nseg * E].rearrange("p (s e) -> p s e", e=E)
                Gop(lambda e, ue=ue, s1=s1: e.tensor_tensor(out=s1[:, :, 1:E], in0=ue[:, :, 1:E], in1=ue[:, :, 0:E - 1], op=ALU.add), [b_uT[g]], [b_s])
                fin = s1
                if w >= 4:
                    Gop(lambda e, s1=s1, s2=s2: e.tensor_tensor(out=s2[:, :, 3:E], in0=s1[:, :, 3:E], in1=s1[:, :, 1:E - 2], op=ALU.add), [b_s], [b_s]); fin = s2
                if w >= 8:
                    Gop(lambda e, s1=s1, s2=s2: e.tensor_tensor(out=s1[:, :, 7:E], in0=s2[:, :, 7:E], in1=s2[:, :, 3:E - 4], op=ALU.add), [b_s], [b_s]); fin = s1
                if w >= 16:
                    Gop(lambda e, s1=s1, s2=s2: e.tensor_tensor(out=s2[:, :, 15:E], in0=s1[:, :, 15:E], in1=s1[:, :, 7:E - 8], op=ALU.add), [b_s], [b_s]); fin = s2
                dv = dT[:, g, 0:NT].rearrange("p (s l) -> p s l", l=L)
                Vop(lambda e, fin=fin, ue=ue, dv=dv, w=w: e.scalar_tensor_tensor(out=dv, in0=fin[:, :, 16:E], scalar=1.0 / w, in1=ue[:, :, 16:E],
                                                                                 op0=ALU.mult, op1=ALU.subtract), [b_s, b_uT[g]], [b_dT])
                if (not smp) and m == 0:
                    Gop(lambda e, fin=fin, g=g: e.tensor_tensor(out=tmp16[:, 0:16], in0=fin[:, 0, 16:32], in1=invcnt[:, g, :], op=ALU.mult), [b_s, b_const], [b_s])
                    Gop(lambda e, ue=ue, g=g: e.tensor_tensor(out=dT[:, g, 0:16], in0=tmp16[:, 0:16], in1=ue[:, 0, 16:32], op=ALU.subtract), [b_s, b_uT[g]], [b_dT])
                pa, bpa = nextpa()
                Top(lambda e, pa=pa, g=g: e.matmul(pa[:, 0:NT], lhsT=wpool[:, g, :], rhs=dT[:, g, 0:NT], start=True, stop=True), [b_dT, b_const], [bpa])
                Vop(lambda e, pa=pa, g=g: e.tensor_scalar(out=poolT[:, g, 0:NT], in0=pa[:, 0:NT], scalar1=pscale[:, g:g + 1], scalar2=None, op0=ALU.mult),
                    [bpa, b_const], [b_poolT])
                if not smp:
                    Gop(lambda e, g=g: e.tensor_copy(out=uT[:, g, 0:16], in_=uT[:, g, 512:528]), [b_uT[g]], [b_uT[g]])

            if tidx == 0:
                late_pro()
            stage('pool')
            if not smp:
                glist = []
                for h in range(8):
                    kinds = [k_ for k_ in ('1c', '1p', '4c', '4p', '16') if not (k_ == '4p' and m == 0)]
                    for ki_, kind in enumerate(kinds):
                        glist.append((h, kind, ki_ == 0, ki_ == len(kinds) - 1))

                def g_params(gi_):
                    h, kind, first, lastk = glist[gi_]
                    ps, bps = pS[gi_ % 2], b_pS[gi_ % 2]
                    pt_, bpt_ = PT[gi_ % 3], b_PT[gi_ % 3]
                    R = 128; c0 = 0
                    if kind == '16':
                        R = 32 * (m + 1); mask = m16[:, m, :]
                    else:
                        mask = mcur if kind in ('1c', '4c') else mprev
                        if kind == '1p' and m == 0:
                            c0 = 128
                    return h, kind, first, lastk, ps, bps, pt_, bpt_, R, c0, mask

                def emit_scores(gi_):
                    h, kind, first, lastk, ps, bps, pt_, bpt_, R, c0, mask = g_params(gi_)
                    Top(lambda e: e.matmul(ps[0:R, c0:512], lhsT=identb[0:R, 0:R], rhs=mask[0:R, c0:512], start=True, stop=False), [b_const], [bps])
                    if kind in ('1c', '1p'):
                        for n in range(c0 // 128, 4):
                            kb = T0 + n * 128 - (128 if kind == '1p' else 0)
                            Top(lambda e, n=n, kb=kb: e.matmul(ps[:, n * 128:(n + 1) * 128], lhsT=kT[0:68, h, kb:kb + 128],
                                                              rhs=qT[0:68, h, n * 128:(n + 1) * 128], start=False, stop=True, skip_group_check=True), [b_kT, b_qT], [bps])
                    elif kind in ('4c', '4p'):
                        for c4 in range(4):
                            kb = T0 + c4 - (512 if kind == '4p' else 0)
                            Top(lambda e, c4=c4, kb=kb: e.matmul(ps[:, c4 * 128:(c4 + 1) * 128], lhsT=kT[0:68, h, kb:kb + 509:4],
                                                                rhs=qT[0:68, h, c4:512:4], start=False, stop=True, skip_group_check=True), [b_kT, b_qT], [bps])
                    else:
                        for c16 in range(16):
                            Top(lambda e, c16=c16: e.matmul(ps[0:R, c16 * 32:(c16 + 1) * 32], lhsT=kT[0:68, h, c16:T0 + 512:16],
                                                           rhs=qT[0:68, h, c16:512:16], start=False, stop=True, skip_group_check=True), [b_kT, b_qT], [bps])
                    Aop(lambda e: e.activation(out=pt_[0:R, c0:512], in_=ps[0:R, c0:512], func=AF.Exp, bias=negM[0:R, 0:1], scale=1.0), [bps, b_const], [bpt_])

                def emit_pv(gi_):
                    h, kind, first, lastk, ps, bps, pt_, bpt_, R, c0, mask = g_params(gi_)
                    (pO_, b_pO_), (pL_, b_pL_) = ((pO, b_pO), (pL, b_pL)) if h % 2 == 0 else ((pA[0], b_pA[0]), (pA[1], b_pA[1]))
                    rlb_, b_rlb_ = rlbs[h % 2], b_rlbs[h % 2]
                    if first:
                        Vop(lambda e: e.memset(pO_[0:64, :], 0.0), [], [b_pO_])
                        Vop(lambda e: e.memset(pL_[0:64, :], 0.0), [], [b_pL_])
                    hs = slice(h * 64, (h + 1) * 64)
                    kw = dict(start=False, stop=False, skip_group_check=True)
                    if kind in ('1c', '1p'):
                        for n in range(c0 // 128, 4):
                            vs = (4 * m + n - (1 if kind == '1p' else 0)) % 5
                            Top(lambda e, n=n, vs=vs: e.matmul(pO_[0:64, n * 128:(n + 1) * 128], lhsT=Vn[:, vs, hs], rhs=pt_[:, n * 128:(n + 1) * 128], **kw),
                                [b_Vn[vs], bpt_], [b_pO_])
                        Top(lambda e: e.matmul(pL_[0:64, c0:512], lhsT=onesb[:, 0:64], rhs=pt_[:, c0:512], **kw), [b_const, bpt_], [b_pL_])
                    elif kind in ('4c', '4p'):
                        for c4 in range(4):
                            vsl = ((m if kind == '4c' else m - 1) % 2) * 4 + c4
                            Top(lambda e, c4=c4, vsl=vsl: e.matmul(pO_[0:64, c4:512:4], lhsT=V4[:, vsl, hs], rhs=pt_[:, c4 * 128:(c4 + 1) * 128], **kw), [b_V4, bpt_], [b_pO_])
                        Top(lambda e: e.matmul(pL_[0:64, :].rearrange("p (i c) -> p c i", c=4), lhsT=onesb[:, 0:64], rhs=pt_[:, 0:512].rearrange("p (c i) -> p c i", c=4), **kw),
                            [b_const, bpt_], [b_pL_])
                    else:
                        for c16 in range(16):
                            Top(lambda e, c16=c16: e.matmul(pO_[0:64, c16:512:16], lhsT=V16[0:R, c16, hs], rhs=pt_[0:R, c16 * 32:(c16 + 1) * 32], **kw), [b_V16, bpt_], [b_pO_])
                        Top(lambda e: e.matmul(pL_[0:64, :].rearrange("p (i c) -> p c i", c=16), lhsT=onesb[0:R, 0:64], rhs=pt_[0:R, 0:512].rearrange("p (c i) -> p c i", c=16), **kw),
                            [b_const, bpt_], [b_pL_])
                    if lastk:
                        Aop(lambda e: e.activation(out=rlb_[0:64, :], in_=pL_[0:64, :], func=AF.Ln), [b_pL_], [b_rlb_])
                        Aop(lambda e: e.activation(out=rlb_[0:64, :], in_=rlb_[0:64, :], func=AF.Exp, scale=-1.0), [b_rlb_], [b_rlb_])
                        Vop(lambda e: e.tensor_tensor(out=attnT[0:64, h, :], in0=pO_[0:64, :], in1=rlb_[0:64, :], op=ALU.mult), [b_pO_, b_rlb_], [b_attnT[h]])

                emit_scores(0)
                for gi_ in range(len(glist)):
                    if gi_ + 1 < len(glist):
                        emit_scores(gi_ + 1)
                    emit_pv(gi_)
            else:
                sample_attention()

            stage('attn')
            for f in range(2):
                sla, bsla = wget(base + 4 + 2 * f)
                slp, bslp = wget(base + 5 + 2 * f, hold=1)
                sva = slotv(sla, 8); svp = slotv(slp, 4)
                for j in range(nsub):
                    pa, bpa = nextpa()
                    for hh in range(8):
                        Top(lambda e, pa=pa, hh=hh, j=j, sva=sva: e.matmul(pa[0:PP, :], lhsT=attnT[0:64, hh, j * PP:(j + 1) * PP], rhs=sva[0:64, hh, :],
                                                                          start=(hh == 0), stop=False), [b_attnT[hh], bsla], [bpa])
                    for g in range(4):
                        Top(lambda e, pa=pa, g=g, j=j, svp=svp: e.matmul(pa[0:PP, :], lhsT=poolT[:, g, j * PP:(j + 1) * PP], rhs=svp[:, g, :],
                                                                        start=False, stop=(g == 3)), [b_poolT, bslp], [bpa])
                    Vop(lambda e, pa=pa, j=j, f=f: e.tensor_tensor(out=xh[0:PP, j, f * 512:(f + 1) * 512], in0=pa[0:PP, :], in1=xh[0:PP, j, f * 512:(f + 1) * 512], op=ALU.add),
                        [bpa, b_xhs[j]], [b_xhs[j]])
            stage('wout')
            P.inherit(F_bufs, A_bufs)
            norm_T(g2)
            hg = hist_g; hv = hist_v
            for i in range(11):
                sl, bsl = wget(base + 8 + i)
                sv = slotv(sl, 8)
                for (col0, kind, a) in ((0, 'g', 2 * i), (256, 'v', 2 * i), (128, 'g', 2 * i + 1), (384, 'v', 2 * i + 1)):
                    pa, bpa = nextpa()
                    for c in range(8):
                        Top(lambda e, pa=pa, c=c, col0=col0, sv=sv: e.matmul(pa[:, 0:NT], lhsT=sv[:, c, col0:col0 + 128], rhs=actT[:, c, 0:NT],
                                                                            start=(c == 0), stop=(c == 7)), [b_actT, bsl], [bpa])
                    ki = 0 if kind == 'g' else 1
                    cn = ccnt[kind]; ccnt[kind] += 1
                    nacc = len(acc[ki])
                    acf = acc[ki][cn % nacc]; bac = b_acc[ki][cn % nacc]
                    ac = acf[:, 0:NT].rearrange("p (s l) -> p s l", l=L)
                    cw = (cpg if kind == 'g' else cpv)
                    par = 0 if smp else (m % 2)
                    hbig = (hg if kind == 'g' else hv); hsml = (hist2_g if kind == 'g' else hist2_v)
                    hold = (hbig[:, a, 0:nseg * 2] if par == 0 else hsml[:, a, 0:2]).rearrange("p (s e) -> p s e", e=2)
                    hnew = (hsml[:, a, 0:2] if par == 0 else hbig[:, a, 0:2]).rearrange("p (s e) -> p s e", e=2)
                    bho = b_histd[(kind, a, par)]; bhn = b_histd[(kind, a, 1 - par)]
                    pav = pa[:, 0:NT].rearrange("p (s l) -> p s l", l=L)
                    if not smp:
                        Aop(lambda e, pav=pav, hnew=hnew: e.activation(out=hnew, in_=pav[:, :, L - 2:L], func=AF.Copy), [bpa], [bhn])
                    Aop(lambda e, pav=pav, ac=ac, cw=cw, a=a: e.activation(out=ac, in_=pav, func=AF.Identity, scale=cw[:, a, 2:3], bias=cw[:, a, 3:4]),
                        [bpa, b_const], [bac])
                    Vop(lambda e, pav=pav, ac=ac, cw=cw, a=a: e.scalar_tensor_tensor(out=ac[:, :, 1:L], in0=pav[:, :, 0:L - 1], scalar=cw[:, a, 1:2], in1=ac[:, :, 1:L],
                                                                                   op0=ALU.mult, op1=ALU.add), [bpa, b_const, bac], [bac])
                    Vop(lambda e, pav=pav, ac=ac, cw=cw, a=a: e.scalar_tensor_tensor(out=ac[:, :, 2:L], in0=pav[:, :, 0:L - 2], scalar=cw[:, a, 0:1], in1=ac[:, :, 2:L],
                                                                                   op0=ALU.mult, op1=ALU.add), [bpa, b_const, bac], [bac])
                    Vop(lambda e, hold=hold, ac=ac, cw=cw, a=a: e.scalar_tensor_tensor(out=ac[:, :, 0:2], in0=hold[:, :, 0:2], scalar=cw[:, a, 0:1], in1=ac[:, :, 0:2],
                                                                                     op0=ALU.mult, op1=ALU.add), [bho, b_const, bac], [bac])
                    Vop(lambda e, hold=hold, ac=ac, cw=cw, a=a: e.scalar_tensor_tensor(out=ac[:, :, 0:1], in0=hold[:, :, 1:2], scalar=cw[:, a, 1:2], in1=ac[:, :, 0:1],
                                                                                     op0=ALU.mult, op1=ALU.add), [bho, b_const, bac], [bac])
                    if kind == 'g':
                        sgi = cn % 2
                        pend_silu[0] = (acf, sgi, bac)
                    else:
                        sgi = cn % 2
                        acf_g, sgi_g, bac_g = pend_silu[0]
                        Aop(lambda e, acf_g=acf_g, sgi_g=sgi_g: e.activation(out=sg[sgi_g][:, 0:NT], in_=acf_g[:, 0:NT], func=AF.Silu), [bac_g], [b_sg[sgi_g]])
                        Gop(lambda e, acf=acf, a=a, sgi=sgi: e.tensor_tensor(out=gT[:, a, 0:NT], in0=sg[sgi][:, 0:NT], in1=acf[:, 0:NT], op=ALU.mult), [b_sg[sgi], bac], [b_gT])
                if last:
                    pa, bpa = nextpa()
                    for c in range(8):
                        Top(lambda e, pa=pa, c=c, sv=sv: e.matmul(pa[0:PP, :], lhsT=actT[:, c, NT - PP:NT], rhs=sv[:, c, :], start=(c == 0), stop=(c == 7)),
                            [b_actT, bsl], [bpa])
                    Aop(lambda e, pa=pa: e.activation(out=upst[0:PP, :], in_=pa[0:PP, :], func=AF.Copy), [bpa], [b_upst])
                    for (co, fo) in ((0, 256 * i), (256, 2816 + 256 * i)):
                        if smp:
                            for jj in range(2):
                                Dop(lambda e, co=co, fo=fo, jj=jj: e.dma_start(out=nfs[:, jj, fo:fo + 256], in_=upst[2 + jj:64:4, co:co + 256]), [b_upst], [], cho())
                        else:
                            Dop(lambda e, co=co, fo=fo: e.dma_start(out=nfp[s, :, fo:fo + 256], in_=upst[126:128, co:co + 256]), [b_upst], [], cho())
            stage('ffn_up')
            accs = [(pA[0], b_pA[0]), (pA[1], b_pA[1]), (pS[0], b_pS[0]), (pS[1], b_pS[1])]
            for f in range(2):
                for pc in range(3):
                    sl, bsl = wget(base + 19 + 3 * f + pc)
                    sv = slotv(sl, 8)
                    for kk in range(8 if pc < 2 else 6):
                        kc = pc * 8 + kk
                        for j in range(nsub):
                            Top(lambda e, j=j, kc=kc, kk=kk, sv=sv: e.matmul(accs[j][0][0:PP, :], lhsT=gT[:, kc, j * PP:(j + 1) * PP], rhs=sv[:, kk, :],
                                                                            start=(kc == 0), stop=(kc == 21)), [b_gT, bsl], [accs[j][1]])
                for j in range(nsub):
                    Vop(lambda e, j=j, f=f: e.tensor_tensor(out=xh[0:PP, j, f * 512:(f + 1) * 512], in0=accs[j][0][0:PP, :], in1=xh[0:PP, j, f * 512:(f + 1) * 512], op=ALU.add),
                        [accs[j][1], b_xhs[j]], [b_xhs[j]])
            if smp:
                ydst = ys.rearrange("(j p) d -> p j d", p=64)
            else:
                ydst = yp[s, T0:T0 + 512, :].rearrange("(j p) d -> p j d", p=128)
            for j in range(nsub):
                Dop(lambda e, j=j: e.dma_start(out=ydst[:, j, :], in_=xh[0:PP, j, :]), [b_xhs[j]], [], ch_ys[j])

        qbd = sb("qbd", [128, 4, 16, 8], BF16); b_qbd = Buf()
        kTp = sb("kTp", [128, 4, 64], BF16); b_kTp = Buf()
        wt = sb("wt", [128, 224], F32); wtn = sb("wtn", [64, 16, 32], F32)
        vgrp = [None]

        def sample_prep():
            cload(wt[:], c_wt); cload(wtn[:], c_wtn)
            Vop(lambda e: e.memset(qbd[:].rearrange("p a b t -> p (a b t)"), 0.0), [], [b_qbd])
            sph = V16[0:120, 0:2, :]
            Dop(lambda e: e.dma_start(out=sph, in_=spool.rearrange("b i c -> (b i) c").rearrange("(two r) c -> r two c", r=120)), [], [b_V16], ch_s[0], q='pool')
            Dop(lambda e: e.dma_start(out=nps[:, 0:11, :], in_=spool[:, 4:15, :]), [], [], cho())
            for two in range(2):
                pt, bpt = nextpt()
                for g in range(4):
                    Top(lambda e, pt=pt, g=g, two=two: e.transpose(out=pt[:, g * 120:(g + 1) * 120], in_=V16[0:120, two, g * 128:(g + 1) * 128],
                                                                  identity=identb[0:120, 0:120]), [b_V16, b_const], [bpt])
                for g in range(4):
                    ue = uT[:, g, 0:320].rearrange("p (s e) -> p s e", e=20)
                    Aop(lambda e, pt=pt, g=g, two=two, ue=ue: e.activation(out=ue[:, 8 * two:8 * two + 8, 1:16], in_=pt[:, g * 120:(g + 1) * 120].rearrange("p (s i) -> p s i", i=15),
                                                                          func=AF.Copy), [bpt], [b_uT[g]])
            sfh = V16[0:32, 2:13, :].rearrange("p a t -> p (a t)")
            Dop(lambda e: e.dma_start(out=sfh[:, 0:5632], in_=sffn.rearrange("b j c -> (b j) c")), [], [b_V16], ch_s[1], q='pool')
            for kind in range(2):
                hs_ = hist_g if kind == 0 else hist_v
                for a0 in (0, 8, 16):
                    na = 8 if a0 < 16 else 6
                    pt, bpt = nextpt()
                    for a in range(a0, a0 + na):
                        f0 = a * 128 + (2816 if kind else 0)
                        Top(lambda e, pt=pt, a=a, a0=a0, f0=f0: e.transpose(out=pt[:, (a - a0) * 32:(a - a0 + 1) * 32], in_=sfh[0:32, f0:f0 + 128],
                                                                           identity=identb[0:32, 0:32]), [b_V16, b_const], [bpt])
                    Aop(lambda e, pt=pt, a0=a0, na=na, hs_=hs_: e.activation(out=hs_[:, a0:a0 + na, :], in_=pt[:, 0:na * 32].rearrange("p (a x) -> p a x", x=32),
                                                                             func=AF.Copy), [bpt], b_hist_all)

        pf = sb("pf", [128, 256], F32); b_pf = Buf()
        pts = sb("pts", [128, 256], BF16); b_pts = Buf()
        KcT = kTflat[:, 7168:10752].rearrange("p (a r) -> p a r", a=4); b_KcT = Buf()

        def sample_attention():
            P.inherit([b_Kc[0], b_Vc[0], b_KcT], [b_kT])
            Vop(lambda e: e.memset(pO[0:64, :], 0.0), [], [b_pO])
            Vop(lambda e: e.memset(pL[0:64, :], 0.0), [], [b_pL])
            for b in range(16):
                kc, bkc = Kc[b % 2], b_Kc[b % 2]
                vc, bvc = Vc[b % 2], b_Vc[b % 2]
                for (src, dstt, bd, chh) in ((ck, kc, bkc, ch_s[0]), (cv, vc, bvc, ch_s[1])):
                    grp = None
                    for r in range(4):
                        grp = Dop(lambda e, src=src, dstt=dstt, r=r, b=b: e.dma_start(out=dstt[r:128:4, 0:3, :],
                                                                                      in_=src[b, r:1536:16, :].rearrange("(tau a) c -> a tau c", a=32)),
                                  [], [bd], chh, group=grp, q='pool')
                    grp = Dop(lambda e, src=src, dstt=dstt, b=b: e.dma_start(out=dstt[:, 3:7, :], in_=src[b, 1536:2048, :].rearrange("(tau p) c -> p tau c", p=128)),
                              [], [bd], chh, group=grp, q='pool')
                for tau in range(7):
                    pt, bpt = nextpt()
                    for hp in range(4):
                        Top(lambda e, pt=pt, hp=hp, tau=tau, kc=kc: e.transpose(out=pt[:, hp * 128:(hp + 1) * 128], in_=kc[:, tau, hp * 128:(hp + 1) * 128], identity=identb[:]),
                            [bkc, b_const], [bpt])
                    Aop(lambda e, pt=pt, tau=tau: e.activation(out=KcT[:, :, tau * 128:(tau + 1) * 128], in_=pt[:, 0:512].rearrange("p (a r) -> p a r", a=4), func=AF.Copy),
                        [bpt], [b_KcT])
                ps, bps = pS[b % 2], b_pS[b % 2]
                for tau in range(7):
                    for hp in range(4):
                        Top(lambda e, ps=ps, tau=tau, hp=hp, b=b: e.matmul(ps[:, tau * 32 + hp * 8:tau * 32 + hp * 8 + 8], lhsT=KcT[:, hp, tau * 128:(tau + 1) * 128],
                                                                          rhs=qbd[:, hp, b, :], start=True, stop=True), [b_KcT, b_qbd], [bps])
                for hp in range(4):
                    Top(lambda e, ps=ps, hp=hp, b=b: e.matmul(ps[0:64, 224 + hp * 8:224 + hp * 8 + 8], lhsT=kTp[:, hp, 0:64], rhs=qbd[:, hp, b, :], start=True, stop=True),
                        [b_kTp, b_qbd], [bps])
                Aop(lambda e, ps=ps: e.activation(out=pf[:, 0:224], in_=ps[:, 0:224], func=AF.Exp, bias=negM[:, 0:1], scale=1.0), [bps, b_const], [b_pf])
                Aop(lambda e, ps=ps: e.activation(out=pf[0:64, 224:256], in_=ps[0:64, 224:256], func=AF.Exp, bias=negM[0:64, 0:1], scale=1.0), [bps, b_const], [b_pf])
                Vop(lambda e: e.tensor_tensor(out=pts[:, 0:224], in0=pf[:, 0:224], in1=wt[:, :], op=ALU.mult), [b_pf, b_const], [b_pts])
                Vop(lambda e, b=b: e.tensor_tensor(out=pts[0:64, 224:256], in0=pf[0:64, 224:256], in1=wtn[0:64, b, :], op=ALU.mult), [b_pf, b_const], [b_pts])
                for h in range(8):
                    o = pO[0:64, h * 64 + b * 4:h * 64 + b * 4 + 4]
                    for tau in range(7):
                        Top(lambda e, o=o, tau=tau, h=h, vc=vc: e.matmul(o, lhsT=vc[:, tau, h * 64:(h + 1) * 64], rhs=pts[:, tau * 32 + h * 4:tau * 32 + h * 4 + 4],
                                                                        start=False, stop=False, skip_group_check=True), [bvc, b_pts], [b_pO])
                    Top(lambda e, o=o, h=h: e.matmul(o, lhsT=Vn[0:64, 0, h * 64:(h + 1) * 64], rhs=pts[0:64, 224 + h * 4:224 + h * 4 + 4],
                                                     start=False, stop=False, skip_group_check=True), [b_Vn[0], b_pts], [b_pO])
                for tau in range(7):
                    Top(lambda e, tau=tau, b=b: e.matmul(pL[0:64, b * 32:(b + 1) * 32], lhsT=onesb[:, 0:64], rhs=pts[:, tau * 32:(tau + 1) * 32],
                                                         start=False, stop=False, skip_group_check=True), [b_const, b_pts], [b_pL])
                Top(lambda e, b=b: e.matmul(pL[0:64, b * 32:(b + 1) * 32], lhsT=onesb[0:64, 0:64], rhs=pts[0:64, 224:256],
                                            start=False, stop=False, skip_group_check=True), [b_const, b_pts], [b_pL])
            Vop(lambda e: e.reciprocal(out=rlb[0:64, :], in_=pL[0:64, :]), [b_pL], [b_rlb])
            for h in range(8):
                Vop(lambda e, h=h: e.tensor_tensor(out=attnT[0:64, h, 0:64].rearrange("p (b t) -> p b t", t=4),
                                                   in0=pO[0:64, h * 64:(h + 1) * 64].rearrange("p (b t) -> p b t", t=4),
                                                   in1=rlb[0:64, :].rearrange("p (b h t) -> p h b t", h=8, t=4)[:, h, :, :], op=ALU.mult),
                    [b_pO, b_rlb], [b_attnT[h]])

        tidx = 0
        try:
          stage('const')
          for s in range(2):
            for g in range(4):
                Gop(lambda e, g=g: e.memset(uT[:, g, 0:16], 0.0), [], [b_uT[g]])
            Gop(lambda e: e.memset(hist_g[:].rearrange("p a x -> p (a x)"), 0.0), [], b_hist_all)
            Gop(lambda e: e.memset(hist_v[:].rearrange("p a x -> p (a x)"), 0.0), [], b_hist_all)
            for m in range(4):
                do_tile(tidx, False, s, m)
                tidx += 1
                stage('tile%d' % (tidx - 1))
          sample_prep()
          stage('sprep')
          do_tile(tidx, True, 0, 0)
        except StopBuild:
            pass

        if os.environ.get('KDBG'):
            print('SBUF remaining', nc.sbuf_bytes_remaining)
        P.finalize()
        with nc.Block() as block:
            @block.tensor
            def _(e): P.run('pe', e)

            @block.scalar
            def _(e): P.run('act', e)

            @block.vector
            def _(e): P.run('dve', e)

            @block.gpsimd
            def _(e): P.run('pool', e)

            @block.sync
            def _(e):
                P.run('sp', e)
                for c in P.chans:
                    if c.count:
                        e.wait_ge(c.sem, c.count)
    return nc


def _consts():
    c = {}
    c["c_id"] = np.eye(128, dtype=np.float32).astype(BF)
    k = np.arange(128)[:, None]; q = np.arange(128)[None, :]
    c["c_mcur"] = np.tile(np.where(k <= q, 0.0, -30000.0).astype(np.float32), (1, 4)).astype(BF)
    c["c_mprev"] = np.tile(np.where(k >= q, 0.0, -30000.0).astype(np.float32), (1, 4)).astype(BF)
    m16 = np.zeros((128, 4, 16, 32), np.float32)
    for m in range(4):
        m16[:, m, :, :] = np.where(np.arange(128)[:, None] <= 32 * m + np.arange(32)[None, :], 0.0, -30000.0).astype(np.float32)[:, None, :]
    c["c_m16"] = m16.reshape(128, 4, 512).astype(BF)
    slopes = 2.0 ** (-np.arange(1, 9, dtype=np.float64))
    t = np.arange(2048)
    kaug = np.zeros((4, 8, 2048), np.float64)
    kaug[0] = (-64 * slopes)[:, None]; kaug[1] = (-slopes)[:, None]
    kaug[2] = 64 * slopes[:, None] * (t // 64)[None, :]; kaug[3] = slopes[:, None] * (t % 64)[None, :]
    c["c_kaug"] = kaug.astype(np.float32).astype(BF)
    qaug = np.zeros((4, 8, 2048), np.float32)
    qaug[0] = (t // 64)[None, :]; qaug[1] = (t % 64)[None, :]; qaug[2] = 1; qaug[3] = 1
    c["c_qaug"] = qaug.astype(BF)
    inv = np.zeros((128, 4, 16), np.float32)
    for g, w in enumerate((2, 4, 8, 16)):
        inv[:, g, :] = 1.0 / np.minimum(w, np.arange(16) + 1)
    c["c_invcnt"] = inv

    def cnt(d):
        d = np.asarray(d)
        return ((d >= 0) & (d <= 128)).astype(np.float64) + ((d >= 0) & (d % 4 == 0) & (d <= 512)) + ((d >= 0) & (d % 16 == 0) & (d <= 2048))
    p = np.arange(128)
    wt = np.zeros((128, 7, 8, 4), np.float64)
    for tau in range(7):
        row = 16 * (32 * tau + p // 4) + p % 4 if tau < 3 else 1536 + 128 * (tau - 3) + p
        for tt in range(4):
            d = 2048 + tt - row
            for h in range(8):
                wt[:, tau, h, tt] = cnt(d) * np.exp(-slopes[h] * d)
    c["c_wt"] = wt.reshape(128, 224).astype(np.float32)
    wtn = np.zeros((64, 16, 8, 4), np.float64)
    for b in range(16):
        for t1 in range(4):
            for tt in range(4):
                d = tt - t1
                if d >= 0:
                    wtn[4 * b + t1, b, :, tt] = cnt(d) * np.exp(-slopes * d)
    c["c_wtn"] = wtn.reshape(64, 16, 32).astype(np.float32)
    return c


_NC = [None]


def kernel(x_prompt, x_sample, cache_k, cache_v, state_pool, state_ffn_conv, g_attn_norm, w_in, g_q, g_k,
           w_pool, pool_scale, w_out, g_ffn_norm, w_up, conv_w, conv_b, w_down):
    f = lambda a: np.ascontiguousarray(np.asarray(a, dtype=np.float32))
    nc = build()
    consts = _consts()
    shared = dict(
        g_attn=f(g_attn_norm).reshape(1, 1024), w_in=f(w_in)[0], gq=np.ascontiguousarray(np.broadcast_to(f(g_q).reshape(1, 512), (128, 512))),
        gk=np.ascontiguousarray(np.broadcast_to(f(g_k).reshape(1, 512), (128, 512))), w_pool=f(w_pool)[0], pool_scale=f(pool_scale).reshape(1, 512),
        w_out=f(w_out)[0], g_ffn=f(g_ffn_norm).reshape(1, 1024), w_up=f(w_up)[0], conv_w=f(conv_w)[0], conv_b=f(conv_b).reshape(1, 5632),
        w_down=f(w_down)[0], **consts)
    xp_ = f(x_prompt); xs_ = f(x_sample); ck_ = f(cache_k)[0]; cv_ = f(cache_v)[0]; sp_ = f(state_pool)[0]; sf_ = f(state_ffn_conv)[0]
    in_maps = []
    for i in range(8):
        d = dict(shared)
        d["xp"] = xp_[2 * i:2 * i + 2]
        d["xs"] = xs_[16 * i:16 * i + 16].reshape(64, 1024)
        d["ck"] = ck_[16 * i:16 * i + 16].reshape(16, 2048, 512)
        d["cv"] = cv_[16 * i:16 * i + 16].reshape(16, 2048, 512)
        d["spool"] = sp_[16 * i:16 * i + 16]
        d["sffn"] = sf_[16 * i:16 * i + 16]
        in_maps.append(d)
    res = run_bass_kernel_spmd(nc, in_maps, core_ids=list(range(8)))
    R = res.results
    cat = lambda k: np.concatenate([np.asarray(r[k], dtype=np.float32) for r in R], axis=0)
    y_p = cat("yp"); y_s = cat("ys").reshape(128, 4, 1024)
    nk_p = cat("nkp").reshape(1, 16, 2048, 8, 64); nv_p = cat("nvp").reshape(1, 16, 2048, 8, 64)
    np_p = cat("npp").reshape(1, 16, 15, 512); nf_p = cat("nfp").reshape(1, 16, 2, 5632)
    nk_s = cat("nks").reshape(1, 128, 4, 8, 64); nv_s = cat("nvs").reshape(1, 128, 4, 8, 64)
    np_s = cat("nps").reshape(1, 128, 15, 512); nf_s = cat("nfs").reshape(1, 128, 2, 5632)
    return (y_p, y_s, nk_p, nv_p, np_p, nf_p, nk_s, nv_s, np_s, nf_s)
```
